# Optimizing a Trainium2 kernel written in Bass

```python
import math, functools
import jax, jax.numpy as jnp
from jax import lax
import numpy as np

D_MODEL = 1024
BATCH = 2
SEQ = 16384
DEPTH = 2

SB_HEADS = 8
SB_HEAD_DIM = 64
SB_WIDTH = SB_HEADS * SB_HEAD_DIM
RET_HEADS = 4
RET_HEAD_DIM = 128
RET_WIDTH = RET_HEADS * RET_HEAD_DIM
EVEN_IN = 3 * SB_WIDTH + 4 * RET_WIDTH
GLA_HEADS = 4
GLA_DK = 128
GLA_DV = 256
GLA_K_WIDTH = GLA_HEADS * GLA_DK
GLA_V_WIDTH = GLA_HEADS * GLA_DV
GLA_GATE_RANK = 16
GLA_GATE_TAU = 16.0
ODD_IN = 2 * GLA_K_WIDTH + 2 * GLA_V_WIDTH + GLA_GATE_RANK
D_FF = 2816
CONV_WIDTH = 3
Q_BLOCK = 128
RET_CHUNK = 128
GLA_CHUNK = 64
ROPE_BASE = 10000.0
EPS = 1e-6
N_EVEN = (DEPTH + 1) // 2
N_ODD = DEPTH // 2

kernel_name = "hybrid_stickbreak_retention_gla_convffn"


def rms_norm(x, g):
    xf = x.astype(jnp.float32)
    y = xf * lax.rsqrt(jnp.mean(xf * xf, axis=-1, keepdims=True) + EPS)
    return (y * g.astype(jnp.float32)).astype(x.dtype)


def to_heads(t, n):
    b, s, w = t.shape
    return t.reshape(b, s, n, w // n).transpose(0, 2, 1, 3)


def from_heads(t):
    b, h, s, d = t.shape
    return t.transpose(0, 2, 1, 3).reshape(b, s, h * d)


def rope(x, pos):
    d = x.shape[-1]
    half = d // 2
    inv = ROPE_BASE ** (-jnp.arange(half, dtype=jnp.float32) / half)
    ang = pos.astype(jnp.float32)[:, None] * inv[None, :]
    cos, sin = jnp.cos(ang), jnp.sin(ang)
    xf = x.astype(jnp.float32)
    x1, x2 = xf[..., :half], xf[..., half:]
    return jnp.concatenate([x1 * cos - x2 * sin, x1 * sin + x2 * cos], axis=-1).astype(x.dtype)


def stick_breaking_attention(q, k, v):
    b, h, s, d = q.shape
    nb = s // Q_BLOCK
    scale = d ** -0.5
    qb = q.reshape(b, h, nb, Q_BLOCK, d).transpose(2, 0, 1, 3, 4)
    kf = k.astype(jnp.float32)
    vf = v.astype(jnp.float32)
    key_pos = jnp.arange(s)

    def block(args):
        qi, i = args
        q_pos = i * Q_BLOCK + jnp.arange(Q_BLOCK)
        z = jnp.einsum('bhqd,bhkd->bhqk', qi.astype(jnp.float32), kf) * scale
        mask = key_pos[None, :] < q_pos[:, None]
        log_beta = jax.nn.log_sigmoid(z)
        log_1m_beta = jnp.where(mask, -jax.nn.softplus(z), 0.0)
        suffix = lax.cumsum(log_1m_beta, axis=3, reverse=True) - log_1m_beta
        w = jnp.where(mask, jnp.exp(log_beta + suffix), 0.0)
        return jnp.einsum('bhqk,bhkd->bhqd', w, vf)

    out = lax.map(block, (qb, jnp.arange(nb)))
    return out.transpose(1, 2, 0, 3, 4).reshape(b, h, s, d).astype(v.dtype)


def retention_chunkwise(q, k, v, log_gamma):
    b, h, s, d = q.shape
    c = RET_CHUNK
    nc = s // c
    qc = q.astype(jnp.float32).reshape(b, h, nc, c, d)
    kc = (k.astype(jnp.float32) * d ** -0.5).reshape(b, h, nc, c, d)
    vc = v.astype(jnp.float32).reshape(b, h, nc, c, d)
    pos = jnp.arange(c, dtype=jnp.float32)
    lg = log_gamma[:, None]
    diff = pos[:, None] - pos[None, :]
    decay_intra = jnp.where(diff >= 0, jnp.exp(lg[:, :, None] * jnp.maximum(diff, 0.0)), 0.0)
    scores = jnp.einsum('bhnid,bhnjd->bhnij', qc, kc) * decay_intra[None, :, None]
    inner = jnp.einsum('bhnij,bhnje->bhnie', scores, vc)
    zeta = jnp.exp(lg * (c - 1 - pos))
    contrib = jnp.einsum('bhnjd,hj,bhnje->bhnde', kc, zeta, vc)
    g_chunk = jnp.exp(lg * c)[None, :, :, None]

    def step(state, cn):
        return g_chunk * state + cn, state

    _, r_prev = lax.scan(step, jnp.zeros((b, h, d, d), jnp.float32), contrib.transpose(2, 0, 1, 3, 4))
    r_prev = r_prev.transpose(1, 2, 0, 3, 4)
    xi = jnp.exp(lg * (pos + 1.0))
    cross = jnp.einsum('bhnid,bhnde->bhnie', qc, r_prev) * xi[None, :, None, :, None]
    return (inner + cross).reshape(b, h, s, d).astype(v.dtype)


def gla_chunked(q, k, v, log_alpha):
    b, h, s, dk = q.shape
    dv = v.shape[-1]
    c = GLA_CHUNK
    nc = s // c
    qc = (q.astype(jnp.float32) * dk ** -0.5).reshape(b, h, nc, c, dk)
    kc = k.astype(jnp.float32).reshape(b, h, nc, c, dk)
    vc = v.astype(jnp.float32).reshape(b, h, nc, c, dv)
    cum = lax.cumsum(log_alpha.astype(jnp.float32).reshape(b, h, nc, c, dk), axis=3)
    cum_last = cum[:, :, :, -1:, :]
    q_t = qc * jnp.exp(cum)
    k_t = kc * jnp.exp(-cum)
    causal = jnp.tril(jnp.ones((c, c), dtype=bool))
    a = jnp.where(causal, jnp.einsum('bhnid,bhnjd->bhnij', q_t, k_t), 0.0)
    intra = jnp.einsum('bhnij,bhnje->bhnie', a, vc)
    contrib = jnp.einsum('bhnjd,bhnje->bhnde', kc * jnp.exp(cum_last - cum), vc)
    chunk_decay = jnp.exp(cum_last[:, :, :, 0, :])

    def step(state, inp):
        dec, cn = inp
        return dec[..., None] * state + cn, state

    _, s_prev = lax.scan(step, jnp.zeros((b, h, dk, dv), jnp.float32),
                         (chunk_decay.transpose(2, 0, 1, 3), contrib.transpose(2, 0, 1, 3, 4)))
    s_prev = s_prev.transpose(1, 2, 0, 3, 4)
    inter = jnp.einsum('bhnid,bhnde->bhnie', q_t, s_prev)
    return (intra + inter).reshape(b, h, s, dv).astype(v.dtype)


def even_mixer(hn, w_in, q_gain, k_gain, ret_gain, w_out):
    s = hn.shape[1]
    p = hn @ w_in
    cuts = [SB_WIDTH, 2 * SB_WIDTH, 3 * SB_WIDTH,
            3 * SB_WIDTH + RET_WIDTH, 3 * SB_WIDTH + 2 * RET_WIDTH, 3 * SB_WIDTH + 3 * RET_WIDTH]
    sb_q, sb_k, sb_v, r_q, r_k, r_v, r_g = jnp.split(p, cuts, axis=-1)
    qa = rms_norm(to_heads(sb_q, SB_HEADS), q_gain)
    ka = rms_norm(to_heads(sb_k, SB_HEADS), k_gain)
    out_a = from_heads(stick_breaking_attention(qa, ka, to_heads(sb_v, SB_HEADS)))
    pos = jnp.arange(s)
    qb = rope(to_heads(r_q, RET_HEADS), pos)
    kb = rope(to_heads(r_k, RET_HEADS), pos)
    log_gamma = jnp.log(1.0 - jnp.exp2(-5.0 - jnp.arange(RET_HEADS, dtype=jnp.float32)))
    ret = retention_chunkwise(qb, kb, to_heads(r_v, RET_HEADS), log_gamma)
    out_b = from_heads(rms_norm(ret, ret_gain)) * jax.nn.silu(r_g)
    return jnp.concatenate([out_a, out_b], axis=-1) @ w_out


def odd_mixer(hn, w_in, w_alpha, b_alpha, out_gain, w_out):
    p = hn @ w_in
    cuts = [GLA_K_WIDTH, 2 * GLA_K_WIDTH, 2 * GLA_K_WIDTH + GLA_V_WIDTH, 2 * GLA_K_WIDTH + 2 * GLA_V_WIDTH]
    g_q, g_k, g_v, g_r, g_a = jnp.split(p, cuts, axis=-1)
    log_alpha = jax.nn.log_sigmoid((g_a @ w_alpha + b_alpha).astype(jnp.float32)) / GLA_GATE_TAU
    o = gla_chunked(to_heads(g_q, GLA_HEADS), to_heads(g_k, GLA_HEADS),
                    to_heads(g_v, GLA_HEADS), to_heads(log_alpha, GLA_HEADS))
    return (from_heads(rms_norm(o, out_gain)) * jax.nn.silu(g_r)) @ w_out


def conv_ffn(hn, w_up, conv_w, conv_b, w_down):
    s = hn.shape[1]
    u, v = jnp.split(hn @ w_up, 2, axis=-1)
    up = jnp.pad(u, ((0, 0), (CONV_WIDTH - 1, 0), (0, 0)))
    uc = conv_b + conv_w[0] * up[:, 0:s]
    for j in range(1, CONV_WIDTH):
        uc = uc + conv_w[j] * up[:, j:j + s]
    return (jax.nn.silu(uc) * v) @ w_down


def setup_inputs(seed: int = 0) -> dict:
    key = jax.random.key(seed)
    ks = jax.random.split(key, 20)
    f32 = jnp.float32

    def nrm(k, shape, scale):
        return jax.random.normal(k, shape, f32) * scale

    return {
        'x': nrm(ks[0], (BATCH, SEQ, D_MODEL), 1.0),
        'mix_norm_g': 1.0 + nrm(ks[1], (DEPTH, D_MODEL), 0.02),
        'even_w_in': nrm(ks[2], (N_EVEN, D_MODEL, EVEN_IN), D_MODEL ** -0.5),
        'sb_q_gain': 1.0 + nrm(ks[3], (N_EVEN, SB_HEAD_DIM), 0.02),
        'sb_k_gain': 1.0 + nrm(ks[4], (N_EVEN, SB_HEAD_DIM), 0.02),
        'ret_out_gain': 1.0 + nrm(ks[5], (N_EVEN, RET_HEAD_DIM), 0.02),
        'even_w_out': nrm(ks[6], (N_EVEN, SB_WIDTH + RET_WIDTH, D_MODEL), (SB_WIDTH + RET_WIDTH) ** -0.5),
        'odd_w_in': nrm(ks[7], (N_ODD, D_MODEL, ODD_IN), D_MODEL ** -0.5),
        'gla_w_alpha': nrm(ks[8], (N_ODD, GLA_GATE_RANK, GLA_K_WIDTH), GLA_GATE_RANK ** -0.5),
        'gla_b_alpha': nrm(ks[9], (N_ODD, GLA_K_WIDTH), 0.1),
        'gla_out_gain': 1.0 + nrm(ks[10], (N_ODD, GLA_DV), 0.02),
        'odd_w_out': nrm(ks[11], (N_ODD, GLA_V_WIDTH, D_MODEL), GLA_V_WIDTH ** -0.5),
        'ffn_norm_g': 1.0 + nrm(ks[12], (DEPTH, D_MODEL), 0.02),
        'ffn_w_up': nrm(ks[13], (DEPTH, D_MODEL, 2 * D_FF), D_MODEL ** -0.5),
        'ffn_conv_w': nrm(ks[14], (DEPTH, CONV_WIDTH, D_FF), CONV_WIDTH ** -0.5),
        'ffn_conv_b': nrm(ks[15], (DEPTH, D_FF), 0.01),
        'ffn_w_down': nrm(ks[16], (DEPTH, D_FF, D_MODEL), D_FF ** -0.5),
    }


def reference(x, mix_norm_g, even_w_in, sb_q_gain, sb_k_gain, ret_out_gain, even_w_out,
              odd_w_in, gla_w_alpha, gla_b_alpha, gla_out_gain, odd_w_out,
              ffn_norm_g, ffn_w_up, ffn_conv_w, ffn_conv_b, ffn_w_down):
    h = x
    for layer in range(DEPTH):
        hn = rms_norm(h, mix_norm_g[layer])
        if layer % 2 == 0:
            e = layer // 2
            mix = even_mixer(hn, even_w_in[e], sb_q_gain[e], sb_k_gain[e], ret_out_gain[e], even_w_out[e])
        else:
            o = layer // 2
            mix = odd_mixer(hn, odd_w_in[o], gla_w_alpha[o], gla_b_alpha[o], gla_out_gain[o], odd_w_out[o])
        h = h + mix.astype(h.dtype)
        hn = rms_norm(h, ffn_norm_g[layer])
        h = h + conv_ffn(hn, ffn_w_up[layer], ffn_conv_w[layer], ffn_conv_b[layer], ffn_w_down[layer]).astype(h.dtype)
    return h
```

```python
import contextlib
import numpy as np
import ml_dtypes
import concourse.bass as bass
import concourse.mybir as mybir
from concourse.bass_utils import run_bass_kernel_spmd

F32 = mybir.dt.float32
BF16 = mybir.dt.bfloat16
AF = mybir.ActivationFunctionType
ALU = mybir.AluOpType
AX = mybir.AxisListType
NPBF = ml_dtypes.bfloat16

D = 1024
DFF = 2816
NFF = DFF // 128
EPS = 1e-6
NCORES = 8


class T:
    def __init__(self, t, name, sl=None, st=None):
        self.t = t if sl is None else t[sl]
        self.name = name
        self.st = st
        self.w = None
        self.r = {}
        self.dsem = None

    def __getitem__(self, k):
        return self.t[k]


class Ctx:
    def __init__(self, nc, es):
        self.nc = nc
        self.es = es
        self.sems = {}
        self.issued = {}
        self.isdma = set()
        self.eng = {"pe": nc.tensor, "act": nc.scalar, "dve": nc.vector, "pool": nc.gpsimd, "sp": nc.sync}
        self.waited = {e: {} for e in self.eng}
        for e in self.eng:
            self._mksem("e_" + e)
        self.ndma = 0
        self.pfx = ""
        self.es_phase = es
        self.ncc = 0

    def begin_phase(self, pfx):
        self.pfx = pfx
        self.es_phase = contextlib.ExitStack()
        self.es_phase.__enter__()

    def end_phase(self):
        self.barrier()
        self.es_phase.__exit__(None, None, None)
        self.es_phase = self.es

    def coll_allgather(self, in_t, out_t, groups, key="cc"):
        if key not in self.sems:
            self._mksem(key, dma=True)
        self._wait("pool", [in_t], [out_t])
        ins = self.nc.gpsimd.collective_compute("AllGather", ALU.bypass, replica_groups=groups,
                                                ins=[in_t.t.ap()], outs=[out_t.t.ap()])
        ins.then_inc(self.sems[key], 1)
        self.issued[key] += 1
        in_t.r[key] = self.issued[key]
        out_t.w = (key, self.issued[key])
        out_t.r = {}
        return ins

    def _mksem(self, key, dma=False):
        self.sems[key] = self.es.enter_context(self.nc.semaphore(key))
        self.issued[key] = 0
        if dma:
            self.isdma.add(key)
        return key

    def sb(self, name, shape, dt):
        name = self.pfx + name
        return T(self.es_phase.enter_context(self.nc.sbuf_tensor(name, list(shape), dt)), name)

    def ps(self, name, shape, dt=F32):
        name = self.pfx + name
        t = T(self.es_phase.enter_context(self.nc.psum_tensor(name, list(shape), dt)), name)
        t.excl = True
        return t

    def dram(self, name, shape, dt, kind):
        return T(self.nc.dram_tensor(name, list(shape), dt, kind=kind), name)

    def _wait(self, e, reads, writes, skip_self_waw=False):
        deps = {}

        def add(tok):
            if tok is None:
                return
            k, v = tok
            if deps.get(k, 0) < v:
                deps[k] = v

        reads = [t.st or t for t in reads]
        writes = [t.st or t for t in writes]
        for t in reads:
            add(t.w)
            if getattr(t, "excl", False):
                for k, v in t.r.items():
                    if k != "e_" + e:
                        add((k, v))
        for t in writes:
            if not (skip_self_waw and t.w is not None and t.w[0] == "e_" + e):
                add(t.w)
            for k, v in t.r.items():
                add((k, v))
        w = self.waited[e]
        for k, v in deps.items():
            if k in self.isdma:
                v = self.issued[k]
            if w.get(k, 0) < v:
                self.eng[e].wait_ge(self.sems[k], v)
                w[k] = v

    def op(self, e, fn, reads=(), writes=(), acc=False):
        if getattr(self, "mute", False):
            return None
        self.nops = getattr(self, "nops", 0) + 1
        if self.nops > getattr(self, "oplimit", 10 ** 9):
            return None
        if getattr(self, "insec", False):
            self.seccount = getattr(self, "seccount", 0) + 1
            if self.seccount > getattr(self, "seclimit", 10 ** 9):
                return None
        self._wait(e, reads, writes, skip_self_waw=acc)
        ins = fn()
        k = "e_" + e
        self.issued[k] += 1
        ins.then_inc(self.sems[k], 1)
        tok = (k, self.issued[k])
        for t in reads:
            t = t.st or t
            if t.r.get(k, 0) < tok[1]:
                t.r[k] = tok[1]
        for t in writes:
            t = t.st or t
            t.w = tok
            t.r = {}
        return ins

    def dma(self, q, out_t, out_ap, in_t, in_ap, sbt=None):
        if sbt is None:
            sbt = out_t if not isinstance(out_t, DT) else in_t
        if sbt.dsem is None:
            sbt.dsem = self._mksem("d_%s" % sbt.name, dma=True)
        self._wait(q, [in_t], [out_t])
        ins = self.eng[q].dma_start(out=out_ap, in_=in_ap)
        k = sbt.dsem
        self.issued[k] += 16
        ins.then_inc(self.sems[k], 16)
        tok = (k, self.issued[k])
        (in_t.st or in_t).r[k] = tok[1]
        (out_t.st or out_t).w = tok
        (out_t.st or out_t).r = {}
        self.ndma += 1
        return ins

    def view(self, base, name, sl):
        return T(base.t, name, sl)

    def barrier(self):
        for e in self.eng:
            for k, v in self.issued.items():
                if v > 0 and k != "e_" + e and self.waited[e].get(k, 0) < v:
                    self.eng[e].wait_ge(self.sems[k], v)
                    self.waited[e][k] = v

    def finish(self):
        for k, v in self.issued.items():
            if v > 0 and self.waited["sp"].get(k, 0) < v:
                self.nc.sync.wait_ge(self.sems[k], v)
                self.waited["sp"][k] = v


class DT(T):
    pass


class Chunked:
    def __init__(self, cx, name, R, total, W, dt):
        self.W = W
        self.n = total // W
        self.ch = [mk_dram(cx, "%s_%d" % (name, k), [R, W], dt, "Internal") for k in range(self.n)]

    def win(self, t0, width):
        k = t0 // self.W
        c0 = t0 - k * self.W
        assert c0 + width <= self.W
        return self.ch[k], self.ch[k].t.ap()[:, c0:c0 + width]

    def wins(self, t0, width):
        out = []
        t = t0
        while t < t0 + width:
            k = t // self.W
            c0 = t - k * self.W
            w = min(self.W - c0, t0 + width - t)
            out.append((self.ch[k], self.ch[k].t.ap()[:, c0:c0 + w], t - t0, w))
            t += w
        return out


def mk_dram(cx, name, shape, dt, kind):
    return DT(cx.nc.dram_tensor(name, list(shape), dt, kind=kind), name)


def build_tail(own, want_hn):
    nc = bass.Bass("TRN2", target_bir_lowering=False)
    with contextlib.ExitStack() as es:
        cx = Ctx(nc, es)
        dr = _tail_drams(cx, "")
        dr["hin"] = mk_dram(cx, "hin", [D, own + 2], F32, "ExternalInput")
        dr["cat"] = mk_dram(cx, "cat", [D, own + 2], BF16, "ExternalInput")
        dr["hout"] = mk_dram(cx, "hout", [D, own], F32, "ExternalOutput")
        dr["hn"] = mk_dram(cx, "hn", [D, own], BF16, "ExternalOutput")
        cx.begin_phase("")
        emit_tail(cx, own, dr, want_hn, fused=False)
        cx.end_phase()
        cx.finish()
    return nc


def _tail_drams(cx, p):
    return dict(
        w_out=mk_dram(cx, p + "w_out", [D, D], F32, "ExternalInput"),
        w_up=mk_dram(cx, p + "w_up", [NFF * 128, 2048], F32, "ExternalInput"),
        w_down=mk_dram(cx, p + "w_down", [DFF, D], F32, "ExternalInput"),
        g_ffn=mk_dram(cx, p + "g_ffn", [128, 8], F32, "ExternalInput"),
        g_next=mk_dram(cx, p + "g_next", [128, 8], F32, "ExternalInput"),
        conv_w=mk_dram(cx, p + "conv_w", [128, NFF * 3], F32, "ExternalInput"),
        conv_b=mk_dram(cx, p + "conv_b", [128, NFF], F32, "ExternalInput"))


def emit_tail(cx, own, dr, want_hn, fused):
    nc = cx.nc
    NT = own // 512
    if True:
        hin, cat, w_out, w_up, w_down = dr["hin"], dr.get("cat"), dr["w_out"], dr["w_up"], dr["w_down"]
        gff_d, gnx_d, cw_d, cb_d, hout, hnout = dr["g_ffn"], dr["g_next"], dr["conv_w"], dr["conv_b"], dr["hout"], dr.get("hn")
        hin_halo = dr.get("hin_halo")
        hlast = dr.get("hlast")
        hin_ap = hin.t.ap().rearrange("(c p) t -> p c t", p=128)
        cat_ap = cat.t.ap().rearrange("(c p) t -> p c t", p=128) if cat is not None else None
        cat_win = dr.get("cat_win")
        hn_outs = dr.get("hn_outs")
        hout_ap = hout.t.ap().rearrange("(c p) t -> p c t", p=128)
        hn_ap = hnout.t.ap().rearrange("(c p) t -> p c t", p=128) if hnout is not None else None
        wout_ap = w_out.t.ap().rearrange("(k p) n -> p k n", p=128)
        wup_ap = w_up.t.ap().rearrange("(c p) f -> c p f", p=128)
        wdown_ap = w_down.t.ap().rearrange("(k p) n -> p k n", p=128)
        hoff = 2 if hin_halo is None else 0

        wout_b = cx.sb("wout_b", [128, 8, 1024], BF16)
        wdown_b = cx.sb("wdown_b", [128, NFF, 1024], BF16)
        stg = [cx.sb("stg%d" % i, [128, 8, 256], F32) for i in range(2)]
        wup_b = [cx.sb("wup_b%d" % i, [128, 8, 256], BF16) for i in range(3)]
        h1s = [cx.sb("h1_%d" % i, [128, 8, 512], F32) for i in range(2)]
        catb = cx.sb("catb", [128, 8, 512], BF16)
        hb = cx.sb("hb", [128, 8, 512], BF16)
        sqs = [cx.sb("sq%d" % i, [128, 512], BF16) for i in range(2)]
        rstd = cx.sb("rstd", [128, 512], F32)
        rstd2 = cx.sb("rstd2", [128, 512], F32)
        Us = [cx.sb("U%d" % i, [128, 514], F32) for i in range(2)]
        t1s = [cx.sb("t1_%d" % i, [128, 512], F32) for i in range(2)]
        t2s = [cx.sb("t2_%d" % i, [128, 512], F32) for i in range(2)]
        sgs = [cx.sb("sg%d" % i, [128, 512], F32) for i in range(2)]
        actb = cx.sb("actb", [128, NFF, 512], BF16)
        tmpo = [cx.sb("tmpo%d" % i, [128, 512], F32) for i in range(2)]
        hnb = hb
        carry = cx.sb("carry", [128, NFF, 2], F32)
        h1h = cx.sb("h1h", [128, 8, 2], F32)
        cath = cx.sb("cath", [128, 8, 2], BF16)
        hbh = cx.sb("hbh", [128, 8, 2], BF16)
        rstdh = cx.sb("rstdh", [128, 2], F32)
        gff = cx.sb("gff", [128, 8], F32)
        gnx = cx.sb("gnx", [128, 8], F32)
        cw = cx.sb("cw", [128, NFF * 3], F32)
        cb = cx.sb("cb", [128, NFF], F32)
        ones = cx.sb("ones", [128, 128], BF16)

        pms = [cx.ps("pm%d" % i, [128, 512]) for i in range(2)]
        pssq = cx.ps("pssq", [128, 512])
        pus = [cx.ps("pu%d" % i, [128, 512]) for i in range(2)]
        pvs = [cx.ps("pv%d" % i, [128, 512]) for i in range(2)]
        puh = cx.ps("puh", [128, 512])

        V, A, P, PE = nc.vector, nc.scalar, nc.gpsimd, nc.tensor

        cx.op("dve", lambda: V.memset(ones[:], 1.0), writes=[ones])
        epst = cx.sb("epst", [128, 1], F32)
        cx.op("dve", lambda: V.memset(epst[:], EPS), writes=[epst])
        for dst, src in ((gff, gff_d), (gnx, gnx_d), (cw, cw_d), (cb, cb_d)):
            cx.dma("sp", dst, dst[:], src, src.t.ap())

        def load_wup(gidx):
            c = gidx % NFF
            s = stg[gidx % 2]
            cx.dma("sp", s, s[:].rearrange("p a b -> p (a b)"), w_up, wup_ap[c])
            wb = wup_b[gidx % 3]
            cx.op("pool", lambda: P.tensor_copy(out=wb[:], in_=s[:]), reads=[s], writes=[wb])

        for k in range(8):
            s = stg[k % 2]
            sv = s[:].rearrange("p a b -> p (a b)")
            cx.dma("sp", s, sv[:, 0:1024], w_out, wout_ap[:, k, :])
            cx.op("act", lambda: A.copy(out=wout_b[:, k, :], in_=sv[:, 0:1024]), reads=[s], writes=[wout_b])
        for c in range(NFF):
            s = stg[c % 2]
            sv = s[:].rearrange("p a b -> p (a b)")
            cx.dma("sp", s, sv[:, 0:1024], w_down, wdown_ap[:, c, :])
            cx.op("act", lambda: A.copy(out=wdown_b[:, c, :], in_=sv[:, 0:1024]), reads=[s], writes=[wdown_b])

        def wout_stage(h1, catt, n):
            for nch in range(8):
                pm = pms[nch % 2]
                for k in range(8):
                    cx.op("pe", lambda: PE.matmul(pm[:, 0:n], wout_b[:, k, nch * 128:(nch + 1) * 128], catt[:, k, 0:n],
                                                  start=(k == 0), stop=(k == 7)),
                          reads=[wout_b, catt], writes=[pm], acc=(k > 0))
                cx.op("dve", lambda: V.tensor_tensor(h1[:, nch, 0:n], h1[:, nch, 0:n], pm[:, 0:n], ALU.add),
                      reads=[pm, h1], writes=[h1])

        def norm_stage(h1, n, g, hbt, rs):
            for k in range(8):
                sq = sqs[k % 2]
                cx.op("act", lambda: A.activation(out=sq[:, 0:n], in_=h1[:, k, 0:n], func=AF.Square),
                      reads=[h1], writes=[sq])
                cx.op("pe", lambda: PE.matmul(pssq[:, 0:n], ones[:, :], sq[:, 0:n], start=(k == 0), stop=(k == 7)),
                      reads=[sq, ones], writes=[pssq], acc=(k > 0))
                if hbt is not None:
                    cx.op("dve", lambda: V.tensor_scalar(hbt[:, k, 0:n], h1[:, k, 0:n], g[:, k:k + 1], None, ALU.mult),
                          reads=[h1, g], writes=[hbt])
            cx.op("act", lambda: A.activation(out=rs[:, 0:n], in_=pssq[:, 0:n], func=AF.Ln, scale=1.0 / D, bias=epst[:, 0:1]),
                  reads=[pssq, epst], writes=[rs])
            cx.op("act", lambda: A.activation(out=rs[:, 0:n], in_=rs[:, 0:n], func=AF.Exp, scale=-0.5),
                  reads=[rs], writes=[rs])

        V, A = nc.vector, nc.scalar
        if fused:
            selt = cx.sb("selt", [128, 4], F32)
            cx.dma("sp", selt, selt[:], dr["sel"], dr["sel"].t.ap())
            cands = [cx.sb("cand%d" % i, [128, 8, 512], BF16) for i in range(2)]
            candh = [cx.sb("candh%d" % i, [128, 8, 2], BF16) for i in range(2)]
            candf = [cx.sb("candf%d" % i, [128, 8, 2], F32) for i in range(2)]

        def blend(dst, srcs, width, bufs, src_t):
            first = True
            for n_, (j, (src_t, ap_)) in enumerate(srcs):
                cb_ = bufs[n_ % 2]
                cx.dma("sp", cb_, cb_[:, :, 0:width], src_t, ap_)
                if first:
                    cx.op("act", lambda: A.activation(out=dst[:, :, 0:width], in_=cb_[:, :, 0:width], func=AF.Copy, scale=selt[:, j:j + 1]),
                          reads=[cb_, selt], writes=[dst])
                    first = False
                else:
                    cx.op("dve", lambda: V.scalar_tensor_tensor(dst[:, :, 0:width], cb_[:, :, 0:width], selt[:, j:j + 1], dst[:, :, 0:width],
                                                                ALU.mult, ALU.add),
                          reads=[cb_, selt, dst], writes=[dst])

        def catw(t0, width):
            d_, a_ = cat_win(t0, width)
            return d_, a_.rearrange("(c p) t -> p c t", p=128)

        def load_tile(ti, h1):
            cx.dma("sp", h1, h1[:], hin, hin_ap[:, :, hoff + ti * 512:hoff + (ti + 1) * 512])
            if not fused:
                cx.dma("sp", catb, catb[:], cat, cat_ap[:, :, 2 + ti * 512:2 + (ti + 1) * 512])
            else:
                blend(catb, [(j, catw(j * own + ti * 512, 512)) for j in range(4)], 512, cands, None)

        if not fused:
            cx.dma("sp", h1h, h1h[:], hin, hin_ap[:, :, 0:2])
            cx.dma("sp", cath, cath[:], cat, cat_ap[:, :, 0:2])
        else:
            blend(cath, [(j, catw(j * own - 2, 2)) for j in range(1, 4)], 2, candh, None)
            if hin_halo is None:
                cx.dma("sp", h1h, h1h[:], hin, hin_ap[:, :, 0:2])
            else:
                hh_ap = hin_halo.t.ap().rearrange("(r c p) t -> r p c t", r=4, p=128)
                blend(h1h, [(r + 1, (hin_halo, hh_ap[r])) for r in range(3)], 2, candf, None)
        wout_stage(h1h, cath, 2)
        norm_stage(h1h, 2, gff, hbh, rstdh)

        load_wup(0)
        load_wup(1)
        for ti in range(NT):
            h1 = h1s[ti % 2]
            load_tile(ti, h1)
            wout_stage(h1, catb, 512)
            norm_stage(h1, 512, gff, hb, rstd)
            for c in range(NFF):
                gidx = ti * NFF + c
                if gidx + 2 < NT * NFF:
                    load_wup(gidx + 2)
                wb = wup_b[gidx % 3]
                pu, pv, U = pus[c % 2], pvs[c % 2], Us[c % 2]
                t1, t2, sg = t1s[c % 2], t2s[c % 2], sgs[c % 2]
                for k in range(8):
                    cx.op("pe", lambda: PE.matmul(pu[:], wb[:, k, 0:128], hb[:, k, :], start=(k == 0), stop=(k == 7)),
                          reads=[wb, hb], writes=[pu], acc=(k > 0))
                if ti == 0:
                    for k in range(8):
                        cx.op("pe", lambda: PE.matmul(puh[:, 0:2], wb[:, k, 0:128], hbh[:, k, 0:2], start=(k == 0), stop=(k == 7)),
                              reads=[wb, hbh], writes=[puh], acc=(k > 0))
                for k in range(8):
                    cx.op("pe", lambda: PE.matmul(pv[:], wb[:, k, 128:256], hb[:, k, :], start=(k == 0), stop=(k == 7)),
                          reads=[wb, hb], writes=[pv], acc=(k > 0))
                if ti == 0:
                    cx.op("dve", lambda: V.tensor_tensor(U[:, 0:2], puh[:, 0:2], rstdh[:, 0:2], ALU.mult),
                          reads=[puh, rstdh], writes=[U])
                else:
                    cx.op("act", lambda: A.copy(out=U[:, 0:2], in_=carry[:, c, :]), reads=[carry], writes=[U])
                cx.op("dve", lambda: V.tensor_tensor(U[:, 2:514], pu[:], rstd[:], ALU.mult),
                      reads=[pu, rstd, U], writes=[U])
                cx.op("act", lambda: A.copy(out=carry[:, c, :], in_=U[:, 512:514]), reads=[U], writes=[carry])
                cx.op("act", lambda: A.activation(out=t1[:], in_=U[:, 0:512], func=AF.Identity,
                                                  scale=cw[:, 3 * c:3 * c + 1], bias=cb[:, c:c + 1]),
                      reads=[U, cw, cb], writes=[t1])
                cx.op("dve", lambda: V.scalar_tensor_tensor(t2[:], U[:, 1:513], cw[:, 3 * c + 1:3 * c + 2], t1[:], ALU.mult, ALU.add),
                      reads=[U, cw, t1], writes=[t2])
                cx.op("dve", lambda: V.scalar_tensor_tensor(t1[:], U[:, 2:514], cw[:, 3 * c + 2:3 * c + 3], t2[:], ALU.mult, ALU.add),
                      reads=[U, cw, t2], writes=[t1])
                cx.op("act", lambda: A.activation(out=sg[:], in_=t1[:], func=AF.Silu), reads=[t1], writes=[sg])
                cx.op("dve", lambda: V.tensor_tensor(actb[:, c, :], sg[:], pv[:], ALU.mult),
                      reads=[sg, pv], writes=[actb])
            for nch in range(8):
                pm = pms[nch % 2]
                tm = tmpo[nch % 2]
                for c in range(NFF):
                    cx.op("pe", lambda: PE.matmul(pm[:], wdown_b[:, c, nch * 128:(nch + 1) * 128], actb[:, c, :],
                                                  start=(c == 0), stop=(c == NFF - 1)),
                          reads=[wdown_b, actb], writes=[pm], acc=(c > 0))
                cx.op("dve", lambda: V.tensor_tensor(tm[:], pm[:], rstd[:], ALU.mult), reads=[pm, rstd], writes=[tm])
                cx.op("dve", lambda: V.tensor_tensor(h1[:, nch, :], h1[:, nch, :], tm[:], ALU.add),
                      reads=[tm, h1], writes=[h1])
            if want_hn:
                norm_stage(h1, 512, gnx, None, rstd2)
                for k in range(8):
                    cx.op("dve", lambda: V.scalar_tensor_tensor(hnb[:, k, :], h1[:, k, :], gnx[:, k:k + 1], rstd2[:], ALU.mult, ALU.mult),
                          reads=[h1, gnx, rstd2], writes=[hnb])
                if hn_outs is None:
                    cx.dma("sp", hnout, hn_ap[:, :, ti * 512:(ti + 1) * 512], hnb, hnb[:])
                else:
                    for (hd_, ha_, ho_, hw_) in hn_outs(ti):
                        cx.dma("sp", hd_, ha_.rearrange("(c p) t -> p c t", p=128), hnb, hnb[:, :, ho_:ho_ + hw_])
            cx.dma("sp", hout, hout_ap[:, :, ti * 512:(ti + 1) * 512], h1, h1[:])
            if hlast is not None and ti == NT - 1:
                cx.dma("sp", hlast, hlast.t.ap().rearrange("(c p) t -> p c t", p=128), h1, h1[:, :, 510:512])


def build_mix0(S, do_b=True, do_ret=True, do_sbn=True, seclimit=10 ** 9):
    nc = bass.Bass("TRN2", target_bir_lowering=False)
    with contextlib.ExitStack() as es:
        cx = Ctx(nc, es)
        cx.seclimit = seclimit
        dr = _mix0_drams(cx, S, "")
        dr["outT"] = mk_dram(cx, "outT", [256, S], BF16, "ExternalOutput")
        cx.begin_phase("")
        emit_mix0(cx, S, dr, do_b, do_ret, do_sbn)
        cx.end_phase()
        cx.finish()
    return nc


def _mix0_drams(cx, S, p):
    return dict(
        xT=mk_dram(cx, p + "xT", [(S // 512) * 128, 4096], F32, "ExternalInput"),
        wsel=mk_dram(cx, p + "wsel", [D, 896], F32, "ExternalInput"),
        g_mix=mk_dram(cx, p + "g_mix", [128, 8], F32, "ExternalInput"),
        gains=mk_dram(cx, p + "gains", [1, 384], F32, "ExternalInput"),
        rope=mk_dram(cx, p + "rope", [S, 256], F32, "ExternalInput"),
        rconst=mk_dram(cx, p + "rconst", [128, 131], F32, "ExternalInput"),
        cmat=mk_dram(cx, p + "cmat", [128, 384], BF16, "ExternalInput"),
        dmask=mk_dram(cx, p + "dmask", [128, 2048], F32, "ExternalInput"))


def emit_mix0(cx, S, dr, do_b=True, do_ret=True, do_sbn=True):
    nc = cx.nc
    NCH = S // 128
    NT = S // 512
    if True:
        xT, wsel, gmix_d, gains_d, rope_d, rconst_d, cmat_d, dmask_d, outT = (
            dr["xT"], dr["wsel"], dr["g_mix"], dr["gains"], dr["rope"], dr["rconst"], dr["cmat"], dr["dmask"], dr.get("outT"))

        x_ap = xT.t.ap().rearrange("(n p) f -> n p f", p=128)
        w_ap = wsel.t.ap().rearrange("(k p) n -> p k n", p=128)
        rope_ap = rope_d.t.ap().rearrange("(n p) c -> p n c", p=128)
        if "out_win" in dr:
            out_win = dr["out_win"]
        else:
            out_win = lambda ti: (outT, outT.t.ap()[:, ti * 512:(ti + 1) * 512])

        V, A, P, PE = nc.vector, nc.scalar, nc.gpsimd, nc.tensor

        qT = cx.sb("qT", [128, S], BF16)
        kT = cx.sb("kT", [128, S], BF16)
        vres = cx.sb("vres", [128, NCH, 128], BF16)
        wb = cx.sb("wb", [128, 8, 896], BF16)
        gmix = cx.sb("gmix", [128, 8], F32)
        gains = cx.sb("gains_s", [128, 384], F32)
        rconst = cx.sb("rconst_s", [128, 131], F32)
        cmat = cx.sb("cmat_s", [128, 384], BF16)
        dmask = cx.sb("dmask_s", [128, 2048], F32)
        ones = cx.sb("ones", [128, 128], BF16)
        epst = cx.sb("epst", [128, 1], F32)
        cx.op("dve", lambda: V.memset(ones[:], 1.0), writes=[ones])
        cx.op("dve", lambda: V.memset(epst[:], EPS), writes=[epst])
        cx.dma("sp", gmix, gmix[:], gmix_d, gmix_d.t.ap())
        cx.dma("sp", gains, gains[:], gains_d, gains_d.t.ap()[0:1, :].broadcast_to([128, 384]))
        cx.dma("sp", rconst, rconst[:], rconst_d, rconst_d.t.ap())
        cx.dma("sp", cmat, cmat[:], cmat_d, cmat_d.t.ap())
        cx.dma("sp", dmask, dmask[:], dmask_d, dmask_d.t.ap())
        cx.op("dve", lambda: V.tensor_scalar(gains[:, 0:128], gains[:, 0:128], 0.125, None, ALU.mult),
              reads=[gains], writes=[gains])
        ident = cmat
        decT = rconst

        with contextlib.ExitStack() as esA:
            cxa_sb = lambda name, shape, dt: T(esA.enter_context(nc.sbuf_tensor(cx.pfx + name, list(shape), dt)), cx.pfx + name)
            def cxa_ps(name, shape, dt=F32):
                t = T(esA.enter_context(nc.psum_tensor(cx.pfx + name, list(shape), dt)), cx.pfx + name)
                t.excl = True
                return t
            stg = [cxa_sb("stg%d" % i, [128, 896], F32) for i in range(2)]
            xts = [cxa_sb("xt%d" % i, [128, 8, 512], F32) for i in range(2)]
            ropes = [cxa_sb("rope%d" % i, [128, 4, 256], F32) for i in range(2)]
            hb = cxa_sb("hb", [128, 8, 512], BF16)
            sq = cxa_sb("sq", [128, 8, 512], BF16)
            rstd = cxa_sb("rstd", [128, 4], F32)
            Rs = [cxa_sb("R%d" % i, [128, 512], F32) for i in range(2)]
            SBs = [cxa_sb("SB%d" % i, [128, 384], F32) for i in range(2)]
            sqt = cxa_sb("sqt", [128, 256], F32)
            qs4 = cxa_sb("qs4", [128, 4], F32)
            rq4 = cxa_sb("rq4", [128, 4], F32)
            qkn = cxa_sb("qkn", [128, 256], BF16)
            ra = [cxa_sb("ra%d" % i, [128, 2, 64], F32) for i in range(4)]
            qkr = cxa_sb("qkr", [128, 2, 128], BF16)
            qTr = cxa_sb("qTr", [128, 128], BF16)
            kTr = cxa_sb("kTr", [128, 128], BF16)
            kz = cxa_sb("kz", [128, 128], BF16)
            vb = cxa_sb("vb", [128, 128], BF16)
            ATb = cxa_sb("ATb", [128, 128], BF16)
            o1 = cxa_sb("o1", [128, 128], F32)
            o2 = cxa_sb("o2", [128, 128], F32)
            junk = cxa_sb("junk", [128, 128], F32)
            os1 = cxa_sb("os1", [128, 1], F32)
            rr = cxa_sb("rr", [128, 1], F32)
            sgt = cxa_sb("sgt", [128, 128], F32)
            ob = cxa_sb("ob", [128, 128], BF16)
            Rst = cxa_sb("Rst", [128, 128], F32)
            Rstb = cxa_sb("Rstb", [128, 128], BF16)
            outb = [cxa_sb("outb%d" % i, [128, 512], BF16) for i in range(2)]

            pp0 = [cxa_ps("pp0_%d" % i, [128, 512]) for i in range(2)]
            pp1 = [cxa_ps("pp1_%d" % i, [128, 512]) for i in range(2)]
            pssq = cxa_ps("pssq", [128, 4])
            ptr_t = cxa_ps("ptr", [128, 8, 128], BF16)
            ptr = [T(ptr_t.t, "ptr%d" % i, (slice(None), i, slice(None)), st=ptr_t) for i in range(8)]
            pmA_t = cxa_ps("pmA", [128, 4, 128])
            pmA = [T(pmA_t.t, "pmA%d" % i, (slice(None), i, slice(None)), st=pmA_t) for i in range(4)]

            for k in range(8):
                s = stg[k % 2]
                cx.dma("sp", s, s[:], wsel, w_ap[:, k, :])
                cx.op("act", lambda: A.copy(out=wb[:, k, :], in_=s[:]), reads=[s], writes=[wb])
            cx.op("dve", lambda: V.memset(Rst[:], 0.0), writes=[Rst])
            cx.op("dve", lambda: V.memset(Rstb[:], 0.0), writes=[Rstb])

            def lnexp_rstd(out_t, out_ap_, in_t, in_ap_, scale):
                cx.op("act", lambda: A.activation(out=out_ap_, in_=in_ap_, func=AF.Ln, scale=scale, bias=epst[:, 0:1]),
                      reads=[in_t, epst], writes=[out_t])
                cx.op("act", lambda: A.activation(out=out_ap_, in_=out_ap_, func=AF.Exp, scale=-0.5),
                      reads=[out_t], writes=[out_t])

            for ti in range(NT):
                xt = xts[ti % 2]
                rp = ropes[ti % 2]
                cx.dma("sp", xt, xt[:].rearrange("p a b -> p (a b)"), xT, x_ap[ti])
                cx.dma("sp", rp, rp[:], rope_d, rope_ap[:, ti * 4:(ti + 1) * 4, :])
                for k in range(8):
                    cx.op("act", lambda: A.activation(out=sq[:, k, :], in_=xt[:, k, :], func=AF.Square), reads=[xt], writes=[sq])
                    cx.op("dve", lambda: V.tensor_scalar(hb[:, k, :], xt[:, k, :], gmix[:, k:k + 1], None, ALU.mult),
                          reads=[xt, gmix], writes=[hb])
                for j in range(4):
                    for k in range(8):
                        cx.op("pe", lambda: PE.matmul(pssq[:, j:j + 1], sq[:, k, j * 128:(j + 1) * 128], ones[:, 0:1],
                                                      start=(k == 0), stop=(k == 7)),
                              reads=[sq, ones], writes=[pssq], acc=(k > 0 or j > 0))
                lnexp_rstd(rstd, rstd[:], pssq, pssq[:], 1.0 / D)
                ob_stage = outb[ti % 2]
                for j in range(4):
                    n = ti * 4 + j
                    p0, p1 = pp0[n % 2], pp1[n % 2]
                    R, SBt = Rs[n % 2], SBs[n % 2]
                    for k in range(8):
                        cx.op("pe", lambda: PE.matmul(p0[:], hb[:, k, j * 128:(j + 1) * 128], wb[:, k, 0:512],
                                                      start=(k == 0), stop=(k == 7)),
                              reads=[hb, wb], writes=[p0], acc=(k > 0))
                    for k in range(8):
                        cx.op("pe", lambda: PE.matmul(p1[:, 0:384], hb[:, k, j * 128:(j + 1) * 128], wb[:, k, 512:896],
                                                      start=(k == 0), stop=(k == 7)),
                              reads=[hb, wb], writes=[p1], acc=(k > 0))
                    cx.op("act", lambda: A.activation(out=R[:], in_=p0[:], func=AF.Copy, scale=rstd[:, j:j + 1]),
                          reads=[p0, rstd], writes=[R])
                    cx.op("act", lambda: A.activation(out=SBt[:], in_=p1[:, 0:384], func=AF.Copy, scale=rstd[:, j:j + 1]),
                          reads=[p1, rstd], writes=[SBt])
                    cx.mute = not do_sbn
                    cx.op("dve", lambda: V.tensor_tensor(sqt[:], SBt[:, 0:256], SBt[:, 0:256], ALU.mult), reads=[SBt], writes=[sqt])
                    cx.op("dve", lambda: V.tensor_reduce(qs4[:], sqt[:].rearrange("p (a d) -> p a d", d=64), AX.X, ALU.add),
                          reads=[sqt], writes=[qs4])
                    lnexp_rstd(rq4, rq4[:], qs4, qs4[:], 1.0 / 64)
                    for h4 in range(4):
                        sl = slice(h4 * 64, (h4 + 1) * 64)
                        cx.op("dve", lambda: V.scalar_tensor_tensor(qkn[:, sl], SBt[:, sl], rq4[:, h4:h4 + 1], gains[:, sl], ALU.mult, ALU.mult),
                              reads=[SBt, rq4, gains], writes=[qkn])
                    cx.op("pe", lambda: PE.transpose(ptr[0][:], qkn[:, 0:128], ident[:, 0:128]), reads=[qkn, ident], writes=[ptr[0]])
                    cx.op("pe", lambda: PE.transpose(ptr[1][:], qkn[:, 128:256], ident[:, 0:128]), reads=[qkn, ident], writes=[ptr[1]])
                    cx.op("act", lambda: A.copy(out=qT[:, n * 128:(n + 1) * 128], in_=ptr[0][:]), reads=[ptr[0]], writes=[qT])
                    cx.op("act", lambda: A.copy(out=kT[:, n * 128:(n + 1) * 128], in_=ptr[1][:]), reads=[ptr[1]], writes=[kT])
                    cx.op("pool", lambda: P.tensor_copy(out=vres[:, n, :], in_=SBt[:, 256:384]), reads=[SBt], writes=[vres])
                    cx.mute = not do_ret
                    cx.insec = True
                    Rv = R[:, 0:256].rearrange("p (a d) -> p a d", a=2)
                    CC = rp[:, j, 0:128].rearrange("p (a d) -> p a d", a=2)
                    SS = rp[:, j, 128:256].rearrange("p (a d) -> p a d", a=2)
                    cx.op("dve", lambda: V.tensor_tensor(ra[0][:], Rv[:, :, 0:64], CC, ALU.mult), reads=[R, rp], writes=[ra[0]])
                    cx.op("dve", lambda: V.tensor_tensor(ra[1][:], Rv[:, :, 64:128], SS, ALU.mult), reads=[R, rp], writes=[ra[1]])
                    cx.op("dve", lambda: V.tensor_tensor(ra[2][:], Rv[:, :, 0:64], SS, ALU.mult), reads=[R, rp], writes=[ra[2]])
                    cx.op("dve", lambda: V.tensor_tensor(ra[3][:], Rv[:, :, 64:128], CC, ALU.mult), reads=[R, rp], writes=[ra[3]])
                    cx.op("dve", lambda: V.tensor_tensor(qkr[:, :, 0:64], ra[0][:], ra[1][:], ALU.subtract), reads=[ra[0], ra[1]], writes=[qkr])
                    cx.op("dve", lambda: V.tensor_tensor(qkr[:, :, 64:128], ra[2][:], ra[3][:], ALU.add), reads=[ra[2], ra[3]], writes=[qkr])
                    cx.op("pe", lambda: PE.transpose(ptr[2][:], qkr[:, 0, :], ident[:, 0:128]), reads=[qkr, ident], writes=[ptr[2]])
                    cx.op("pe", lambda: PE.transpose(ptr[3][:], qkr[:, 1, :], ident[:, 0:128]), reads=[qkr, ident], writes=[ptr[3]])
                    cx.op("act", lambda: A.copy(out=qTr[:], in_=ptr[2][:]), reads=[ptr[2]], writes=[qTr])
                    cx.op("act", lambda: A.copy(out=kTr[:], in_=ptr[3][:]), reads=[ptr[3]], writes=[kTr])
                    cx.op("dve", lambda: V.tensor_scalar(kz[:], qkr[:, 1, :], rconst[:, 128:129], None, ALU.mult), reads=[qkr, rconst], writes=[kz])
                    cx.op("pool", lambda: P.tensor_copy(out=vb[:], in_=R[:, 256:384]), reads=[R], writes=[vb])
                    psc, po, pc, pk = pmA
                    cx.op("pe", lambda: PE.matmul(psc[:], kTr[:], qTr[:], start=True, stop=True), reads=[kTr, qTr], writes=[psc])
                    cx.op("dve", lambda: V.tensor_tensor(ATb[:], psc[:], decT[:, 0:128], ALU.mult), reads=[psc, decT], writes=[ATb])
                    cx.op("pe", lambda: PE.matmul(pc[:], qTr[:], Rstb[:], start=True, stop=True), reads=[qTr, Rstb], writes=[pc])
                    cx.op("pe", lambda: PE.matmul(po[:], ATb[:], vb[:], start=True, stop=True), reads=[ATb, vb], writes=[po])
                    cx.op("pe", lambda: PE.matmul(pk[:], kz[:], vb[:], start=True, stop=True), reads=[kz, vb], writes=[pk])
                    cx.op("act", lambda: A.activation(out=o1[:], in_=pc[:], func=AF.Copy, scale=rconst[:, 129:130]),
                          reads=[pc, rconst], writes=[o1])
                    cx.op("dve", lambda: V.tensor_tensor(o2[:], o1[:], po[:], ALU.add), reads=[o1, po], writes=[o2])
                    cx.op("dve", lambda: V.scalar_tensor_tensor(Rst[:], Rst[:], rconst[:, 130:131], pk[:], ALU.mult, ALU.add),
                          reads=[Rst, rconst, pk], writes=[Rst])
                    cx.op("act", lambda: A.copy(out=Rstb[:], in_=Rst[:]), reads=[Rst], writes=[Rstb])
                    cx.op("act", lambda: A.activation(out=junk[:], in_=o2[:], func=AF.Square, accum_out=os1[:, 0:1]),
                          reads=[o2], writes=[junk, os1])
                    lnexp_rstd(rr, rr[:], os1, os1[:], 1.0 / 128)
                    cx.op("act", lambda: A.activation(out=sgt[:], in_=R[:, 384:512], func=AF.Silu), reads=[R], writes=[sgt])
                    cx.op("dve", lambda: V.scalar_tensor_tensor(o1[:], o2[:], rr[:, 0:1], gains[:, 256:384], ALU.mult, ALU.mult),
                          reads=[o2, rr, gains], writes=[o1])
                    cx.op("dve", lambda: V.tensor_tensor(ob[:], o1[:], sgt[:], ALU.mult), reads=[o1, sgt], writes=[ob])
                    cx.op("pe", lambda: PE.transpose(ptr[4][:], ob[:], ident[:, 0:128]), reads=[ob, ident], writes=[ptr[4]])
                    cx.op("act", lambda: A.copy(out=ob_stage[:, j * 128:(j + 1) * 128], in_=ptr[4][:]), reads=[ptr[4]], writes=[ob_stage])
                cx.mute = False
                cx.insec = False
                od_, oa_ = out_win(ti)
                cx.dma("sp", od_, oa_[128:256, :], ob_stage, ob_stage[:])
            cx.barrier()

        with contextlib.ExitStack() as esB:
          if do_b:
                cxb_sb = lambda name, shape, dt: T(esB.enter_context(nc.sbuf_tensor(cx.pfx + name, list(shape), dt)), cx.pfx + name)
                def cxb_ps(name, shape, dt=F32):
                    t = T(esB.enter_context(nc.psum_tensor(cx.pfx + name, list(shape), dt)), cx.pfx + name)
                    t.excl = True
                    return t
                Es = [[cxb_sb("E%d_%d" % (h, i), [128, 512], F32) for i in range(2)] for h in range(2)]
                SPs = [[cxb_sb("SP%d_%d" % (h, i), [128, 512], BF16) for i in range(3)] for h in range(2)]
                Xs = [[cxb_sb("X%d_%d" % (h, i), [128, 512], F32) for i in range(2)] for h in range(2)]
                Ws = [[cxb_sb("W%d_%d" % (h, i), [128, 512], BF16) for i in range(2)] for h in range(2)]
                oa = [[cxb_sb("oa%d_%d" % (h, i), [64, 512], BF16) for i in range(2)] for h in range(2)]
                pz = [[cxb_ps("pz%d_%d" % (h, i), [128, 512]) for i in range(2)] for h in range(2)]
                pcs = [cxb_ps("pcs%d" % h, [128, 512]) for h in range(2)]
                pout = [cxb_ps("pout%d" % h, [64, 512]) for h in range(2)]
                Umat = cmat
                for qs in range(NT):
                    kbs = list(range(4 * qs + 3, -1, -1))
                    nst = len(kbs)

                    def zmm(i):
                        kb = kbs[i]
                        for h in range(2):
                            hs = slice(h * 64, (h + 1) * 64)
                            p = pz[h][i % 2]
                            cx.op("pe", lambda: PE.matmul(p[:], kT[hs, kb * 128:(kb + 1) * 128], qT[hs, qs * 512:(qs + 1) * 512],
                                                          start=True, stop=True), reads=[kT, qT], writes=[p])
                    zmm(0)
                    for i in range(nst):
                        kb = kbs[i]
                        if i + 1 < nst:
                            zmm(i + 1)
                        for h in range(2):
                            E, SP, p = Es[h][i % 2], SPs[h][i % 3], pz[h][i % 2]
                            cx.op("act", lambda: A.activation(out=E[:], in_=p[:], func=AF.Exp), reads=[p], writes=[E])
                            if kb >= 4 * qs:
                                dm = kb - 4 * qs
                                cx.op("pool", lambda: P.tensor_tensor(E[:], E[:], dmask[:, dm * 512:(dm + 1) * 512], ALU.mult),
                                      reads=[E, dmask], writes=[E])
                            cx.op("act", lambda: A.activation(out=SP[:], in_=E[:], func=AF.Ln, bias=1.0), reads=[E], writes=[SP])
                        for h in range(2):
                            SP = SPs[h][i % 3]
                            cx.op("pe", lambda: PE.matmul(pcs[h][:], Umat[:, 128:256], SP[:], start=(i == 0), stop=False, skip_group_check=True),
                                  reads=[Umat, SP], writes=[pcs[h]], acc=(i > 0))
                            if i > 0:
                                SPp = SPs[h][(i - 1) % 3]
                                cx.op("pe", lambda: PE.matmul(pcs[h][:], Umat[:, 256:384], SPp[:], start=False, stop=False, skip_group_check=True),
                                      reads=[Umat, SPp], writes=[pcs[h]], acc=True)
                        for h in range(2):
                            X = Xs[h][i % 2]
                            cx.op("act", lambda: A.activation(out=X[:], in_=pcs[h][:], func=AF.Exp, scale=-1.0), reads=[pcs[h]], writes=[X])
                        for h in range(2):
                            X, W, E = Xs[h][i % 2], Ws[h][i % 2], Es[h][i % 2]
                            cx.op("dve", lambda: V.tensor_tensor(W[:], E[:], X[:], ALU.mult), reads=[E, X], writes=[W])
                        for h in range(2):
                            W = Ws[h][i % 2]
                            cx.op("pe", lambda: PE.matmul(pout[h][:], vres[:, kb, h * 64:(h + 1) * 64], W[:], start=(i == 0), stop=(i == nst - 1)),
                                  reads=[vres, W], writes=[pout[h]], acc=(i > 0))
                    for h in range(2):
                        o = oa[h][qs % 2]
                        cx.op("dve", lambda: V.tensor_copy(out=o[:], in_=pout[h][:]), reads=[pout[h]], writes=[o])
                        od_, oa_ = out_win(qs)
                        cx.dma("sp", od_, oa_[h * 64:(h + 1) * 64, :], o, o[:])


def _v8(g):
    return np.ascontiguousarray(np.asarray(g, np.float32).reshape(8, 128).T)


def _const_tables(S):
    pos = np.arange(S, dtype=np.float32)
    inv = (np.float32(10000.0) ** (-(np.arange(64, dtype=np.float32) / np.float32(64)))).astype(np.float32)
    ang = (pos[:, None] * inv[None, :]).astype(np.float32).astype(np.float64)
    cos, sin = np.cos(ang).astype(np.float32), np.sin(ang).astype(np.float32)
    rope = np.ascontiguousarray(np.concatenate([cos, cos, sin, sin], axis=1))
    j = np.arange(128)
    ident = (j[:, None] == j[None, :])
    U = (j[:, None] >= j[None, :])
    L = (j[:, None] < j[None, :])
    cmat = np.concatenate([ident, U, L], axis=1).astype(np.float32).astype(NPBF)
    strict = (j[:, None] < j[None, :]).astype(np.float32)
    dmask = np.zeros((128, 4, 512), np.float32)
    for dm in range(4):
        for qb in range(4):
            if qb == dm:
                dmask[:, dm, qb * 128:(qb + 1) * 128] = strict
            elif qb > dm:
                dmask[:, dm, qb * 128:(qb + 1) * 128] = 1.0
    return rope, cmat, np.ascontiguousarray(dmask.reshape(128, 2048))


def _ret_consts(g):
    lg = np.log(np.float32(1.0) - np.exp2(np.float32(-5.0 - g))).astype(np.float64)
    j = np.arange(128, dtype=np.float64)
    sc = 128.0 ** -0.5
    diff = j[None, :] - j[:, None]
    decT = np.where(diff >= 0, np.exp(lg * np.maximum(diff, 0.0)), 0.0) * sc
    zeta = np.exp(lg * (127.0 - j)) * sc
    xi = np.exp(lg * (j + 1.0))
    gch = np.full(128, np.exp(lg * 128.0))
    return np.ascontiguousarray(np.concatenate([decT, zeta[:, None], xi[:, None], gch[:, None]], axis=1).astype(np.float32))


def _mix0_maps(xT_b, S, mix_g, w_in, q_gain, k_gain, ret_gain, tables):
    rope, cmat, dmask = tables
    xtl = [np.ascontiguousarray(xb.reshape(8, 128, S // 512, 512).transpose(2, 1, 0, 3).reshape((S // 512) * 128, 4096)) for xb in xT_b]
    maps = []
    for c in range(NCORES):
        b, g = c // 4, c % 4
        cols = np.concatenate([
            1536 + g * 128 + np.arange(128), 2048 + g * 128 + np.arange(128),
            2560 + g * 128 + np.arange(128), 3072 + g * 128 + np.arange(128),
            g * 128 + np.arange(128), 512 + g * 128 + np.arange(128), 1024 + g * 128 + np.arange(128)])
        wsel = np.ascontiguousarray(w_in[:, cols])
        gains = np.concatenate([q_gain, q_gain, k_gain, k_gain, ret_gain]).astype(np.float32)[None, :]
        maps.append(dict(xT=xtl[b], wsel=wsel, g_mix=_v8(mix_g), gains=np.ascontiguousarray(gains), rope=rope,
                         rconst=_ret_consts(g), cmat=cmat, dmask=dmask))
    return maps


def build_gla(S, oplimit=10 ** 9):
    nc = bass.Bass("TRN2", target_bir_lowering=False)
    with contextlib.ExitStack() as es:
        cx = Ctx(nc, es)
        cx.oplimit = oplimit
        dr = _gla_drams(cx, "")
        dr["hnT"] = mk_dram(cx, "hnT", [D, S], BF16, "ExternalInput")
        dr["outT"] = mk_dram(cx, "outT", [256, S], BF16, "ExternalOutput")
        cx.begin_phase("")
        emit_gla(cx, S, dr, gathered=False)
        cx.end_phase()
        cx.finish()
    return nc


def _gla_drams(cx, p):
    return dict(
        wsel=mk_dram(cx, p + "wsel", [D, 784], F32, "ExternalInput"),
        walpha=mk_dram(cx, p + "walpha", [17, 128], F32, "ExternalInput"),
        gain=mk_dram(cx, p + "gain", [1, 256], F32, "ExternalInput"),
        cmat=mk_dram(cx, p + "cmat", [128, 384], BF16, "ExternalInput"))


def emit_gla(cx, S, dr, gathered):
    nc = cx.nc
    NT = S // 512
    if True:
        hnT, wsel, walpha_d, gain_d, cmat_d, outT = dr.get("hnT"), dr["wsel"], dr["walpha"], dr["gain"], dr["cmat"], dr.get("outT")
        own = S // 4
        if gathered:
            hn_wins = dr["hn_wins"]
        else:
            h_ap = hnT.t.ap().rearrange("(c p) t -> p c t", p=128)
            hn_wins = lambda ti: [(hnT, h_ap[:, :, ti * 512:(ti + 1) * 512], 0, 512)]
        if "out_win" in dr:
            out_win = dr["out_win"]
        else:
            out_win = lambda ti: (outT, outT.t.ap()[:, ti * 512:(ti + 1) * 512])
        w_ap = wsel.t.ap().rearrange("(k p) n -> p k n", p=128)
        V, A, P, PE = nc.vector, nc.scalar, nc.gpsimd, nc.tensor

        wb = cx.sb("wb", [128, 8, 784], BF16)
        stg = [cx.sb("stg%d" % i, [128, 784], F32) for i in range(2)]
        wal_f = cx.sb("wal_f", [17, 128], F32)
        wal_b = cx.sb("wal_b", [17, 128], BF16)
        gain = cx.sb("gain_s", [128, 256], F32)
        cmat = cx.sb("cmat_s", [128, 384], BF16)
        trif = cx.sb("trif", [128, 128], F32)
        ones = cx.sb("ones", [128, 2], BF16)
        epst = cx.sb("epst", [128, 1], F32)
        hts = [cx.sb("ht%d" % i, [128, 8, 512], BF16) for i in range(2)]
        gaT = cx.sb("gaT", [32, 128], BF16)
        ee = cx.sb("ee", [128, 128], F32)
        sp = cx.sb("sp", [128, 128], F32)
        sph = cx.sb("sph", [128, 128], BF16)
        spl = cx.sb("spl", [128, 128], BF16)
        csum = cx.sb("csum", [128, 128], F32)
        dif = cx.sb("dif", [128, 128], F32)
        eq = cx.sb("eq", [128, 128], F32)
        ek = cx.sb("ek", [128, 128], F32)
        ekd = cx.sb("ekd", [128, 128], F32)
        qt = cx.sb("qt", [128, 128], BF16)
        kt = cx.sb("kt", [128, 128], BF16)
        kd = cx.sb("kd", [128, 128], BF16)
        vb = cx.sb("vb", [128, 256], BF16)
        sg = cx.sb("sg", [128, 256], F32)
        qTf = cx.sb("qTf", [128, 128], BF16)
        q0T = cx.sb("q0T", [128, 128], BF16)
        q1T = cx.sb("q1T", [128, 128], BF16)
        kTf = cx.sb("kTf", [128, 128], BF16)
        ATb = cx.sb("ATb", [128, 128], BF16)
        St = cx.sb("St", [128, 256], F32)
        Sb = [cx.sb("Sb%d" % i, [128, 256], BF16) for i in range(2)]
        dec = cx.sb("dec", [128, 2], F32)
        junk = cx.sb("junk", [128, 256], F32)
        os1 = cx.sb("os1", [128, 1], F32)
        rr = cx.sb("rr", [128, 1], F32)
        y1 = cx.sb("y1", [128, 256], F32)
        y2 = cx.sb("y2", [128, 256], BF16)
        outb = [cx.sb("outb%d" % i, [128, 2, 512], BF16) for i in range(2)]

        pA = [cx.ps("pA%d" % i, [128, 512]) for i in range(2)]
        pB_t = cx.ps("pBt", [128, 2, 256])
        pB = [T(pB_t.t, "pB%d" % i, (slice(None), i, slice(None)), st=pB_t) for i in range(2)]
        pm_t = cx.ps("pm", [128, 4, 128])
        py, pcum, ptot, paT = [T(pm_t.t, "pm%d" % i, (slice(None), i, slice(None)), st=pm_t) for i in range(4)]
        po = cx.ps("po", [128, 256])
        pk = cx.ps("pk", [128, 256])
        pm2_t = cx.ps("pm2", [128, 512])
        pga = T(pm2_t.t, "pga", (slice(0, 16), slice(0, 128)), st=pm2_t)
        pdec = T(pm2_t.t, "pdec", (slice(None), slice(128, 130)), st=pm2_t)
        ptr_t = cx.ps("ptr", [128, 8, 128], BF16)
        ptr = [T(ptr_t.t, "ptr%d" % i, (slice(None), i, slice(None)), st=ptr_t) for i in range(4)]

        cx.op("dve", lambda: V.memset(ones[:], 1.0), writes=[ones])
        cx.op("dve", lambda: V.memset(epst[:], EPS), writes=[epst])
        cx.op("dve", lambda: V.memset(gaT[:], 1.0), writes=[gaT])
        cx.op("dve", lambda: V.memset(q0T[:], 0.0), writes=[q0T])
        cx.op("dve", lambda: V.memset(q1T[:], 0.0), writes=[q1T])
        cx.op("dve", lambda: V.memset(St[:], 0.0), writes=[St])
        cx.op("dve", lambda: V.memset(Sb[0][:], 0.0), writes=[Sb[0]])
        cx.dma("sp", wal_f, wal_f[:], walpha_d, walpha_d.t.ap())
        cx.dma("sp", gain, gain[:], gain_d, gain_d.t.ap()[0:1, :].broadcast_to([128, 256]))
        cx.dma("sp", cmat, cmat[:], cmat_d, cmat_d.t.ap())
        cx.op("act", lambda: A.copy(out=wal_b[:], in_=wal_f[:]), reads=[wal_f], writes=[wal_b])
        cx.op("dve", lambda: V.tensor_copy(out=trif[:], in_=cmat[:, 128:256]), reads=[cmat], writes=[trif])
        for k in range(8):
            s = stg[k % 2]
            cx.dma("sp", s, s[:], wsel, w_ap[:, k, :])
            cx.op("act", lambda: A.copy(out=wb[:, k, :], in_=s[:]), reads=[s], writes=[wb])
        ident, tri, blk = cmat, cmat, cmat

        def lnexp_rstd(out_t, out_ap_, in_t, in_ap_, scale):
            cx.op("act", lambda: A.activation(out=out_ap_, in_=in_ap_, func=AF.Ln, scale=scale, bias=epst[:, 0:1]),
                  reads=[in_t, epst], writes=[out_t])
            cx.op("act", lambda: A.activation(out=out_ap_, in_=out_ap_, func=AF.Exp, scale=-0.5),
                  reads=[out_t], writes=[out_t])

        for ti in range(NT):
            ht = hts[ti % 2]
            for (hd_, ha_, ho_, hw_) in hn_wins(ti):
                cx.dma("sp", ht, ht[:, :, ho_:ho_ + hw_], hd_, ha_)
            ost = outb[ti % 2]
            for j in range(4):
                n = ti * 4 + j
                ts = slice(j * 128, (j + 1) * 128)
                a_, b_ = pA[n % 2], pB[n % 2]
                for k in range(8):
                    cx.op("pe", lambda: PE.matmul(a_[:], ht[:, k, ts], wb[:, k, 0:512], start=(k == 0), stop=(k == 7)),
                          reads=[ht, wb], writes=[a_], acc=(k > 0))
                for k in range(8):
                    cx.op("pe", lambda: PE.matmul(b_[:, 0:256], ht[:, k, ts], wb[:, k, 512:768], start=(k == 0), stop=(k == 7)),
                          reads=[ht, wb], writes=[b_], acc=(k > 0))
                for k in range(8):
                    cx.op("pe", lambda: PE.matmul(pga[:], wb[:, k, 768:784], ht[:, k, ts], start=(k == 0), stop=(k == 7)),
                          reads=[ht, wb], writes=[pga], acc=(k > 0))
                cx.op("act", lambda: A.copy(out=gaT[0:16, :], in_=pga[:]), reads=[pga], writes=[gaT])
                cx.op("pe", lambda: PE.matmul(py[:], gaT[0:17, :], wal_b[:, :], start=True, stop=True), reads=[gaT, wal_b], writes=[py])
                cx.op("act", lambda: A.activation(out=ee[:], in_=py[:], func=AF.Exp, scale=-1.0), reads=[py], writes=[ee])
                cx.op("act", lambda: A.activation(out=sp[:], in_=ee[:], func=AF.Ln, bias=1.0), reads=[ee], writes=[sp])
                cx.op("pool", lambda: P.tensor_copy(out=sph[:], in_=sp[:]), reads=[sp], writes=[sph])
                cx.op("dve", lambda: V.tensor_tensor(spl[:], sp[:], sph[:], ALU.subtract), reads=[sp, sph], writes=[spl])
                cx.op("pe", lambda: PE.matmul(pcum[:], tri[:, 128:256], sph[:], start=True, stop=False), reads=[tri, sph], writes=[pcum])
                cx.op("pe", lambda: PE.matmul(pcum[:], tri[:, 128:256], spl[:], start=False, stop=True), reads=[tri, spl], writes=[pcum], acc=True)
                cx.op("pe", lambda: PE.matmul(ptot[:], blk[:, 256:384], sph[:], start=True, stop=False), reads=[blk, sph], writes=[ptot])
                cx.op("pe", lambda: PE.matmul(ptot[:], blk[:, 256:384], spl[:], start=False, stop=True), reads=[blk, spl], writes=[ptot], acc=True)
                for c in range(2):
                    cs = slice(c * 64, (c + 1) * 64)
                    cx.op("pe", lambda: PE.matmul(pdec[:, c:c + 1], sph[cs, :], ones[cs, 0:1], start=True, stop=False),
                          reads=[sph, ones], writes=[pdec])
                    cx.op("pe", lambda: PE.matmul(pdec[:, c:c + 1], spl[cs, :], ones[cs, 0:1], start=False, stop=True),
                          reads=[spl, ones], writes=[pdec], acc=True)
                cx.op("act", lambda: A.copy(out=csum[:], in_=pcum[:]), reads=[pcum], writes=[csum])
                cx.op("dve", lambda: V.tensor_tensor(dif[:], csum[:], ptot[:], ALU.subtract), reads=[csum, ptot], writes=[dif])
                cx.op("act", lambda: A.activation(out=eq[:], in_=csum[:], func=AF.Exp, scale=-1.0 / 16), reads=[csum], writes=[eq])
                cx.op("act", lambda: A.activation(out=ek[:], in_=csum[:], func=AF.Exp, scale=1.0 / 16), reads=[csum], writes=[ek])
                cx.op("act", lambda: A.activation(out=ekd[:], in_=dif[:], func=AF.Exp, scale=1.0 / 16), reads=[dif], writes=[ekd])
                cx.op("act", lambda: A.activation(out=dec[:], in_=pdec[:], func=AF.Exp, scale=-1.0 / 16), reads=[pdec], writes=[dec])
                cx.op("dve", lambda: V.scalar_tensor_tensor(qt[:], a_[:, 0:128], 128.0 ** -0.5, eq[:], ALU.mult, ALU.mult),
                      reads=[a_, eq], writes=[qt])
                cx.op("dve", lambda: V.tensor_tensor(kt[:], a_[:, 128:256], ek[:], ALU.mult), reads=[a_, ek], writes=[kt])
                cx.op("dve", lambda: V.tensor_tensor(kd[:], a_[:, 128:256], ekd[:], ALU.mult), reads=[a_, ekd], writes=[kd])
                cx.op("act", lambda: A.copy(out=vb[:], in_=a_[:, 256:512]), reads=[a_], writes=[vb])
                cx.op("act", lambda: A.activation(out=sg[:], in_=b_[:, 0:256], func=AF.Silu), reads=[b_], writes=[sg])
                cx.op("pe", lambda: PE.transpose(ptr[0][:], qt[:], ident[:, 0:128]), reads=[qt, ident], writes=[ptr[0]])
                cx.op("pe", lambda: PE.transpose(ptr[1][:], kt[:], ident[:, 0:128]), reads=[kt, ident], writes=[ptr[1]])
                cx.op("act", lambda: A.copy(out=qTf[:], in_=ptr[0][:]), reads=[ptr[0]], writes=[qTf])
                cx.op("act", lambda: A.copy(out=kTf[:], in_=ptr[1][:]), reads=[ptr[1]], writes=[kTf])
                cx.op("pool", lambda: P.tensor_copy(out=q0T[:, 0:64], in_=qTf[:, 0:64]), reads=[qTf], writes=[q0T])
                cx.op("pool", lambda: P.tensor_copy(out=q1T[:, 64:128], in_=qTf[:, 64:128]), reads=[qTf], writes=[q1T])
                cx.op("pe", lambda: PE.matmul(paT[:], kTf[:], qTf[:], start=True, stop=True), reads=[kTf, qTf], writes=[paT])
                cx.op("dve", lambda: V.tensor_tensor(ATb[:], paT[:], trif[:], ALU.mult), reads=[paT, trif], writes=[ATb])
                for c in range(2):
                    cs = slice(c * 64, (c + 1) * 64)
                    if c == 0:
                        cx.op("pe", lambda: PE.matmul(po[:], ATb[:], vb[:], start=True, stop=False), reads=[ATb, vb], writes=[po])
                        cx.op("pe", lambda: PE.matmul(po[:], q0T[:], Sb[0][:], start=False, stop=False), reads=[q0T, Sb[0]], writes=[po], acc=True)
                    cx.op("pe", lambda: PE.matmul(pk[:], kd[cs, :], vb[cs, :], start=True, stop=True), reads=[kd, vb], writes=[pk])
                    cx.op("dve", lambda: V.scalar_tensor_tensor(St[:], St[:], dec[:, c:c + 1], pk[:], ALU.mult, ALU.add),
                          reads=[St, dec, pk], writes=[St])
                    nb = Sb[1 - c]
                    cx.op("act", lambda: A.copy(out=nb[:], in_=St[:]), reads=[St], writes=[nb])
                    if c == 0:
                        cx.op("pe", lambda: PE.matmul(po[:], q1T[:], Sb[1][:], start=False, stop=True), reads=[q1T, Sb[1]], writes=[po], acc=True)
                cx.op("act", lambda: A.activation(out=junk[:], in_=po[:], func=AF.Square, accum_out=os1[:, 0:1]), reads=[po], writes=[junk, os1])
                lnexp_rstd(rr, rr[:], os1, os1[:], 1.0 / 256)
                cx.op("dve", lambda: V.scalar_tensor_tensor(y1[:], po[:], rr[:, 0:1], gain[:], ALU.mult, ALU.mult), reads=[po, rr, gain], writes=[y1])
                cx.op("dve", lambda: V.tensor_tensor(y2[:], y1[:], sg[:], ALU.mult), reads=[y1, sg], writes=[y2])
                for ec in range(2):
                    cx.op("pe", lambda: PE.transpose(ptr[2 + ec][:], y2[:, ec * 128:(ec + 1) * 128], ident[:, 0:128]), reads=[y2, ident], writes=[ptr[2 + ec]])
                    cx.op("act", lambda: A.copy(out=ost[:, ec, ts], in_=ptr[2 + ec][:]), reads=[ptr[2 + ec]], writes=[ost])
            od_, oa_ = out_win(ti)
            cx.dma("sp", od_, oa_.rearrange("(c p) t -> p c t", p=128), ost, ost[:])


def _gla_tables():
    j = np.arange(128)
    same = (j[:, None] // 64) == (j[None, :] // 64)
    ident = (j[:, None] == j[None, :])
    tri = same & (j[:, None] <= j[None, :])
    return np.concatenate([ident, tri, same], axis=1).astype(np.float32).astype(NPBF)


def _gla_maps(hnT_b, S, w_in, w_alpha, b_alpha, out_gain):
    cm = _gla_tables()
    maps = []
    for c in range(NCORES):
        b, g = c // 4, c % 4
        cols = np.concatenate([g * 128 + np.arange(128), 512 + g * 128 + np.arange(128),
                               1024 + g * 256 + np.arange(256), 2048 + g * 256 + np.arange(256), 3072 + np.arange(16)])
        wsel = np.ascontiguousarray(w_in[:, cols])
        wal = np.ascontiguousarray(np.concatenate([w_alpha[:, g * 128:(g + 1) * 128], b_alpha[None, g * 128:(g + 1) * 128]], axis=0).astype(np.float32))
        m = dict(wsel=wsel, walpha=wal, gain=np.ascontiguousarray(out_gain[None, :].astype(np.float32)), cmat=cm)
        if hnT_b[b] is not None:
            m["hnT"] = hnT_b[b]
        maps.append(m)
    return maps


_CACHE = {}


def _get(name, fn):
    if name not in _CACHE:
        _CACHE[name] = fn()
    return _CACHE[name]


def _tail_maps(hT_b, catT_b, own, w_out, g_ffn, g_next, w_up, conv_w, conv_b, w_down):
    cw = np.ascontiguousarray(np.asarray(conv_w, np.float32).reshape(3, NFF, 128).transpose(2, 1, 0).reshape(128, NFF * 3))
    cb = np.ascontiguousarray(np.asarray(conv_b, np.float32).reshape(NFF, 128).T)
    w_up = np.ascontiguousarray(np.asarray(w_up, np.float32).reshape(8, 128, 2, NFF, 128).transpose(3, 1, 0, 2, 4).reshape(NFF * 128, 2048))
    maps = []
    for c in range(NCORES):
        b, gi = c // 4, c % 4
        lo, hi = gi * own - 2, (gi + 1) * own

        def halo_slice(src, dt):
            if src is None:
                return None
            if lo < 0:
                return np.ascontiguousarray(np.concatenate([np.zeros((D, 2), dt), src[b][:, 0:hi]], axis=1))
            return np.ascontiguousarray(src[b][:, lo:hi])
        m = dict(w_out=w_out, w_up=w_up, w_down=w_down, g_ffn=_v8(g_ffn), g_next=_v8(g_next), conv_w=cw, conv_b=cb)
        hin, cat = halo_slice(hT_b, np.float32), halo_slice(catT_b, NPBF)
        if hin is not None:
            m["hin"] = hin
        if cat is not None:
            m["cat"] = cat
        maps.append(m)
    return maps


def kernel_unfused(x, mix_norm_g, even_w_in, sb_q_gain, sb_k_gain, ret_out_gain, even_w_out,
           odd_w_in, gla_w_alpha, gla_b_alpha, gla_out_gain, odd_w_out,
           ffn_norm_g, ffn_w_up, ffn_conv_w, ffn_conv_b, ffn_w_down):
    f = lambda a: np.ascontiguousarray(np.asarray(a, dtype=np.float32))
    x = f(x)
    B, S, _ = x.shape
    own = S // 4
    cores = list(range(NCORES))
    xT = [np.ascontiguousarray(x[b].T) for b in range(B)]

    nc1 = _get(("mix0", S), lambda: build_mix0(S))
    maps1 = _mix0_maps(xT, S, f(mix_norm_g)[0], f(even_w_in)[0], f(sb_q_gain)[0], f(sb_k_gain)[0], f(ret_out_gain)[0],
                       _const_tables(S))
    r1 = run_bass_kernel_spmd(nc1, maps1, core_ids=cores).results
    catT = [np.zeros((D, S), NPBF) for _ in range(B)]
    for c in cores:
        b, g = c // 4, c % 4
        o = np.asarray(r1[c]["outT"])
        catT[b][g * 128:(g + 1) * 128] = o[0:128]
        catT[b][512 + g * 128:512 + (g + 1) * 128] = o[128:256]

    nc2 = _get(("tail", own), lambda: build_tail(own, True))
    maps2 = _tail_maps(xT, catT, own, f(even_w_out)[0], f(ffn_norm_g)[0], f(mix_norm_g)[1], f(ffn_w_up)[0],
                       f(ffn_conv_w)[0], f(ffn_conv_b)[0], f(ffn_w_down)[0])
    r2 = run_bass_kernel_spmd(nc2, maps2, core_ids=cores).results
    h1T = [np.concatenate([np.asarray(r2[b * 4 + gi]["hout"]) for gi in range(4)], axis=1) for b in range(B)]
    hnT = [np.ascontiguousarray(np.concatenate([np.asarray(r2[b * 4 + gi]["hn"]) for gi in range(4)], axis=1)) for b in range(B)]

    nc3 = _get(("gla", S), lambda: build_gla(S))
    maps3 = _gla_maps(hnT, S, f(odd_w_in)[0], f(gla_w_alpha)[0], f(gla_b_alpha)[0], f(gla_out_gain)[0])
    r3 = run_bass_kernel_spmd(nc3, maps3, core_ids=cores).results
    cat2T = [np.zeros((D, S), NPBF) for _ in range(B)]
    for c in cores:
        b, g = c // 4, c % 4
        cat2T[b][g * 256:(g + 1) * 256] = np.asarray(r3[c]["outT"])

    maps4 = _tail_maps(h1T, cat2T, own, f(odd_w_out)[0], f(ffn_norm_g)[1], f(mix_norm_g)[1], f(ffn_w_up)[1],
                       f(ffn_conv_w)[1], f(ffn_conv_b)[1], f(ffn_w_down)[1])
    r4 = run_bass_kernel_spmd(nc2, maps4, core_ids=cores).results
    out = np.empty((B, S, D), np.float32)
    for b in range(B):
        oT = np.concatenate([np.asarray(r4[b * 4 + gi]["hout"]) for gi in range(4)], axis=1)
        out[b] = oT.T
    return out


def build_fused(S):
    own = S // 4
    nc = bass.Bass("TRN2", target_bir_lowering=False)
    groups = [[0, 1, 2, 3], [4, 5, 6, 7]]
    W1 = min(S, 1024)
    W2 = min(own, 256)
    with contextlib.ExitStack() as es:
        cx = Ctx(nc, es)
        m0 = _mix0_drams(cx, S, "m0_")
        t0 = _tail_drams(cx, "t0_")
        gl = _gla_drams(cx, "gl_")
        t1 = _tail_drams(cx, "t1_")
        xown = mk_dram(cx, "xown", [D, own + 2], F32, "ExternalInput")
        sel = mk_dram(cx, "sel", [128, 4], F32, "ExternalInput")
        outT = mk_dram(cx, "outT", [D, own], F32, "ExternalOutput")
        o1 = Chunked(cx, "i_o1", 256, S, W1, BF16)
        G1 = Chunked(cx, "i_G1", 1024, S, W1, BF16)
        h1d = mk_dram(cx, "i_h1d", [D, own], F32, "Internal")
        hnd = Chunked(cx, "i_hnd", D, own, W2, BF16)
        G2 = Chunked(cx, "i_G2", 4 * D, own, W2, BF16)
        hl = mk_dram(cx, "i_hl", [D, 2], F32, "Internal")
        H2 = mk_dram(cx, "i_H2", [4 * D, 2], F32, "Internal")
        o3 = Chunked(cx, "i_o3", 256, S, W1, BF16)
        G3 = Chunked(cx, "i_G3", 1024, S, W1, BF16)

        cx.begin_phase("m0_")
        m0["out_win"] = lambda ti: o1.win(ti * 512, 512)
        emit_mix0(cx, S, m0)
        cx.end_phase()
        for k in range(o1.n):
            cx.coll_allgather(o1.ch[k], G1.ch[k], groups, key="cc1")

        cx.begin_phase("t0_")
        t0.update(hin=xown, cat_win=G1.win, sel=sel, hout=h1d, hn=hnd.ch[0], hlast=hl,
                  hn_outs=lambda ti: hnd.wins(ti * 512, 512))
        emit_tail(cx, own, t0, True, fused=True)
        cx.end_phase()
        for k in range(hnd.n):
            cx.coll_allgather(hnd.ch[k], G2.ch[k], groups, key="cc2")
        cx.coll_allgather(hl, H2, groups, key="cc2")

        def hn_wins(ti):
            r = ti // (own // 512)
            c0 = (ti % (own // 512)) * 512
            res = []
            for (d_, a_, o_, w_) in G2.wins(c0, 512):
                res.append((d_, a_.rearrange("(r c p) t -> r p c t", r=4, p=128)[r], o_, w_))
            return res

        cx.begin_phase("gl_")
        gl.update(hn_wins=hn_wins, out_win=lambda ti: o3.win(ti * 512, 512))
        emit_gla(cx, S, gl, gathered=True)
        cx.end_phase()
        for k in range(o3.n):
            cx.coll_allgather(o3.ch[k], G3.ch[k], groups, key="cc3")

        cx.begin_phase("t1_")
        t1.update(hin=h1d, hin_halo=H2, cat_win=G3.win, sel=sel, hout=outT, hn=None)
        emit_tail(cx, own, t1, False, fused=True)
        cx.end_phase()
        cx.finish()
        print("fused: sems", len(cx.sems), "dmas", cx.ndma, "ops", getattr(cx, "nops", 0))
    return nc


def _fused_maps(x, mix_norm_g, even_w_in, sb_q_gain, sb_k_gain, ret_out_gain, even_w_out,
                odd_w_in, gla_w_alpha, gla_b_alpha, gla_out_gain, odd_w_out,
                ffn_norm_g, ffn_w_up, ffn_conv_w, ffn_conv_b, ffn_w_down):
    B, S, _ = x.shape
    own = S // 4
    xT = [np.ascontiguousarray(x[b].T) for b in range(B)]
    m0 = _mix0_maps(xT, S, mix_norm_g[0], even_w_in[0], sb_q_gain[0], sb_k_gain[0], ret_out_gain[0], _const_tables(S))
    perm = np.concatenate([np.concatenate([g * 128 + np.arange(128), 512 + g * 128 + np.arange(128)]) for g in range(4)])
    w_out0 = np.ascontiguousarray(even_w_out[0][perm])
    dummy_cat = [None] * B
    tl0 = _tail_maps(xT, None, own, w_out0, ffn_norm_g[0], mix_norm_g[1], ffn_w_up[0], ffn_conv_w[0], ffn_conv_b[0], ffn_w_down[0])
    tl1 = _tail_maps(None, None, own, np.ascontiguousarray(odd_w_out[0]), ffn_norm_g[1], mix_norm_g[1], ffn_w_up[1], ffn_conv_w[1],
                     ffn_conv_b[1], ffn_w_down[1])
    gm = _gla_maps([None] * B, S, odd_w_in[0], gla_w_alpha[0], gla_b_alpha[0], gla_out_gain[0])
    maps = []
    for c in range(NCORES):
        gi = c % 4
        m = {}
        for k, v in m0[c].items():
            m["m0_" + k] = v
        for k, v in tl0[c].items():
            if k == "hin":
                m["xown"] = v
            elif k != "cat":
                m["t0_" + k] = v
        for k, v in tl1[c].items():
            if k not in ("hin", "cat"):
                m["t1_" + k] = v
        for k, v in gm[c].items():
            if k != "hnT":
                m["gl_" + k] = v
        selv = np.zeros((128, 4), np.float32)
        selv[:, gi] = 1.0
        m["sel"] = selv
        maps.append(m)
    return maps


def kernel(x, mix_norm_g, even_w_in, sb_q_gain, sb_k_gain, ret_out_gain, even_w_out,
           odd_w_in, gla_w_alpha, gla_b_alpha, gla_out_gain, odd_w_out,
           ffn_norm_g, ffn_w_up, ffn_conv_w, ffn_conv_b, ffn_w_down):
    f = lambda a: np.ascontiguousarray(np.asarray(a, dtype=np.float32))
    args = [f(a) for a in (x, mix_norm_g, even_w_in, sb_q_gain, sb_k_gain, ret_out_gain, even_w_out,
                           odd_w_in, gla_w_alpha, gla_b_alpha, gla_out_gain, odd_w_out,
                           ffn_norm_g, ffn_w_up, ffn_conv_w, ffn_conv_b, ffn_w_down)]
    B, S, _ = args[0].shape
    nc = _get(("fused", S), lambda: build_fused(S))
    maps = _fused_maps(*args)
    res = run_bass_kernel_spmd(nc, maps, core_ids=list(range(NCORES))).results
    out = np.empty((B, S, D), np.float32)
    for b in range(B):
        oT = np.concatenate([np.asarray(res[b * 4 + gi]["outT"]) for gi in range(4)], axis=1)
        out[b] = oT.T
    return out
```

```python
import contextlib
import numpy as np
import ml_dtypes
import concourse.bass as bass
import concourse.mybir as mybir
from concourse.bass_utils import run_bass_kernel_spmd

F32 = mybir.dt.float32
BF16 = mybir.dt.bfloat16
AF = mybir.ActivationFunctionType
ALU = mybir.AluOpType
AX = mybir.AxisListType
NPBF = ml_dtypes.bfloat16

D = 1024
DFF = 2816
NFF = DFF // 128
EPS = 1e-6
NCORES = 8


class T:
    def __init__(self, t, name, sl=None, st=None):
        self.t = t if sl is None else t[sl]
        self.name = name
        self.st = st
        self.w = None
        self.r = {}
        self.dsem = None
        self.wd = {}

    def __getitem__(self, k):
        return self.t[k]


class Ctx:
    def __init__(self, nc, es):
        self.nc = nc
        self.es = es
        self.sems = {}
        self.issued = {}
        self.isdma = set()
        self.eng = {"pe": nc.tensor, "act": nc.scalar, "dve": nc.vector, "pool": nc.gpsimd, "sp": nc.sync}
        self.waited = {e: {} for e in self.eng}
        for e in self.eng:
            self._mksem("e_" + e)
        self.ndma = 0
        self.pfx = ""
        self.es_phase = es
        self.ncc = 0

    def begin_phase(self, pfx):
        self.pfx = pfx
        self.es_phase = contextlib.ExitStack()
        self.es_phase.__enter__()

    def end_phase(self):
        self.barrier()
        self.es_phase.__exit__(None, None, None)
        self.es_phase = self.es

    def coll_allgather(self, in_t, out_t, groups, key="cc"):
        if key not in self.sems:
            self._mksem(key, dma=True)
        self._wait("pool", [in_t], [out_t])
        ins = self.nc.gpsimd.collective_compute("AllGather", ALU.bypass, replica_groups=groups,
                                                ins=[in_t.t.ap()], outs=[out_t.t.ap()])
        ins.then_inc(self.sems[key], 1)
        self.issued[key] += 1
        in_t.r[key] = self.issued[key]
        out_t.w = (key, self.issued[key])
        out_t.r = {}
        return ins

    def _mksem(self, key, dma=False):
        self.sems[key] = self.es.enter_context(self.nc.semaphore(key))
        self.issued[key] = 0
        if dma:
            self.isdma.add(key)
        return key

    def sb(self, name, shape, dt):
        name = self.pfx + name
        return T(self.es_phase.enter_context(self.nc.sbuf_tensor(name, list(shape), dt)), name)

    def ps(self, name, shape, dt=F32):
        name = self.pfx + name
        t = T(self.es_phase.enter_context(self.nc.psum_tensor(name, list(shape), dt)), name)
        t.excl = True
        return t

    def dram(self, name, shape, dt, kind):
        return T(self.nc.dram_tensor(name, list(shape), dt, kind=kind), name)

    def _wait(self, e, reads, writes, skip_self_waw=False):
        deps = {}

        def add(tok):
            if tok is None:
                return
            k, v = tok
            if deps.get(k, 0) < v:
                deps[k] = v

        reads = [t.st or t for t in reads]
        writes = [t.st or t for t in writes]
        for t in reads:
            add(t.w)
            for k_, v_ in t.wd.items():
                add((k_, v_))
            if getattr(t, "excl", False):
                for k, v in t.r.items():
                    if k != "e_" + e:
                        add((k, v))
        for t in writes:
            if not (skip_self_waw and t.w is not None and t.w[0] == "e_" + e):
                add(t.w)
            for k, v in t.r.items():
                add((k, v))
        w = self.waited[e]
        for k, v in deps.items():
            if k in self.isdma:
                v = self.issued[k]
            if w.get(k, 0) < v:
                self.eng[e].wait_ge(self.sems[k], v)
                w[k] = v

    def op(self, e, fn, reads=(), writes=(), acc=False):
        if getattr(self, "mute", False):
            return None
        self.nops = getattr(self, "nops", 0) + 1
        if self.nops > getattr(self, "oplimit", 10 ** 9):
            return None
        if getattr(self, "insec", False):
            self.seccount = getattr(self, "seccount", 0) + 1
            if self.seccount > getattr(self, "seclimit", 10 ** 9):
                return None
        self._wait(e, reads, writes, skip_self_waw=acc)
        ins = fn()
        k = "e_" + e
        self.issued[k] += 1
        ins.then_inc(self.sems[k], 1)
        tok = (k, self.issued[k])
        for t in reads:
            t = t.st or t
            if t.r.get(k, 0) < tok[1]:
                t.r[k] = tok[1]
        for t in writes:
            t = t.st or t
            t.w = tok
            t.r = {}
        return ins

    def dma(self, q, out_t, out_ap, in_t, in_ap, sbt=None):
        if sbt is None:
            sbt = out_t if not isinstance(out_t, DT) else in_t
        if sbt.dsem is None:
            sbt.dsem = self._mksem("d_%s" % sbt.name, dma=True)
        to_dram = isinstance(out_t, DT)
        if to_dram:
            self._wait(q, [in_t], [])
            w_ = self.waited[q]
            for k_, v_ in list(out_t.r.items()):
                if k_ in self.isdma:
                    v_ = self.issued[k_]
                if w_.get(k_, 0) < v_:
                    self.eng[q].wait_ge(self.sems[k_], v_)
                    w_[k_] = v_
        else:
            self._wait(q, [in_t], [out_t])
        ins = self.eng[q].dma_start(out=out_ap, in_=in_ap)
        k = sbt.dsem
        self.issued[k] += 16
        ins.then_inc(self.sems[k], 16)
        tok = (k, self.issued[k])
        (in_t.st or in_t).r[k] = tok[1]
        if to_dram:
            out_t.wd[k] = tok[1]
            out_t.w = tok
        else:
            (out_t.st or out_t).w = tok
            (out_t.st or out_t).r = {}
        self.ndma += 1
        return ins

    def view(self, base, name, sl):
        return T(base.t, name, sl)

    def barrier(self):
        for e in self.eng:
            for k, v in self.issued.items():
                if v > 0 and k != "e_" + e and self.waited[e].get(k, 0) < v:
                    self.eng[e].wait_ge(self.sems[k], v)
                    self.waited[e][k] = v

    def finish(self):
        for k, v in self.issued.items():
            if v > 0 and self.waited["sp"].get(k, 0) < v:
                self.nc.sync.wait_ge(self.sems[k], v)
                self.waited["sp"][k] = v


class DT(T):
    pass


class Chunked:
    def __init__(self, cx, name, R, total, W, dt):
        self.W = W
        self.n = total // W
        self.ch = [mk_dram(cx, "%s_%d" % (name, k), [R, W], dt, "Internal") for k in range(self.n)]

    def win(self, t0, width):
        k = t0 // self.W
        c0 = t0 - k * self.W
        assert c0 + width <= self.W
        return self.ch[k], self.ch[k].t.ap()[:, c0:c0 + width]

    def wins(self, t0, width):
        out = []
        t = t0
        while t < t0 + width:
            k = t // self.W
            c0 = t - k * self.W
            w = min(self.W - c0, t0 + width - t)
            out.append((self.ch[k], self.ch[k].t.ap()[:, c0:c0 + w], t - t0, w))
            t += w
        return out


def mk_dram(cx, name, shape, dt, kind):
    return DT(cx.nc.dram_tensor(name, list(shape), dt, kind=kind), name)


def build_tail(own, want_hn):
    nc = bass.Bass("TRN2", target_bir_lowering=False)
    with contextlib.ExitStack() as es:
        cx = Ctx(nc, es)
        dr = _tail_drams(cx, "")
        dr["hin"] = mk_dram(cx, "hin", [D, own + 2], F32, "ExternalInput")
        dr["cat"] = mk_dram(cx, "cat", [D, own + 2], BF16, "ExternalInput")
        dr["hout"] = mk_dram(cx, "hout", [D, own], F32, "ExternalOutput")
        dr["hn"] = mk_dram(cx, "hn", [D, own], BF16, "ExternalOutput")
        cx.begin_phase("")
        emit_tail(cx, own, dr, want_hn, fused=False)
        cx.end_phase()
        cx.finish()
    return nc


def _tail_drams(cx, p):
    return dict(
        w_out=mk_dram(cx, p + "w_out", [D, D], F32, "ExternalInput"),
        w_up=mk_dram(cx, p + "w_up", [NFF * 128, 2048], F32, "ExternalInput"),
        w_down=mk_dram(cx, p + "w_down", [DFF, D], F32, "ExternalInput"),
        g_ffn=mk_dram(cx, p + "g_ffn", [128, 8], F32, "ExternalInput"),
        g_next=mk_dram(cx, p + "g_next", [128, 8], F32, "ExternalInput"),
        conv_w=mk_dram(cx, p + "conv_w", [128, NFF * 3], F32, "ExternalInput"),
        conv_b=mk_dram(cx, p + "conv_b", [128, NFF], F32, "ExternalInput"))


def emit_tail(cx, own, dr, want_hn, fused):
    nc = cx.nc
    NT = own // 512
    if True:
        hin, cat, w_out, w_up, w_down = dr["hin"], dr.get("cat"), dr["w_out"], dr["w_up"], dr["w_down"]
        gff_d, gnx_d, cw_d, cb_d, hout, hnout = dr["g_ffn"], dr["g_next"], dr["conv_w"], dr["conv_b"], dr["hout"], dr.get("hn")
        hin_halo = dr.get("hin_halo")
        hlast = dr.get("hlast")
        hin_ap = hin.t.ap().rearrange("(c p) t -> p c t", p=128)
        cat_ap = cat.t.ap().rearrange("(c p) t -> p c t", p=128) if cat is not None else None
        cat_win = dr.get("cat_win")
        hn_outs = dr.get("hn_outs")
        hout_ap = hout.t.ap().rearrange("(c p) t -> p c t", p=128)
        hn_ap = hnout.t.ap().rearrange("(c p) t -> p c t", p=128) if hnout is not None else None
        wout_ap = w_out.t.ap().rearrange("(k p) n -> p k n", p=128)
        wup_ap = w_up.t.ap().rearrange("(c p) f -> c p f", p=128)
        wdown_ap = w_down.t.ap().rearrange("(k p) n -> p k n", p=128)
        hoff = 2 if hin_halo is None else 0

        wout_b = cx.sb("wout_b", [128, 8, 1024], BF16)
        wdown_b = cx.sb("wdown_b", [128, NFF, 1024], BF16)
        stg = [cx.sb("stg%d" % i, [128, 8, 256], F32) for i in range(2)]
        wup_b = [cx.sb("wup_b%d" % i, [128, 8, 256], BF16) for i in range(3)]
        h1s = [cx.sb("h1_%d" % i, [128, 8, 512], F32) for i in range(2)]
        catb = cx.sb("catb", [128, 8, 512], BF16)
        hb = cx.sb("hb", [128, 8, 512], BF16)
        sqs = [cx.sb("sq%d" % i, [128, 512], BF16) for i in range(2)]
        rstd = cx.sb("rstd", [128, 512], F32)
        rstd2 = cx.sb("rstd2", [128, 512], F32)
        Us = [cx.sb("U%d" % i, [128, 514], F32) for i in range(2)]
        t1s = [cx.sb("t1_%d" % i, [128, 512], F32) for i in range(2)]
        t2s = [cx.sb("t2_%d" % i, [128, 512], F32) for i in range(2)]
        sgs = [cx.sb("sg%d" % i, [128, 512], F32) for i in range(2)]
        actb = cx.sb("actb", [128, NFF, 512], BF16)
        tmpo = [cx.sb("tmpo%d" % i, [128, 512], F32) for i in range(2)]
        hnb = hb
        carry = cx.sb("carry", [128, NFF, 2], F32)
        h1h = cx.sb("h1h", [128, 8, 2], F32)
        cath = cx.sb("cath", [128, 8, 2], BF16)
        hbh = cx.sb("hbh", [128, 8, 2], BF16)
        rstdh = cx.sb("rstdh", [128, 2], F32)
        gff = cx.sb("gff", [128, 8], F32)
        gnx = cx.sb("gnx", [128, 8], F32)
        cw = cx.sb("cw", [128, NFF * 3], F32)
        cb = cx.sb("cb", [128, NFF], F32)
        ones = cx.sb("ones", [128, 128], BF16)

        pms = [cx.ps("pm%d" % i, [128, 512]) for i in range(2)]
        pssq = cx.ps("pssq", [128, 512])
        pus = [cx.ps("pu%d" % i, [128, 512]) for i in range(2)]
        pvs = [cx.ps("pv%d" % i, [128, 512]) for i in range(2)]
        puh = cx.ps("puh", [128, 512])

        V, A, P, PE = nc.vector, nc.scalar, nc.gpsimd, nc.tensor

        cx.op("dve", lambda: V.memset(ones[:], 1.0), writes=[ones])
        epst = cx.sb("epst", [128, 1], F32)
        cx.op("dve", lambda: V.memset(epst[:], EPS), writes=[epst])
        for dst, src in ((gff, gff_d), (gnx, gnx_d), (cw, cw_d), (cb, cb_d)):
            cx.dma("sp", dst, dst[:], src, src.t.ap())

        def load_wup(gidx):
            c = gidx % NFF
            s = stg[gidx % 2]
            cx.dma("sp", s, s[:].rearrange("p a b -> p (a b)"), w_up, wup_ap[c])
            wb = wup_b[gidx % 3]
            cx.op("pool", lambda: P.tensor_copy(out=wb[:], in_=s[:]), reads=[s], writes=[wb])

        for k in range(8):
            s = stg[k % 2]
            sv = s[:].rearrange("p a b -> p (a b)")
            cx.dma("sp", s, sv[:, 0:1024], w_out, wout_ap[:, k, :])
            cx.op("act", lambda: A.copy(out=wout_b[:, k, :], in_=sv[:, 0:1024]), reads=[s], writes=[wout_b])
        for c in range(NFF):
            s = stg[c % 2]
            sv = s[:].rearrange("p a b -> p (a b)")
            cx.dma("sp", s, sv[:, 0:1024], w_down, wdown_ap[:, c, :])
            cx.op("act", lambda: A.copy(out=wdown_b[:, c, :], in_=sv[:, 0:1024]), reads=[s], writes=[wdown_b])

        def wout_stage(h1, catt, n):
            for nch in range(8):
                pm = pms[nch % 2]
                for k in range(8):
                    cx.op("pe", lambda: PE.matmul(pm[:, 0:n], wout_b[:, k, nch * 128:(nch + 1) * 128], catt[:, k, 0:n],
                                                  start=(k == 0), stop=(k == 7)),
                          reads=[wout_b, catt], writes=[pm], acc=(k > 0))
                cx.op("dve", lambda: V.tensor_tensor(h1[:, nch, 0:n], h1[:, nch, 0:n], pm[:, 0:n], ALU.add),
                      reads=[pm, h1], writes=[h1])

        def norm_stage(h1, n, g, hbt, rs):
            for k in range(8):
                sq = sqs[k % 2]
                cx.op("act", lambda: A.activation(out=sq[:, 0:n], in_=h1[:, k, 0:n], func=AF.Square),
                      reads=[h1], writes=[sq])
                cx.op("pe", lambda: PE.matmul(pssq[:, 0:n], ones[:, :], sq[:, 0:n], start=(k == 0), stop=(k == 7)),
                      reads=[sq, ones], writes=[pssq], acc=(k > 0))
                if hbt is not None:
                    cx.op("dve", lambda: V.tensor_scalar(hbt[:, k, 0:n], h1[:, k, 0:n], g[:, k:k + 1], None, ALU.mult),
                          reads=[h1, g], writes=[hbt])
            cx.op("act", lambda: A.activation(out=rs[:, 0:n], in_=pssq[:, 0:n], func=AF.Ln, scale=1.0 / D, bias=epst[:, 0:1]),
                  reads=[pssq, epst], writes=[rs])
            cx.op("act", lambda: A.activation(out=rs[:, 0:n], in_=rs[:, 0:n], func=AF.Exp, scale=-0.5),
                  reads=[rs], writes=[rs])

        V, A = nc.vector, nc.scalar
        if fused:
            selt = cx.sb("selt", [128, 4], F32)
            cx.dma("sp", selt, selt[:], dr["sel"], dr["sel"].t.ap())
            cands = [cx.sb("cand%d" % i, [128, 8, 512], BF16) for i in range(2)]
            candh = [cx.sb("candh%d" % i, [128, 8, 2], BF16) for i in range(2)]
            candf = [cx.sb("candf%d" % i, [128, 8, 2], F32) for i in range(2)]

        def blend(dst, srcs, width, bufs, src_t):
            first = True
            for n_, (j, (src_t, ap_)) in enumerate(srcs):
                cb_ = bufs[n_ % 2]
                cx.dma("sp", cb_, cb_[:, :, 0:width], src_t, ap_)
                if first:
                    cx.op("act", lambda: A.activation(out=dst[:, :, 0:width], in_=cb_[:, :, 0:width], func=AF.Copy, scale=selt[:, j:j + 1]),
                          reads=[cb_, selt], writes=[dst])
                    first = False
                else:
                    cx.op("dve", lambda: V.scalar_tensor_tensor(dst[:, :, 0:width], cb_[:, :, 0:width], selt[:, j:j + 1], dst[:, :, 0:width],
                                                                ALU.mult, ALU.add),
                          reads=[cb_, selt, dst], writes=[dst])

        def catw(t0, width):
            d_, a_ = cat_win(t0, width)
            return d_, a_.rearrange("(c p) t -> p c t", p=128)

        def load_tile(ti, h1):
            cx.dma("sp", h1, h1[:], hin, hin_ap[:, :, hoff + ti * 512:hoff + (ti + 1) * 512])
            if not fused:
                cx.dma("sp", catb, catb[:], cat, cat_ap[:, :, 2 + ti * 512:2 + (ti + 1) * 512])
            else:
                blend(catb, [(j, catw(j * own + ti * 512, 512)) for j in range(4)], 512, cands, None)

        if not fused:
            cx.dma("sp", h1h, h1h[:], hin, hin_ap[:, :, 0:2])
            cx.dma("sp", cath, cath[:], cat, cat_ap[:, :, 0:2])
        else:
            blend(cath, [(j, catw(j * own - 2, 2)) for j in range(1, 4)], 2, candh, None)
            if hin_halo is None:
                cx.dma("sp", h1h, h1h[:], hin, hin_ap[:, :, 0:2])
            else:
                hh_ap = hin_halo.t.ap().rearrange("(r c p) t -> r p c t", r=4, p=128)
                blend(h1h, [(r + 1, (hin_halo, hh_ap[r])) for r in range(3)], 2, candf, None)
        wout_stage(h1h, cath, 2)
        norm_stage(h1h, 2, gff, hbh, rstdh)

        load_wup(0)
        load_wup(1)
        for ti in range(NT):
            h1 = h1s[ti % 2]
            load_tile(ti, h1)
            wout_stage(h1, catb, 512)
            norm_stage(h1, 512, gff, hb, rstd)
            for c in range(NFF):
                gidx = ti * NFF + c
                if c == 4 and ti > 0 and "after_hn" in dr:
                    dr["after_hn"](ti - 1)
                if gidx + 2 < NT * NFF:
                    load_wup(gidx + 2)
                wb = wup_b[gidx % 3]
                pu, pv, U = pus[c % 2], pvs[c % 2], Us[c % 2]
                t1, t2, sg = t1s[c % 2], t2s[c % 2], sgs[c % 2]
                for k in range(8):
                    cx.op("pe", lambda: PE.matmul(pu[:], wb[:, k, 0:128], hb[:, k, :], start=(k == 0), stop=(k == 7)),
                          reads=[wb, hb], writes=[pu], acc=(k > 0))
                if ti == 0:
                    for k in range(8):
                        cx.op("pe", lambda: PE.matmul(puh[:, 0:2], wb[:, k, 0:128], hbh[:, k, 0:2], start=(k == 0), stop=(k == 7)),
                              reads=[wb, hbh], writes=[puh], acc=(k > 0))
                for k in range(8):
                    cx.op("pe", lambda: PE.matmul(pv[:], wb[:, k, 128:256], hb[:, k, :], start=(k == 0), stop=(k == 7)),
                          reads=[wb, hb], writes=[pv], acc=(k > 0))
                if ti == 0:
                    cx.op("dve", lambda: V.tensor_tensor(U[:, 0:2], puh[:, 0:2], rstdh[:, 0:2], ALU.mult),
                          reads=[puh, rstdh], writes=[U])
                else:
                    cx.op("act", lambda: A.copy(out=U[:, 0:2], in_=carry[:, c, :]), reads=[carry], writes=[U])
                cx.op("dve", lambda: V.tensor_tensor(U[:, 2:514], pu[:], rstd[:], ALU.mult),
                      reads=[pu, rstd, U], writes=[U])
                cx.op("act", lambda: A.copy(out=carry[:, c, :], in_=U[:, 512:514]), reads=[U], writes=[carry])
                cx.op("act", lambda: A.activation(out=t1[:], in_=U[:, 0:512], func=AF.Identity,
                                                  scale=cw[:, 3 * c:3 * c + 1], bias=cb[:, c:c + 1]),
                      reads=[U, cw, cb], writes=[t1])
                cx.op("dve", lambda: V.scalar_tensor_tensor(t2[:], U[:, 1:513], cw[:, 3 * c + 1:3 * c + 2], t1[:], ALU.mult, ALU.add),
                      reads=[U, cw, t1], writes=[t2])
                cx.op("dve", lambda: V.scalar_tensor_tensor(t1[:], U[:, 2:514], cw[:, 3 * c + 2:3 * c + 3], t2[:], ALU.mult, ALU.add),
                      reads=[U, cw, t2], writes=[t1])
                cx.op("act", lambda: A.activation(out=sg[:], in_=t1[:], func=AF.Silu), reads=[t1], writes=[sg])
                cx.op("dve", lambda: V.tensor_tensor(actb[:, c, :], sg[:], pv[:], ALU.mult),
                      reads=[sg, pv], writes=[actb])
            for nch in range(8):
                pm = pms[nch % 2]
                tm = tmpo[nch % 2]
                for c in range(NFF):
                    cx.op("pe", lambda: PE.matmul(pm[:], wdown_b[:, c, nch * 128:(nch + 1) * 128], actb[:, c, :],
                                                  start=(c == 0), stop=(c == NFF - 1)),
                          reads=[wdown_b, actb], writes=[pm], acc=(c > 0))
                cx.op("dve", lambda: V.tensor_tensor(tm[:], pm[:], rstd[:], ALU.mult), reads=[pm, rstd], writes=[tm])
                cx.op("dve", lambda: V.tensor_tensor(h1[:, nch, :], h1[:, nch, :], tm[:], ALU.add),
                      reads=[tm, h1], writes=[h1])
            if want_hn:
                norm_stage(h1, 512, gnx, None, rstd2)
                for k in range(8):
                    cx.op("dve", lambda: V.scalar_tensor_tensor(hnb[:, k, :], h1[:, k, :], gnx[:, k:k + 1], rstd2[:], ALU.mult, ALU.mult),
                          reads=[h1, gnx, rstd2], writes=[hnb])
                if hn_outs is None:
                    cx.dma("sp", hnout, hn_ap[:, :, ti * 512:(ti + 1) * 512], hnb, hnb[:])
                else:
                    for (hd_, ha_, ho_, hw_) in hn_outs(ti):
                        cx.dma("sp", hd_, ha_.rearrange("(c p) t -> p c t", p=128), hnb, hnb[:, :, ho_:ho_ + hw_])
            cx.dma("sp", hout, hout_ap[:, :, ti * 512:(ti + 1) * 512], h1, h1[:])
            if hlast is not None and ti == NT - 1:
                cx.dma("sp", hlast, hlast.t.ap().rearrange("(c p) t -> p c t", p=128), h1, h1[:, :, 510:512])
            if ti == NT - 1 and "after_hn" in dr:
                dr["after_hn"](ti)


def build_mix0(S, do_b=True, do_ret=True, do_sbn=True, seclimit=10 ** 9):
    nc = bass.Bass("TRN2", target_bir_lowering=False)
    with contextlib.ExitStack() as es:
        cx = Ctx(nc, es)
        cx.seclimit = seclimit
        dr = _mix0_drams(cx, S, "")
        dr["outT"] = mk_dram(cx, "outT", [256, S], BF16, "ExternalOutput")
        cx.begin_phase("")
        emit_mix0(cx, S, dr, do_b, do_ret, do_sbn)
        cx.end_phase()
        cx.finish()
    return nc


def _mix0_drams(cx, S, p):
    return dict(
        xT=mk_dram(cx, p + "xT", [(S // 512) * 128, 4096], F32, "ExternalInput"),
        wsel=mk_dram(cx, p + "wsel", [D, 896], F32, "ExternalInput"),
        g_mix=mk_dram(cx, p + "g_mix", [128, 8], F32, "ExternalInput"),
        gains=mk_dram(cx, p + "gains", [1, 384], F32, "ExternalInput"),
        rope=mk_dram(cx, p + "rope", [S, 256], F32, "ExternalInput"),
        rconst=mk_dram(cx, p + "rconst", [128, 131], F32, "ExternalInput"),
        cmat=mk_dram(cx, p + "cmat", [128, 384], BF16, "ExternalInput"),
        dmask=mk_dram(cx, p + "dmask", [128, 2048], F32, "ExternalInput"))


def emit_mix0(cx, S, dr, do_b=True, do_ret=True, do_sbn=True):
    nc = cx.nc
    NCH = S // 128
    NT = S // 512
    if True:
        xT, wsel, gmix_d, gains_d, rope_d, rconst_d, cmat_d, dmask_d, outT = (
            dr["xT"], dr["wsel"], dr["g_mix"], dr["gains"], dr["rope"], dr["rconst"], dr["cmat"], dr["dmask"], dr.get("outT"))

        x_ap = xT.t.ap().rearrange("(n p) f -> n p f", p=128)
        w_ap = wsel.t.ap().rearrange("(k p) n -> p k n", p=128)
        rope_ap = rope_d.t.ap().rearrange("(n p) c -> p n c", p=128)
        if "out_win" in dr:
            out_win = dr["out_win"]
        else:
            out_win = lambda ti: (outT, outT.t.ap()[:, ti * 512:(ti + 1) * 512])

        V, A, P, PE = nc.vector, nc.scalar, nc.gpsimd, nc.tensor

        qT = cx.sb("qT", [128, S], BF16)
        kT = cx.sb("kT", [128, S], BF16)
        vres = cx.sb("vres", [128, NCH, 128], BF16)
        wb = cx.sb("wb", [128, 8, 896], BF16)
        gmix = cx.sb("gmix", [128, 8], F32)
        gains = cx.sb("gains_s", [128, 384], F32)
        rconst = cx.sb("rconst_s", [128, 131], F32)
        cmat = cx.sb("cmat_s", [128, 384], BF16)
        dmask = cx.sb("dmask_s", [128, 2048], F32)
        ones = cx.sb("ones", [128, 128], BF16)
        epst = cx.sb("epst", [128, 1], F32)
        cx.op("dve", lambda: V.memset(ones[:], 1.0), writes=[ones])
        cx.op("dve", lambda: V.memset(epst[:], EPS), writes=[epst])
        cx.dma("sp", gmix, gmix[:], gmix_d, gmix_d.t.ap())
        cx.dma("sp", gains, gains[:], gains_d, gains_d.t.ap()[0:1, :].broadcast_to([128, 384]))
        cx.dma("sp", rconst, rconst[:], rconst_d, rconst_d.t.ap())
        cx.dma("sp", cmat, cmat[:], cmat_d, cmat_d.t.ap())
        cx.dma("sp", dmask, dmask[:], dmask_d, dmask_d.t.ap())
        cx.op("dve", lambda: V.tensor_scalar(gains[:, 0:128], gains[:, 0:128], 0.125, None, ALU.mult),
              reads=[gains], writes=[gains])
        ident = cmat
        decT = rconst

        with contextlib.ExitStack() as esA:
            cxa_sb = lambda name, shape, dt: T(esA.enter_context(nc.sbuf_tensor(cx.pfx + name, list(shape), dt)), cx.pfx + name)
            def cxa_ps(name, shape, dt=F32):
                t = T(esA.enter_context(nc.psum_tensor(cx.pfx + name, list(shape), dt)), cx.pfx + name)
                t.excl = True
                return t
            stg = [cxa_sb("stg%d" % i, [128, 896], F32) for i in range(2)]
            xts = [cxa_sb("xt%d" % i, [128, 8, 512], F32) for i in range(2)]
            ropes = [cxa_sb("rope%d" % i, [128, 4, 256], F32) for i in range(2)]
            hb = cxa_sb("hb", [128, 8, 512], BF16)
            sq = cxa_sb("sq", [128, 8, 512], BF16)
            rstd = cxa_sb("rstd", [128, 4], F32)
            Rs = [cxa_sb("R%d" % i, [128, 512], F32) for i in range(2)]
            SBs = [cxa_sb("SB%d" % i, [128, 384], F32) for i in range(2)]
            sqt = cxa_sb("sqt", [128, 256], F32)
            qs4 = cxa_sb("qs4", [128, 4], F32)
            rq4 = cxa_sb("rq4", [128, 4], F32)
            qkn = cxa_sb("qkn", [128, 256], BF16)
            ra = [cxa_sb("ra%d" % i, [128, 2, 64], F32) for i in range(4)]
            qkr = cxa_sb("qkr", [128, 2, 128], BF16)
            qTr = cxa_sb("qTr", [128, 128], BF16)
            kTr = cxa_sb("kTr", [128, 128], BF16)
            kz = cxa_sb("kz", [128, 128], BF16)
            vb = cxa_sb("vb", [128, 128], BF16)
            ATb = cxa_sb("ATb", [128, 128], BF16)
            o1 = cxa_sb("o1", [128, 128], F32)
            o2 = cxa_sb("o2", [128, 128], F32)
            junk = cxa_sb("junk", [128, 128], F32)
            os1 = cxa_sb("os1", [128, 1], F32)
            rr = cxa_sb("rr", [128, 1], F32)
            sgt = cxa_sb("sgt", [128, 128], F32)
            ob = cxa_sb("ob", [128, 128], BF16)
            Rst = cxa_sb("Rst", [128, 128], F32)
            Rstb = cxa_sb("Rstb", [128, 128], BF16)
            outb = [cxa_sb("outb%d" % i, [128, 512], BF16) for i in range(2)]

            pp0 = [cxa_ps("pp0_%d" % i, [128, 512]) for i in range(2)]
            pp1 = [cxa_ps("pp1_%d" % i, [128, 512]) for i in range(2)]
            pssq = cxa_ps("pssq", [128, 4])
            ptr_t = cxa_ps("ptr", [128, 8, 128], BF16)
            ptr = [T(ptr_t.t, "ptr%d" % i, (slice(None), i, slice(None)), st=ptr_t) for i in range(8)]
            pmA_t = cxa_ps("pmA", [128, 4, 128])
            pmA = [T(pmA_t.t, "pmA%d" % i, (slice(None), i, slice(None)), st=pmA_t) for i in range(4)]

            for k in range(8):
                s = stg[k % 2]
                cx.dma("sp", s, s[:], wsel, w_ap[:, k, :])
                cx.op("act", lambda: A.copy(out=wb[:, k, :], in_=s[:]), reads=[s], writes=[wb])
            cx.op("dve", lambda: V.memset(Rst[:], 0.0), writes=[Rst])
            cx.op("dve", lambda: V.memset(Rstb[:], 0.0), writes=[Rstb])

            def lnexp_rstd(out_t, out_ap_, in_t, in_ap_, scale):
                cx.op("act", lambda: A.activation(out=out_ap_, in_=in_ap_, func=AF.Ln, scale=scale, bias=epst[:, 0:1]),
                      reads=[in_t, epst], writes=[out_t])
                cx.op("act", lambda: A.activation(out=out_ap_, in_=out_ap_, func=AF.Exp, scale=-0.5),
                      reads=[out_t], writes=[out_t])

            for ti in range(NT):
                xt = xts[ti % 2]
                rp = ropes[ti % 2]
                cx.dma("sp", xt, xt[:].rearrange("p a b -> p (a b)"), xT, x_ap[ti])
                cx.dma("sp", rp, rp[:], rope_d, rope_ap[:, ti * 4:(ti + 1) * 4, :])
                for k in range(8):
                    cx.op("act", lambda: A.activation(out=sq[:, k, :], in_=xt[:, k, :], func=AF.Square), reads=[xt], writes=[sq])
                    cx.op("dve", lambda: V.tensor_scalar(hb[:, k, :], xt[:, k, :], gmix[:, k:k + 1], None, ALU.mult),
                          reads=[xt, gmix], writes=[hb])
                for j in range(4):
                    for k in range(8):
                        cx.op("pe", lambda: PE.matmul(pssq[:, j:j + 1], sq[:, k, j * 128:(j + 1) * 128], ones[:, 0:1],
                                                      start=(k == 0), stop=(k == 7)),
                              reads=[sq, ones], writes=[pssq], acc=(k > 0 or j > 0))
                lnexp_rstd(rstd, rstd[:], pssq, pssq[:], 1.0 / D)
                ob_stage = outb[ti % 2]
                for j in range(4):
                    n = ti * 4 + j
                    p0, p1 = pp0[n % 2], pp1[n % 2]
                    R, SBt = Rs[n % 2], SBs[n % 2]
                    for k in range(8):
                        cx.op("pe", lambda: PE.matmul(p0[:], hb[:, k, j * 128:(j + 1) * 128], wb[:, k, 0:512],
                                                      start=(k == 0), stop=(k == 7)),
                              reads=[hb, wb], writes=[p0], acc=(k > 0))
                    for k in range(8):
                        cx.op("pe", lambda: PE.matmul(p1[:, 0:384], hb[:, k, j * 128:(j + 1) * 128], wb[:, k, 512:896],
                                                      start=(k == 0), stop=(k == 7)),
                              reads=[hb, wb], writes=[p1], acc=(k > 0))
                    cx.op("act", lambda: A.activation(out=R[:], in_=p0[:], func=AF.Copy, scale=rstd[:, j:j + 1]),
                          reads=[p0, rstd], writes=[R])
                    cx.op("act", lambda: A.activation(out=SBt[:], in_=p1[:, 0:384], func=AF.Copy, scale=rstd[:, j:j + 1]),
                          reads=[p1, rstd], writes=[SBt])
                    cx.mute = not do_sbn
                    cx.op("dve", lambda: V.tensor_tensor(sqt[:], SBt[:, 0:256], SBt[:, 0:256], ALU.mult), reads=[SBt], writes=[sqt])
                    cx.op("dve", lambda: V.tensor_reduce(qs4[:], sqt[:].rearrange("p (a d) -> p a d", d=64), AX.X, ALU.add),
                          reads=[sqt], writes=[qs4])
                    lnexp_rstd(rq4, rq4[:], qs4, qs4[:], 1.0 / 64)
                    for h4 in range(4):
                        sl = slice(h4 * 64, (h4 + 1) * 64)
                        cx.op("dve", lambda: V.scalar_tensor_tensor(qkn[:, sl], SBt[:, sl], rq4[:, h4:h4 + 1], gains[:, sl], ALU.mult, ALU.mult),
                              reads=[SBt, rq4, gains], writes=[qkn])
                    cx.op("pe", lambda: PE.transpose(ptr[0][:], qkn[:, 0:128], ident[:, 0:128]), reads=[qkn, ident], writes=[ptr[0]])
                    cx.op("pe", lambda: PE.transpose(ptr[1][:], qkn[:, 128:256], ident[:, 0:128]), reads=[qkn, ident], writes=[ptr[1]])
                    cx.op("act", lambda: A.copy(out=qT[:, n * 128:(n + 1) * 128], in_=ptr[0][:]), reads=[ptr[0]], writes=[qT])
                    cx.op("act", lambda: A.copy(out=kT[:, n * 128:(n + 1) * 128], in_=ptr[1][:]), reads=[ptr[1]], writes=[kT])
                    cx.op("pool", lambda: P.tensor_copy(out=vres[:, n, :], in_=SBt[:, 256:384]), reads=[SBt], writes=[vres])
                    cx.mute = not do_ret
                    cx.insec = True
                    Rv = R[:, 0:256].rearrange("p (a d) -> p a d", a=2)
                    CC = rp[:, j, 0:128].rearrange("p (a d) -> p a d", a=2)
                    SS = rp[:, j, 128:256].rearrange("p (a d) -> p a d", a=2)
                    cx.op("dve", lambda: V.tensor_tensor(ra[0][:], Rv[:, :, 0:64], CC, ALU.mult), reads=[R, rp], writes=[ra[0]])
                    cx.op("dve", lambda: V.tensor_tensor(ra[1][:], Rv[:, :, 64:128], SS, ALU.mult), reads=[R, rp], writes=[ra[1]])
                    cx.op("dve", lambda: V.tensor_tensor(ra[2][:], Rv[:, :, 0:64], SS, ALU.mult), reads=[R, rp], writes=[ra[2]])
                    cx.op("dve", lambda: V.tensor_tensor(ra[3][:], Rv[:, :, 64:128], CC, ALU.mult), reads=[R, rp], writes=[ra[3]])
                    cx.op("dve", lambda: V.tensor_tensor(qkr[:, :, 0:64], ra[0][:], ra[1][:], ALU.subtract), reads=[ra[0], ra[1]], writes=[qkr])
                    cx.op("dve", lambda: V.tensor_tensor(qkr[:, :, 64:128], ra[2][:], ra[3][:], ALU.add), reads=[ra[2], ra[3]], writes=[qkr])
                    cx.op("pe", lambda: PE.transpose(ptr[2][:], qkr[:, 0, :], ident[:, 0:128]), reads=[qkr, ident], writes=[ptr[2]])
                    cx.op("pe", lambda: PE.transpose(ptr[3][:], qkr[:, 1, :], ident[:, 0:128]), reads=[qkr, ident], writes=[ptr[3]])
                    cx.op("act", lambda: A.copy(out=qTr[:], in_=ptr[2][:]), reads=[ptr[2]], writes=[qTr])
                    cx.op("act", lambda: A.copy(out=kTr[:], in_=ptr[3][:]), reads=[ptr[3]], writes=[kTr])
                    cx.op("dve", lambda: V.tensor_scalar(kz[:], qkr[:, 1, :], rconst[:, 128:129], None, ALU.mult), reads=[qkr, rconst], writes=[kz])
                    cx.op("pool", lambda: P.tensor_copy(out=vb[:], in_=R[:, 256:384]), reads=[R], writes=[vb])
                    psc, po, pc, pk = pmA
                    cx.op("pe", lambda: PE.matmul(psc[:], kTr[:], qTr[:], start=True, stop=True), reads=[kTr, qTr], writes=[psc])
                    cx.op("dve", lambda: V.tensor_tensor(ATb[:], psc[:], decT[:, 0:128], ALU.mult), reads=[psc, decT], writes=[ATb])
                    cx.op("pe", lambda: PE.matmul(pc[:], qTr[:], Rstb[:], start=True, stop=True), reads=[qTr, Rstb], writes=[pc])
                    cx.op("pe", lambda: PE.matmul(po[:], ATb[:], vb[:], start=True, stop=True), reads=[ATb, vb], writes=[po])
                    cx.op("pe", lambda: PE.matmul(pk[:], kz[:], vb[:], start=True, stop=True), reads=[kz, vb], writes=[pk])
                    cx.op("act", lambda: A.activation(out=o1[:], in_=pc[:], func=AF.Copy, scale=rconst[:, 129:130]),
                          reads=[pc, rconst], writes=[o1])
                    cx.op("dve", lambda: V.tensor_tensor(o2[:], o1[:], po[:], ALU.add), reads=[o1, po], writes=[o2])
                    cx.op("dve", lambda: V.scalar_tensor_tensor(Rst[:], Rst[:], rconst[:, 130:131], pk[:], ALU.mult, ALU.add),
                          reads=[Rst, rconst, pk], writes=[Rst])
                    cx.op("act", lambda: A.copy(out=Rstb[:], in_=Rst[:]), reads=[Rst], writes=[Rstb])
                    cx.op("act", lambda: A.activation(out=junk[:], in_=o2[:], func=AF.Square, accum_out=os1[:, 0:1]),
                          reads=[o2], writes=[junk, os1])
                    lnexp_rstd(rr, rr[:], os1, os1[:], 1.0 / 128)
                    cx.op("act", lambda: A.activation(out=sgt[:], in_=R[:, 384:512], func=AF.Silu), reads=[R], writes=[sgt])
                    cx.op("dve", lambda: V.scalar_tensor_tensor(o1[:], o2[:], rr[:, 0:1], gains[:, 256:384], ALU.mult, ALU.mult),
                          reads=[o2, rr, gains], writes=[o1])
                    cx.op("dve", lambda: V.tensor_tensor(ob[:], o1[:], sgt[:], ALU.mult), reads=[o1, sgt], writes=[ob])
                    cx.op("pe", lambda: PE.transpose(ptr[4][:], ob[:], ident[:, 0:128]), reads=[ob, ident], writes=[ptr[4]])
                    cx.op("act", lambda: A.copy(out=ob_stage[:, j * 128:(j + 1) * 128], in_=ptr[4][:]), reads=[ptr[4]], writes=[ob_stage])
                cx.mute = False
                cx.insec = False
                od_, oa_ = out_win(ti)
                cx.dma("sp", od_, oa_[128:256, :], ob_stage, ob_stage[:])
            cx.barrier()

        with contextlib.ExitStack() as esB:
          if do_b:
                cxb_sb = lambda name, shape, dt: T(esB.enter_context(nc.sbuf_tensor(cx.pfx + name, list(shape), dt)), cx.pfx + name)
                def cxb_ps(name, shape, dt=F32):
                    t = T(esB.enter_context(nc.psum_tensor(cx.pfx + name, list(shape), dt)), cx.pfx + name)
                    t.excl = True
                    return t
                Es = [[cxb_sb("E%d_%d" % (h, i), [128, 512], F32) for i in range(2)] for h in range(2)]
                SPs = [[cxb_sb("SP%d_%d" % (h, i), [128, 512], BF16) for i in range(3)] for h in range(2)]
                Xs = [[cxb_sb("X%d_%d" % (h, i), [128, 512], F32) for i in range(2)] for h in range(2)]
                Ws = [[cxb_sb("W%d_%d" % (h, i), [128, 512], BF16) for i in range(2)] for h in range(2)]
                oa = [[cxb_sb("oa%d_%d" % (h, i), [64, 512], BF16) for i in range(2)] for h in range(2)]
                pz = [[cxb_ps("pz%d_%d" % (h, i), [128, 512]) for i in range(2)] for h in range(2)]
                pcs = [cxb_ps("pcs%d" % h, [128, 512]) for h in range(2)]
                pout = [cxb_ps("pout%d" % h, [64, 512]) for h in range(2)]
                Umat = cmat
                for qs in range(NT):
                    kbs = list(range(4 * qs + 3, -1, -1))
                    nst = len(kbs)

                    def zmm(i):
                        kb = kbs[i]
                        for h in range(2):
                            hs = slice(h * 64, (h + 1) * 64)
                            p = pz[h][i % 2]
                            cx.op("pe", lambda: PE.matmul(p[:], kT[hs, kb * 128:(kb + 1) * 128], qT[hs, qs * 512:(qs + 1) * 512],
                                                          start=True, stop=True), reads=[kT, qT], writes=[p])
                    zmm(0)
                    for i in range(nst):
                        kb = kbs[i]
                        if i == 3 and qs > 0 and "after_out" in dr:
                            dr["after_out"](qs - 1)
                        if i + 1 < nst:
                            zmm(i + 1)
                        for h in range(2):
                            E, SP, p = Es[h][i % 2], SPs[h][i % 3], pz[h][i % 2]
                            cx.op("act", lambda: A.activation(out=E[:], in_=p[:], func=AF.Exp), reads=[p], writes=[E])
                            if kb >= 4 * qs:
                                dm = kb - 4 * qs
                                cx.op("pool", lambda: P.tensor_tensor(E[:], E[:], dmask[:, dm * 512:(dm + 1) * 512], ALU.mult),
                                      reads=[E, dmask], writes=[E])
                            cx.op("act", lambda: A.activation(out=SP[:], in_=E[:], func=AF.Ln, bias=1.0), reads=[E], writes=[SP])
                        for h in range(2):
                            SP = SPs[h][i % 3]
                            cx.op("pe", lambda: PE.matmul(pcs[h][:], Umat[:, 128:256], SP[:], start=(i == 0), stop=False, skip_group_check=True),
                                  reads=[Umat, SP], writes=[pcs[h]], acc=(i > 0))
                            if i > 0:
                                SPp = SPs[h][(i - 1) % 3]
                                cx.op("pe", lambda: PE.matmul(pcs[h][:], Umat[:, 256:384], SPp[:], start=False, stop=False, skip_group_check=True),
                                      reads=[Umat, SPp], writes=[pcs[h]], acc=True)
                        for h in range(2):
                            X = Xs[h][i % 2]
                            cx.op("act", lambda: A.activation(out=X[:], in_=pcs[h][:], func=AF.Exp, scale=-1.0), reads=[pcs[h]], writes=[X])
                        for h in range(2):
                            X, W, E = Xs[h][i % 2], Ws[h][i % 2], Es[h][i % 2]
                            cx.op("dve", lambda: V.tensor_tensor(W[:], E[:], X[:], ALU.mult), reads=[E, X], writes=[W])
                        for h in range(2):
                            W = Ws[h][i % 2]
                            cx.op("pe", lambda: PE.matmul(pout[h][:], vres[:, kb, h * 64:(h + 1) * 64], W[:], start=(i == 0), stop=(i == nst - 1)),
                                  reads=[vres, W], writes=[pout[h]], acc=(i > 0))
                    for h in range(2):
                        o = oa[h][qs % 2]
                        cx.op("dve", lambda: V.tensor_copy(out=o[:], in_=pout[h][:]), reads=[pout[h]], writes=[o])
                        od_, oa_ = out_win(qs)
                        cx.dma("sp", od_, oa_[h * 64:(h + 1) * 64, :], o, o[:])
                    if qs == NT - 1 and "after_out" in dr:
                        dr["after_out"](qs)


def _v8(g):
    return np.ascontiguousarray(np.asarray(g, np.float32).reshape(8, 128).T)


def _const_tables(S):
    pos = np.arange(S, dtype=np.float32)
    inv = (np.float32(10000.0) ** (-(np.arange(64, dtype=np.float32) / np.float32(64)))).astype(np.float32)
    ang = (pos[:, None] * inv[None, :]).astype(np.float32).astype(np.float64)
    cos, sin = np.cos(ang).astype(np.float32), np.sin(ang).astype(np.float32)
    rope = np.ascontiguousarray(np.concatenate([cos, cos, sin, sin], axis=1))
    j = np.arange(128)
    ident = (j[:, None] == j[None, :])
    U = (j[:, None] >= j[None, :])
    L = (j[:, None] < j[None, :])
    cmat = np.concatenate([ident, U, L], axis=1).astype(np.float32).astype(NPBF)
    strict = (j[:, None] < j[None, :]).astype(np.float32)
    dmask = np.zeros((128, 4, 512), np.float32)
    for dm in range(4):
        for qb in range(4):
            if qb == dm:
                dmask[:, dm, qb * 128:(qb + 1) * 128] = strict
            elif qb > dm:
                dmask[:, dm, qb * 128:(qb + 1) * 128] = 1.0
    return rope, cmat, np.ascontiguousarray(dmask.reshape(128, 2048))


def _ret_consts(g):
    lg = np.log(np.float32(1.0) - np.exp2(np.float32(-5.0 - g))).astype(np.float64)
    j = np.arange(128, dtype=np.float64)
    sc = 128.0 ** -0.5
    diff = j[None, :] - j[:, None]
    decT = np.where(diff >= 0, np.exp(lg * np.maximum(diff, 0.0)), 0.0) * sc
    zeta = np.exp(lg * (127.0 - j)) * sc
    xi = np.exp(lg * (j + 1.0))
    gch = np.full(128, np.exp(lg * 128.0))
    return np.ascontiguousarray(np.concatenate([decT, zeta[:, None], xi[:, None], gch[:, None]], axis=1).astype(np.float32))


def _mix0_maps(xT_b, S, mix_g, w_in, q_gain, k_gain, ret_gain, tables):
    rope, cmat, dmask = tables
    xtl = [np.ascontiguousarray(xb.reshape(8, 128, S // 512, 512).transpose(2, 1, 0, 3).reshape((S // 512) * 128, 4096)) for xb in xT_b]
    maps = []
    for c in range(NCORES):
        b, g = c // 4, c % 4
        cols = np.concatenate([
            1536 + g * 128 + np.arange(128), 2048 + g * 128 + np.arange(128),
            2560 + g * 128 + np.arange(128), 3072 + g * 128 + np.arange(128),
            g * 128 + np.arange(128), 512 + g * 128 + np.arange(128), 1024 + g * 128 + np.arange(128)])
        wsel = np.ascontiguousarray(w_in[:, cols])
        gains = np.concatenate([q_gain, q_gain, k_gain, k_gain, ret_gain]).astype(np.float32)[None, :]
        maps.append(dict(xT=xtl[b], wsel=wsel, g_mix=_v8(mix_g), gains=np.ascontiguousarray(gains), rope=rope,
                         rconst=_ret_consts(g), cmat=cmat, dmask=dmask))
    return maps


def build_gla(S, oplimit=10 ** 9):
    nc = bass.Bass("TRN2", target_bir_lowering=False)
    with contextlib.ExitStack() as es:
        cx = Ctx(nc, es)
        cx.oplimit = oplimit
        dr = _gla_drams(cx, "")
        dr["hnT"] = mk_dram(cx, "hnT", [D, S], BF16, "ExternalInput")
        dr["outT"] = mk_dram(cx, "outT", [256, S], BF16, "ExternalOutput")
        cx.begin_phase("")
        emit_gla(cx, S, dr, gathered=False)
        cx.end_phase()
        cx.finish()
    return nc


def _gla_drams(cx, p):
    return dict(
        wsel=mk_dram(cx, p + "wsel", [D, 784], F32, "ExternalInput"),
        walpha=mk_dram(cx, p + "walpha", [17, 128], F32, "ExternalInput"),
        gain=mk_dram(cx, p + "gain", [1, 256], F32, "ExternalInput"),
        cmat=mk_dram(cx, p + "cmat", [128, 384], BF16, "ExternalInput"))


def emit_gla(cx, S, dr, gathered):
    nc = cx.nc
    NT = S // 512
    if True:
        hnT, wsel, walpha_d, gain_d, cmat_d, outT = dr.get("hnT"), dr["wsel"], dr["walpha"], dr["gain"], dr["cmat"], dr.get("outT")
        own = S // 4
        if gathered:
            hn_wins = dr["hn_wins"]
        else:
            h_ap = hnT.t.ap().rearrange("(c p) t -> p c t", p=128)
            hn_wins = lambda ti: [(hnT, h_ap[:, :, ti * 512:(ti + 1) * 512], 0, 512)]
        if "out_win" in dr:
            out_win = dr["out_win"]
        else:
            out_win = lambda ti: (outT, outT.t.ap()[:, ti * 512:(ti + 1) * 512])
        w_ap = wsel.t.ap().rearrange("(k p) n -> p k n", p=128)
        V, A, P, PE = nc.vector, nc.scalar, nc.gpsimd, nc.tensor

        wb = cx.sb("wb", [128, 8, 784], BF16)
        stg = [cx.sb("stg%d" % i, [128, 784], F32) for i in range(2)]
        wal_f = cx.sb("wal_f", [17, 128], F32)
        wal_b = cx.sb("wal_b", [17, 128], BF16)
        gain = cx.sb("gain_s", [128, 256], F32)
        cmat = cx.sb("cmat_s", [128, 384], BF16)
        trif = cx.sb("trif", [128, 128], F32)
        ones = cx.sb("ones", [128, 2], BF16)
        epst = cx.sb("epst", [128, 1], F32)
        hts = [cx.sb("ht%d" % i, [128, 8, 512], BF16) for i in range(2)]
        gaT = cx.sb("gaT", [32, 128], BF16)
        ee = cx.sb("ee", [128, 128], F32)
        sp = cx.sb("sp", [128, 128], F32)
        sph = cx.sb("sph", [128, 128], BF16)
        spl = cx.sb("spl", [128, 128], BF16)
        csum = cx.sb("csum", [128, 128], F32)
        dif = cx.sb("dif", [128, 128], F32)
        eq = cx.sb("eq", [128, 128], F32)
        ek = cx.sb("ek", [128, 128], F32)
        ekd = cx.sb("ekd", [128, 128], F32)
        qt = cx.sb("qt", [128, 128], BF16)
        kt = cx.sb("kt", [128, 128], BF16)
        kd = cx.sb("kd", [128, 128], BF16)
        vb = cx.sb("vb", [128, 256], BF16)
        sg = cx.sb("sg", [128, 256], F32)
        qTf = cx.sb("qTf", [128, 128], BF16)
        q0T = cx.sb("q0T", [128, 128], BF16)
        q1T = cx.sb("q1T", [128, 128], BF16)
        kTf = cx.sb("kTf", [128, 128], BF16)
        ATb = cx.sb("ATb", [128, 128], BF16)
        St = cx.sb("St", [128, 256], F32)
        Sb = [cx.sb("Sb%d" % i, [128, 256], BF16) for i in range(2)]
        dec = cx.sb("dec", [128, 2], F32)
        junk = cx.sb("junk", [128, 256], F32)
        os1 = cx.sb("os1", [128, 1], F32)
        rr = cx.sb("rr", [128, 1], F32)
        y1 = cx.sb("y1", [128, 256], F32)
        y2 = cx.sb("y2", [128, 256], BF16)
        outb = [cx.sb("outb%d" % i, [128, 2, 512], BF16) for i in range(2)]

        pA = [cx.ps("pA%d" % i, [128, 512]) for i in range(2)]
        pB_t = cx.ps("pBt", [128, 2, 256])
        pB = [T(pB_t.t, "pB%d" % i, (slice(None), i, slice(None)), st=pB_t) for i in range(2)]
        pm_t = cx.ps("pm", [128, 4, 128])
        py, pcum, ptot, paT = [T(pm_t.t, "pm%d" % i, (slice(None), i, slice(None)), st=pm_t) for i in range(4)]
        po = cx.ps("po", [128, 256])
        pk = cx.ps("pk", [128, 256])
        pm2_t = cx.ps("pm2", [128, 512])
        pga = T(pm2_t.t, "pga", (slice(0, 16), slice(0, 128)), st=pm2_t)
        pdec = T(pm2_t.t, "pdec", (slice(None), slice(128, 130)), st=pm2_t)
        ptr_t = cx.ps("ptr", [128, 8, 128], BF16)
        ptr = [T(ptr_t.t, "ptr%d" % i, (slice(None), i, slice(None)), st=ptr_t) for i in range(4)]

        cx.op("dve", lambda: V.memset(ones[:], 1.0), writes=[ones])
        cx.op("dve", lambda: V.memset(epst[:], EPS), writes=[epst])
        cx.op("dve", lambda: V.memset(gaT[:], 1.0), writes=[gaT])
        cx.op("dve", lambda: V.memset(q0T[:], 0.0), writes=[q0T])
        cx.op("dve", lambda: V.memset(q1T[:], 0.0), writes=[q1T])
        cx.op("dve", lambda: V.memset(St[:], 0.0), writes=[St])
        cx.op("dve", lambda: V.memset(Sb[0][:], 0.0), writes=[Sb[0]])
        cx.dma("sp", wal_f, wal_f[:], walpha_d, walpha_d.t.ap())
        cx.dma("sp", gain, gain[:], gain_d, gain_d.t.ap()[0:1, :].broadcast_to([128, 256]))
        cx.dma("sp", cmat, cmat[:], cmat_d, cmat_d.t.ap())
        cx.op("act", lambda: A.copy(out=wal_b[:], in_=wal_f[:]), reads=[wal_f], writes=[wal_b])
        cx.op("dve", lambda: V.tensor_copy(out=trif[:], in_=cmat[:, 128:256]), reads=[cmat], writes=[trif])
        for k in range(8):
            s = stg[k % 2]
            cx.dma("sp", s, s[:], wsel, w_ap[:, k, :])
            cx.op("act", lambda: A.copy(out=wb[:, k, :], in_=s[:]), reads=[s], writes=[wb])
        ident, tri, blk = cmat, cmat, cmat

        def lnexp_rstd(out_t, out_ap_, in_t, in_ap_, scale):
            cx.op("act", lambda: A.activation(out=out_ap_, in_=in_ap_, func=AF.Ln, scale=scale, bias=epst[:, 0:1]),
                  reads=[in_t, epst], writes=[out_t])
            cx.op("act", lambda: A.activation(out=out_ap_, in_=out_ap_, func=AF.Exp, scale=-0.5),
                  reads=[out_t], writes=[out_t])

        for ti in range(NT):
            ht = hts[ti % 2]
            for (hd_, ha_, ho_, hw_) in hn_wins(ti):
                cx.dma("sp", ht, ht[:, :, ho_:ho_ + hw_], hd_, ha_)
            ost = outb[ti % 2]
            for j in range(4):
                n = ti * 4 + j
                ts = slice(j * 128, (j + 1) * 128)
                a_, b_ = pA[n % 2], pB[n % 2]
                if j == 1 and ti > 0 and "after_out" in dr:
                    dr["after_out"](ti - 1)
                for k in range(8):
                    cx.op("pe", lambda: PE.matmul(a_[:], ht[:, k, ts], wb[:, k, 0:512], start=(k == 0), stop=(k == 7)),
                          reads=[ht, wb], writes=[a_], acc=(k > 0))
                for k in range(8):
                    cx.op("pe", lambda: PE.matmul(b_[:, 0:256], ht[:, k, ts], wb[:, k, 512:768], start=(k == 0), stop=(k == 7)),
                          reads=[ht, wb], writes=[b_], acc=(k > 0))
                for k in range(8):
                    cx.op("pe", lambda: PE.matmul(pga[:], wb[:, k, 768:784], ht[:, k, ts], start=(k == 0), stop=(k == 7)),
                          reads=[ht, wb], writes=[pga], acc=(k > 0))
                cx.op("act", lambda: A.copy(out=gaT[0:16, :], in_=pga[:]), reads=[pga], writes=[gaT])
                cx.op("pe", lambda: PE.matmul(py[:], gaT[0:17, :], wal_b[:, :], start=True, stop=True), reads=[gaT, wal_b], writes=[py])
                cx.op("act", lambda: A.activation(out=ee[:], in_=py[:], func=AF.Exp, scale=-1.0), reads=[py], writes=[ee])
                cx.op("act", lambda: A.activation(out=sp[:], in_=ee[:], func=AF.Ln, bias=1.0), reads=[ee], writes=[sp])
                cx.op("pool", lambda: P.tensor_copy(out=sph[:], in_=sp[:]), reads=[sp], writes=[sph])
                cx.op("dve", lambda: V.tensor_tensor(spl[:], sp[:], sph[:], ALU.subtract), reads=[sp, sph], writes=[spl])
                cx.op("pe", lambda: PE.matmul(pcum[:], tri[:, 128:256], sph[:], start=True, stop=False), reads=[tri, sph], writes=[pcum])
                cx.op("pe", lambda: PE.matmul(pcum[:], tri[:, 128:256], spl[:], start=False, stop=True), reads=[tri, spl], writes=[pcum], acc=True)
                cx.op("pe", lambda: PE.matmul(ptot[:], blk[:, 256:384], sph[:], start=True, stop=False), reads=[blk, sph], writes=[ptot])
                cx.op("pe", lambda: PE.matmul(ptot[:], blk[:, 256:384], spl[:], start=False, stop=True), reads=[blk, spl], writes=[ptot], acc=True)
                for c in range(2):
                    cs = slice(c * 64, (c + 1) * 64)
                    cx.op("pe", lambda: PE.matmul(pdec[:, c:c + 1], sph[cs, :], ones[cs, 0:1], start=True, stop=False),
                          reads=[sph, ones], writes=[pdec])
                    cx.op("pe", lambda: PE.matmul(pdec[:, c:c + 1], spl[cs, :], ones[cs, 0:1], start=False, stop=True),
                          reads=[spl, ones], writes=[pdec], acc=True)
                cx.op("act", lambda: A.copy(out=csum[:], in_=pcum[:]), reads=[pcum], writes=[csum])
                cx.op("dve", lambda: V.tensor_tensor(dif[:], csum[:], ptot[:], ALU.subtract), reads=[csum, ptot], writes=[dif])
                cx.op("act", lambda: A.activation(out=eq[:], in_=csum[:], func=AF.Exp, scale=-1.0 / 16), reads=[csum], writes=[eq])
                cx.op("act", lambda: A.activation(out=ek[:], in_=csum[:], func=AF.Exp, scale=1.0 / 16), reads=[csum], writes=[ek])
                cx.op("act", lambda: A.activation(out=ekd[:], in_=dif[:], func=AF.Exp, scale=1.0 / 16), reads=[dif], writes=[ekd])
                cx.op("act", lambda: A.activation(out=dec[:], in_=pdec[:], func=AF.Exp, scale=-1.0 / 16), reads=[pdec], writes=[dec])
                cx.op("dve", lambda: V.scalar_tensor_tensor(qt[:], a_[:, 0:128], 128.0 ** -0.5, eq[:], ALU.mult, ALU.mult),
                      reads=[a_, eq], writes=[qt])
                cx.op("dve", lambda: V.tensor_tensor(kt[:], a_[:, 128:256], ek[:], ALU.mult), reads=[a_, ek], writes=[kt])
                cx.op("dve", lambda: V.tensor_tensor(kd[:], a_[:, 128:256], ekd[:], ALU.mult), reads=[a_, ekd], writes=[kd])
                cx.op("act", lambda: A.copy(out=vb[:], in_=a_[:, 256:512]), reads=[a_], writes=[vb])
                cx.op("act", lambda: A.activation(out=sg[:], in_=b_[:, 0:256], func=AF.Silu), reads=[b_], writes=[sg])
                cx.op("pe", lambda: PE.transpose(ptr[0][:], qt[:], ident[:, 0:128]), reads=[qt, ident], writes=[ptr[0]])
                cx.op("pe", lambda: PE.transpose(ptr[1][:], kt[:], ident[:, 0:128]), reads=[kt, ident], writes=[ptr[1]])
                cx.op("act", lambda: A.copy(out=qTf[:], in_=ptr[0][:]), reads=[ptr[0]], writes=[qTf])
                cx.op("act", lambda: A.copy(out=kTf[:], in_=ptr[1][:]), reads=[ptr[1]], writes=[kTf])
                cx.op("pool", lambda: P.tensor_copy(out=q0T[:, 0:64], in_=qTf[:, 0:64]), reads=[qTf], writes=[q0T])
                cx.op("pool", lambda: P.tensor_copy(out=q1T[:, 64:128], in_=qTf[:, 64:128]), reads=[qTf], writes=[q1T])
                cx.op("pe", lambda: PE.matmul(paT[:], kTf[:], qTf[:], start=True, stop=True), reads=[kTf, qTf], writes=[paT])
                cx.op("dve", lambda: V.tensor_tensor(ATb[:], paT[:], trif[:], ALU.mult), reads=[paT, trif], writes=[ATb])
                for c in range(2):
                    cs = slice(c * 64, (c + 1) * 64)
                    if c == 0:
                        cx.op("pe", lambda: PE.matmul(po[:], ATb[:], vb[:], start=True, stop=False), reads=[ATb, vb], writes=[po])
                        cx.op("pe", lambda: PE.matmul(po[:], q0T[:], Sb[0][:], start=False, stop=False), reads=[q0T, Sb[0]], writes=[po], acc=True)
                    cx.op("pe", lambda: PE.matmul(pk[:], kd[cs, :], vb[cs, :], start=True, stop=True), reads=[kd, vb], writes=[pk])
                    cx.op("dve", lambda: V.scalar_tensor_tensor(St[:], St[:], dec[:, c:c + 1], pk[:], ALU.mult, ALU.add),
                          reads=[St, dec, pk], writes=[St])
                    nb = Sb[1 - c]
                    cx.op("act", lambda: A.copy(out=nb[:], in_=St[:]), reads=[St], writes=[nb])
                    if c == 0:
                        cx.op("pe", lambda: PE.matmul(po[:], q1T[:], Sb[1][:], start=False, stop=True), reads=[q1T, Sb[1]], writes=[po], acc=True)
                cx.op("act", lambda: A.activation(out=junk[:], in_=po[:], func=AF.Square, accum_out=os1[:, 0:1]), reads=[po], writes=[junk, os1])
                lnexp_rstd(rr, rr[:], os1, os1[:], 1.0 / 256)
                cx.op("dve", lambda: V.scalar_tensor_tensor(y1[:], po[:], rr[:, 0:1], gain[:], ALU.mult, ALU.mult), reads=[po, rr, gain], writes=[y1])
                cx.op("dve", lambda: V.tensor_tensor(y2[:], y1[:], sg[:], ALU.mult), reads=[y1, sg], writes=[y2])
                for ec in range(2):
                    cx.op("pe", lambda: PE.transpose(ptr[2 + ec][:], y2[:, ec * 128:(ec + 1) * 128], ident[:, 0:128]), reads=[y2, ident], writes=[ptr[2 + ec]])
                    cx.op("act", lambda: A.copy(out=ost[:, ec, ts], in_=ptr[2 + ec][:]), reads=[ptr[2 + ec]], writes=[ost])
            od_, oa_ = out_win(ti)
            cx.dma("sp", od_, oa_.rearrange("(c p) t -> p c t", p=128), ost, ost[:])
            if ti == NT - 1 and "after_out" in dr:
                dr["after_out"](ti)


def _gla_tables():
    j = np.arange(128)
    same = (j[:, None] // 64) == (j[None, :] // 64)
    ident = (j[:, None] == j[None, :])
    tri = same & (j[:, None] <= j[None, :])
    return np.concatenate([ident, tri, same], axis=1).astype(np.float32).astype(NPBF)


def _gla_maps(hnT_b, S, w_in, w_alpha, b_alpha, out_gain):
    cm = _gla_tables()
    maps = []
    for c in range(NCORES):
        b, g = c // 4, c % 4
        cols = np.concatenate([g * 128 + np.arange(128), 512 + g * 128 + np.arange(128),
                               1024 + g * 256 + np.arange(256), 2048 + g * 256 + np.arange(256), 3072 + np.arange(16)])
        wsel = np.ascontiguousarray(w_in[:, cols])
        wal = np.ascontiguousarray(np.concatenate([w_alpha[:, g * 128:(g + 1) * 128], b_alpha[None, g * 128:(g + 1) * 128]], axis=0).astype(np.float32))
        m = dict(wsel=wsel, walpha=wal, gain=np.ascontiguousarray(out_gain[None, :].astype(np.float32)), cmat=cm)
        if hnT_b[b] is not None:
            m["hnT"] = hnT_b[b]
        maps.append(m)
    return maps


_CACHE = {}


def _get(name, fn):
    if name not in _CACHE:
        _CACHE[name] = fn()
    return _CACHE[name]


def _tail_maps(hT_b, catT_b, own, w_out, g_ffn, g_next, w_up, conv_w, conv_b, w_down):
    cw = np.ascontiguousarray(np.asarray(conv_w, np.float32).reshape(3, NFF, 128).transpose(2, 1, 0).reshape(128, NFF * 3))
    cb = np.ascontiguousarray(np.asarray(conv_b, np.float32).reshape(NFF, 128).T)
    w_up = np.ascontiguousarray(np.asarray(w_up, np.float32).reshape(8, 128, 2, NFF, 128).transpose(3, 1, 0, 2, 4).reshape(NFF * 128, 2048))
    maps = []
    for c in range(NCORES):
        b, gi = c // 4, c % 4
        lo, hi = gi * own - 2, (gi + 1) * own

        def halo_slice(src, dt):
            if src is None:
                return None
            if lo < 0:
                return np.ascontiguousarray(np.concatenate([np.zeros((D, 2), dt), src[b][:, 0:hi]], axis=1))
            return np.ascontiguousarray(src[b][:, lo:hi])
        m = dict(w_out=w_out, w_up=w_up, w_down=w_down, g_ffn=_v8(g_ffn), g_next=_v8(g_next), conv_w=cw, conv_b=cb)
        hin, cat = halo_slice(hT_b, np.float32), halo_slice(catT_b, NPBF)
        if hin is not None:
            m["hin"] = hin
        if cat is not None:
            m["cat"] = cat
        maps.append(m)
    return maps


def kernel_unfused(x, mix_norm_g, even_w_in, sb_q_gain, sb_k_gain, ret_out_gain, even_w_out,
           odd_w_in, gla_w_alpha, gla_b_alpha, gla_out_gain, odd_w_out,
           ffn_norm_g, ffn_w_up, ffn_conv_w, ffn_conv_b, ffn_w_down):
    f = lambda a: np.ascontiguousarray(np.asarray(a, dtype=np.float32))
    x = f(x)
    B, S, _ = x.shape
    own = S // 4
    cores = list(range(NCORES))
    xT = [np.ascontiguousarray(x[b].T) for b in range(B)]

    nc1 = _get(("mix0", S), lambda: build_mix0(S))
    maps1 = _mix0_maps(xT, S, f(mix_norm_g)[0], f(even_w_in)[0], f(sb_q_gain)[0], f(sb_k_gain)[0], f(ret_out_gain)[0],
                       _const_tables(S))
    r1 = run_bass_kernel_spmd(nc1, maps1, core_ids=cores).results
    catT = [np.zeros((D, S), NPBF) for _ in range(B)]
    for c in cores:
        b, g = c // 4, c % 4
        o = np.asarray(r1[c]["outT"])
        catT[b][g * 128:(g + 1) * 128] = o[0:128]
        catT[b][512 + g * 128:512 + (g + 1) * 128] = o[128:256]

    nc2 = _get(("tail", own), lambda: build_tail(own, True))
    maps2 = _tail_maps(xT, catT, own, f(even_w_out)[0], f(ffn_norm_g)[0], f(mix_norm_g)[1], f(ffn_w_up)[0],
                       f(ffn_conv_w)[0], f(ffn_conv_b)[0], f(ffn_w_down)[0])
    r2 = run_bass_kernel_spmd(nc2, maps2, core_ids=cores).results
    h1T = [np.concatenate([np.asarray(r2[b * 4 + gi]["hout"]) for gi in range(4)], axis=1) for b in range(B)]
    hnT = [np.ascontiguousarray(np.concatenate([np.asarray(r2[b * 4 + gi]["hn"]) for gi in range(4)], axis=1)) for b in range(B)]

    nc3 = _get(("gla", S), lambda: build_gla(S))
    maps3 = _gla_maps(hnT, S, f(odd_w_in)[0], f(gla_w_alpha)[0], f(gla_b_alpha)[0], f(gla_out_gain)[0])
    r3 = run_bass_kernel_spmd(nc3, maps3, core_ids=cores).results
    cat2T = [np.zeros((D, S), NPBF) for _ in range(B)]
    for c in cores:
        b, g = c // 4, c % 4
        cat2T[b][g * 256:(g + 1) * 256] = np.asarray(r3[c]["outT"])

    maps4 = _tail_maps(h1T, cat2T, own, f(odd_w_out)[0], f(ffn_norm_g)[1], f(mix_norm_g)[1], f(ffn_w_up)[1],
                       f(ffn_conv_w)[1], f(ffn_conv_b)[1], f(ffn_w_down)[1])
    r4 = run_bass_kernel_spmd(nc2, maps4, core_ids=cores).results
    out = np.empty((B, S, D), np.float32)
    for b in range(B):
        oT = np.concatenate([np.asarray(r4[b * 4 + gi]["hout"]) for gi in range(4)], axis=1)
        out[b] = oT.T
    return out


def build_fused(S):
    own = S // 4
    nc = bass.Bass("TRN2", target_bir_lowering=False)
    groups = [[0, 1, 2, 3], [4, 5, 6, 7]]
    W1 = min(S, 1024)
    W2 = min(own, 256)
    with contextlib.ExitStack() as es:
        cx = Ctx(nc, es)
        m0 = _mix0_drams(cx, S, "m0_")
        t0 = _tail_drams(cx, "t0_")
        gl = _gla_drams(cx, "gl_")
        t1 = _tail_drams(cx, "t1_")
        xown = mk_dram(cx, "xown", [D, own + 2], F32, "ExternalInput")
        sel = mk_dram(cx, "sel", [128, 4], F32, "ExternalInput")
        outT = mk_dram(cx, "outT", [D, own], F32, "ExternalOutput")
        o1 = Chunked(cx, "i_o1", 256, S, W1, BF16)
        G1 = Chunked(cx, "i_G1", 1024, S, W1, BF16)
        h1d = mk_dram(cx, "i_h1d", [D, own], F32, "Internal")
        hnd = Chunked(cx, "i_hnd", D, own, W2, BF16)
        G2 = Chunked(cx, "i_G2", 4 * D, own, W2, BF16)
        hl = mk_dram(cx, "i_hl", [D, 2], F32, "Internal")
        H2 = mk_dram(cx, "i_H2", [4 * D, 2], F32, "Internal")
        o3 = Chunked(cx, "i_o3", 256, S, W1, BF16)
        G3 = Chunked(cx, "i_G3", 1024, S, W1, BF16)

        def chunk_hook(src, dst, W, key):
            def hook(ti):
                for k in range(src.n):
                    if (k + 1) * W <= (ti + 1) * 512 and k not in hook.done:
                        hook.done.add(k)
                        cx.coll_allgather(src.ch[k], dst.ch[k], groups, key=key)
            hook.done = set()
            return hook

        cx.begin_phase("m0_")
        m0["out_win"] = lambda ti: o1.win(ti * 512, 512)
        m0["after_out"] = chunk_hook(o1, G1, W1, "cc1")
        emit_mix0(cx, S, m0)
        cx.end_phase()
        assert len(m0["after_out"].done) == o1.n

        cx.begin_phase("t0_")
        t0.update(hin=xown, cat_win=G1.win, sel=sel, hout=h1d, hn=hnd.ch[0], hlast=hl,
                  hn_outs=lambda ti: hnd.wins(ti * 512, 512))
        t0["after_hn"] = chunk_hook(hnd, G2, W2, "cc2")
        emit_tail(cx, own, t0, True, fused=True)
        cx.coll_allgather(hl, H2, groups, key="cc2")
        cx.end_phase()
        assert len(t0["after_hn"].done) == hnd.n

        def hn_wins(ti):
            r = ti // (own // 512)
            c0 = (ti % (own // 512)) * 512
            res = []
            for (d_, a_, o_, w_) in G2.wins(c0, 512):
                res.append((d_, a_.rearrange("(r c p) t -> r p c t", r=4, p=128)[r], o_, w_))
            return res

        cx.begin_phase("gl_")
        gl.update(hn_wins=hn_wins, out_win=lambda ti: o3.win(ti * 512, 512))
        gl["after_out"] = chunk_hook(o3, G3, W1, "cc3")
        emit_gla(cx, S, gl, gathered=True)
        cx.end_phase()
        assert len(gl["after_out"].done) == o3.n

        cx.begin_phase("t1_")
        t1.update(hin=h1d, hin_halo=H2, cat_win=G3.win, sel=sel, hout=outT, hn=None)
        emit_tail(cx, own, t1, False, fused=True)
        cx.end_phase()
        cx.finish()
        print("fused: sems", len(cx.sems), "dmas", cx.ndma, "ops", getattr(cx, "nops", 0))
    return nc


def _fused_maps(x, mix_norm_g, even_w_in, sb_q_gain, sb_k_gain, ret_out_gain, even_w_out,
                odd_w_in, gla_w_alpha, gla_b_alpha, gla_out_gain, odd_w_out,
                ffn_norm_g, ffn_w_up, ffn_conv_w, ffn_conv_b, ffn_w_down):
    B, S, _ = x.shape
    own = S // 4
    xT = [np.ascontiguousarray(x[b].T) for b in range(B)]
    m0 = _mix0_maps(xT, S, mix_norm_g[0], even_w_in[0], sb_q_gain[0], sb_k_gain[0], ret_out_gain[0], _const_tables(S))
    perm = np.concatenate([np.concatenate([g * 128 + np.arange(128), 512 + g * 128 + np.arange(128)]) for g in range(4)])
    w_out0 = np.ascontiguousarray(even_w_out[0][perm])
    dummy_cat = [None] * B
    tl0 = _tail_maps(xT, None, own, w_out0, ffn_norm_g[0], mix_norm_g[1], ffn_w_up[0], ffn_conv_w[0], ffn_conv_b[0], ffn_w_down[0])
    tl1 = _tail_maps(None, None, own, np.ascontiguousarray(odd_w_out[0]), ffn_norm_g[1], mix_norm_g[1], ffn_w_up[1], ffn_conv_w[1],
                     ffn_conv_b[1], ffn_w_down[1])
    gm = _gla_maps([None] * B, S, odd_w_in[0], gla_w_alpha[0], gla_b_alpha[0], gla_out_gain[0])
    maps = []
    for c in range(NCORES):
        gi = c % 4
        m = {}
        for k, v in m0[c].items():
            m["m0_" + k] = v
        for k, v in tl0[c].items():
            if k == "hin":
                m["xown"] = v
            elif k != "cat":
                m["t0_" + k] = v
        for k, v in tl1[c].items():
            if k not in ("hin", "cat"):
                m["t1_" + k] = v
        for k, v in gm[c].items():
            if k != "hnT":
                m["gl_" + k] = v
        selv = np.zeros((128, 4), np.float32)
        selv[:, gi] = 1.0
        m["sel"] = selv
        maps.append(m)
    return maps


def kernel(x, mix_norm_g, even_w_in, sb_q_gain, sb_k_gain, ret_out_gain, even_w_out,
           odd_w_in, gla_w_alpha, gla_b_alpha, gla_out_gain, odd_w_out,
           ffn_norm_g, ffn_w_up, ffn_conv_w, ffn_conv_b, ffn_w_down):
    f = lambda a: np.ascontiguousarray(np.asarray(a, dtype=np.float32))
    args = [f(a) for a in (x, mix_norm_g, even_w_in, sb_q_gain, sb_k_gain, ret_out_gain, even_w_out,
                           odd_w_in, gla_w_alpha, gla_b_alpha, gla_out_gain, odd_w_out,
                           ffn_norm_g, ffn_w_up, ffn_conv_w, ffn_conv_b, ffn_w_down)]
    B, S, _ = args[0].shape
    nc = _get(("fused", S), lambda: build_fused(S))
    maps = _fused_maps(*args)
    res = run_bass_kernel_spmd(nc, maps, core_ids=list(range(NCORES))).results
    out = np.empty((B, S, D), np.float32)
    for b in range(B):
        oT = np.concatenate([np.asarray(res[b * 4 + gi]["outT"]) for gi in range(4)], axis=1)
        out[b] = oT.T
    return out
```

```python
import contextlib
import numpy as np
import ml_dtypes
import concourse.bass as bass
import concourse.mybir as mybir
from concourse.bass_utils import run_bass_kernel_spmd

F32 = mybir.dt.float32
BF16 = mybir.dt.bfloat16
AF = mybir.ActivationFunctionType
ALU = mybir.AluOpType
AX = mybir.AxisListType
NPBF = ml_dtypes.bfloat16

D = 1024
DFF = 2816
NFF = DFF // 128
EPS = 1e-6
NCORES = 8


class T:
    def __init__(self, t, name, sl=None, st=None):
        self.t = t if sl is None else t[sl]
        self.name = name
        self.st = st
        self.w = None
        self.r = {}
        self.dsem = None
        self.wd = {}

    def __getitem__(self, k):
        return self.t[k]


class Ctx:
    def __init__(self, nc, es):
        self.nc = nc
        self.es = es
        self.sems = {}
        self.issued = {}
        self.isdma = set()
        self.eng = {"pe": nc.tensor, "act": nc.scalar, "dve": nc.vector, "pool": nc.gpsimd, "sp": nc.sync}
        self.waited = {e: {} for e in self.eng}
        for e in self.eng:
            self._mksem("e_" + e)
        self.ndma = 0
        self.pfx = ""
        self.es_phase = es
        self.ncc = 0

    def begin_phase(self, pfx):
        self.pfx = pfx
        self.es_phase = contextlib.ExitStack()
        self.es_phase.__enter__()

    def end_phase(self):
        self.barrier()
        self.es_phase.__exit__(None, None, None)
        self.es_phase = self.es

    def coll_allgather(self, in_t, out_t, groups, key="cc"):
        if key not in self.sems:
            self._mksem(key, dma=True)
        self._wait("pool", [in_t], [out_t])
        ins = self.nc.gpsimd.collective_compute("AllGather", ALU.bypass, replica_groups=groups,
                                                ins=[in_t.t.ap()], outs=[out_t.t.ap()])
        ins.then_inc(self.sems[key], 1)
        self.issued[key] += 1
        in_t.r[key] = self.issued[key]
        out_t.w = (key, self.issued[key])
        out_t.r = {}
        return ins

    def _mksem(self, key, dma=False):
        self.sems[key] = self.es.enter_context(self.nc.semaphore(key))
        self.issued[key] = 0
        if dma:
            self.isdma.add(key)
        return key

    def sb(self, name, shape, dt):
        name = self.pfx + name
        return T(self.es_phase.enter_context(self.nc.sbuf_tensor(name, list(shape), dt)), name)

    def ps(self, name, shape, dt=F32):
        name = self.pfx + name
        t = T(self.es_phase.enter_context(self.nc.psum_tensor(name, list(shape), dt)), name)
        t.excl = True
        return t

    def dram(self, name, shape, dt, kind):
        return T(self.nc.dram_tensor(name, list(shape), dt, kind=kind), name)

    def _wait(self, e, reads, writes, skip_self_waw=False):
        deps = {}

        def add(tok):
            if tok is None:
                return
            k, v = tok
            if deps.get(k, 0) < v:
                deps[k] = v

        reads = [t.st or t for t in reads]
        writes = [t.st or t for t in writes]
        for t in reads:
            add(t.w)
            for k_, v_ in t.wd.items():
                add((k_, v_))
            if getattr(t, "excl", False):
                for k, v in t.r.items():
                    if k != "e_" + e:
                        add((k, v))
        for t in writes:
            if not (skip_self_waw and t.w is not None and t.w[0] == "e_" + e):
                add(t.w)
            for k, v in t.r.items():
                add((k, v))
        w = self.waited[e]
        for k, v in deps.items():
            if k in self.isdma:
                v = self.issued[k]
            if w.get(k, 0) < v:
                self.eng[e].wait_ge(self.sems[k], v)
                w[k] = v

    def op(self, e, fn, reads=(), writes=(), acc=False):
        if getattr(self, "mute", False):
            return None
        self.nops = getattr(self, "nops", 0) + 1
        if self.nops > getattr(self, "oplimit", 10 ** 9):
            return None
        if getattr(self, "insec", False):
            self.seccount = getattr(self, "seccount", 0) + 1
            if self.seccount > getattr(self, "seclimit", 10 ** 9):
                return None
        self._wait(e, reads, writes, skip_self_waw=acc)
        ins = fn()
        k = "e_" + e
        self.issued[k] += 1
        ins.then_inc(self.sems[k], 1)
        tok = (k, self.issued[k])
        for t in reads:
            t = t.st or t
            if t.r.get(k, 0) < tok[1]:
                t.r[k] = tok[1]
        for t in writes:
            t = t.st or t
            t.w = tok
            t.r = {}
        return ins

    def dma(self, q, out_t, out_ap, in_t, in_ap, sbt=None):
        if sbt is None:
            sbt = out_t if not isinstance(out_t, DT) else in_t
        if sbt.dsem is None:
            sbt.dsem = self._mksem("d_%s" % sbt.name, dma=True)
        to_dram = isinstance(out_t, DT)
        if to_dram:
            self._wait(q, [in_t], [])
            w_ = self.waited[q]
            for k_, v_ in list(out_t.r.items()):
                if k_ in self.isdma:
                    v_ = self.issued[k_]
                if w_.get(k_, 0) < v_:
                    self.eng[q].wait_ge(self.sems[k_], v_)
                    w_[k_] = v_
        else:
            self._wait(q, [in_t], [out_t])
        ins = self.eng[q].dma_start(out=out_ap, in_=in_ap)
        k = sbt.dsem
        self.issued[k] += 16
        ins.then_inc(self.sems[k], 16)
        tok = (k, self.issued[k])
        (in_t.st or in_t).r[k] = tok[1]
        if to_dram:
            out_t.wd[k] = tok[1]
            out_t.w = tok
        else:
            (out_t.st or out_t).w = tok
            (out_t.st or out_t).r = {}
        self.ndma += 1
        return ins

    def view(self, base, name, sl):
        return T(base.t, name, sl)

    def barrier(self):
        for e in self.eng:
            for k, v in self.issued.items():
                if v > 0 and k != "e_" + e and self.waited[e].get(k, 0) < v:
                    self.eng[e].wait_ge(self.sems[k], v)
                    self.waited[e][k] = v

    def finish(self):
        for k, v in self.issued.items():
            if v > 0 and self.waited["sp"].get(k, 0) < v:
                self.nc.sync.wait_ge(self.sems[k], v)
                self.waited["sp"][k] = v


class DT(T):
    pass


class Chunked:
    def __init__(self, cx, name, R, total, W, dt):
        self.W = W
        self.n = total // W
        self.ch = [mk_dram(cx, "%s_%d" % (name, k), [R, W], dt, "Internal") for k in range(self.n)]

    def win(self, t0, width):
        k = t0 // self.W
        c0 = t0 - k * self.W
        assert c0 + width <= self.W
        return self.ch[k], self.ch[k].t.ap()[:, c0:c0 + width]

    def wins(self, t0, width):
        out = []
        t = t0
        while t < t0 + width:
            k = t // self.W
            c0 = t - k * self.W
            w = min(self.W - c0, t0 + width - t)
            out.append((self.ch[k], self.ch[k].t.ap()[:, c0:c0 + w], t - t0, w))
            t += w
        return out


def mk_dram(cx, name, shape, dt, kind):
    return DT(cx.nc.dram_tensor(name, list(shape), dt, kind=kind), name)


def build_tail(own, want_hn):
    nc = bass.Bass("TRN2", target_bir_lowering=False)
    with contextlib.ExitStack() as es:
        cx = Ctx(nc, es)
        dr = _tail_drams(cx, "")
        dr["hin"] = mk_dram(cx, "hin", [D, own + 2], F32, "ExternalInput")
        dr["cat"] = mk_dram(cx, "cat", [D, own + 2], BF16, "ExternalInput")
        dr["hout"] = mk_dram(cx, "hout", [D, own], F32, "ExternalOutput")
        dr["hn"] = mk_dram(cx, "hn", [D, own], BF16, "ExternalOutput")
        cx.begin_phase("")
        emit_tail(cx, own, dr, want_hn, fused=False)
        cx.end_phase()
        cx.finish()
    return nc


def _tail_drams(cx, p):
    return dict(
        w_out=mk_dram(cx, p + "w_out", [D, D], F32, "ExternalInput"),
        w_up=mk_dram(cx, p + "w_up", [NFF * 128, 2048], F32, "ExternalInput"),
        w_down=mk_dram(cx, p + "w_down", [DFF, D], F32, "ExternalInput"),
        g_ffn=mk_dram(cx, p + "g_ffn", [128, 8], F32, "ExternalInput"),
        g_next=mk_dram(cx, p + "g_next", [128, 8], F32, "ExternalInput"),
        conv_w=mk_dram(cx, p + "conv_w", [128, NFF * 3], F32, "ExternalInput"),
        conv_b=mk_dram(cx, p + "conv_b", [128, NFF], F32, "ExternalInput"))


def emit_tail(cx, own, dr, want_hn, fused):
    nc = cx.nc
    NT = own // 512
    if True:
        hin, cat, w_out, w_up, w_down = dr["hin"], dr.get("cat"), dr["w_out"], dr["w_up"], dr["w_down"]
        gff_d, gnx_d, cw_d, cb_d, hout, hnout = dr["g_ffn"], dr["g_next"], dr["conv_w"], dr["conv_b"], dr["hout"], dr.get("hn")
        hin_halo = dr.get("hin_halo")
        hlast = dr.get("hlast")
        hin_ap = hin.t.ap().rearrange("(c p) t -> p c t", p=128)
        cat_ap = cat.t.ap().rearrange("(c p) t -> p c t", p=128) if cat is not None else None
        cat_win = dr.get("cat_win")
        hn_outs = dr.get("hn_outs")
        hout_ap = hout.t.ap().rearrange("(c p) t -> p c t", p=128)
        hn_ap = hnout.t.ap().rearrange("(c p) t -> p c t", p=128) if hnout is not None else None
        wout_ap = w_out.t.ap().rearrange("(k p) n -> p k n", p=128)
        wup_ap = w_up.t.ap().rearrange("(c p) f -> c p f", p=128)
        wdown_ap = w_down.t.ap().rearrange("(k p) n -> p k n", p=128)
        hoff = 2 if hin_halo is None else 0

        wout_b = cx.sb("wout_b", [128, 8, 1024], BF16)
        wdown_b = cx.sb("wdown_b", [128, NFF, 1024], BF16)
        stg = [cx.sb("stg%d" % i, [128, 8, 256], F32) for i in range(2)]
        wup_b = [cx.sb("wup_b%d" % i, [128, 8, 256], BF16) for i in range(3)]
        h1s = [cx.sb("h1_%d" % i, [128, 8, 512], F32) for i in range(2)]
        catb = cx.sb("catb", [128, 8, 512], BF16)
        hb = cx.sb("hb", [128, 8, 512], BF16)
        sqs = [cx.sb("sq%d" % i, [128, 512], BF16) for i in range(2)]
        rstd = cx.sb("rstd", [128, 512], F32)
        rstd2 = cx.sb("rstd2", [128, 512], F32)
        Us = [cx.sb("U%d" % i, [128, 514], F32) for i in range(2)]
        t1s = [cx.sb("t1_%d" % i, [128, 512], F32) for i in range(2)]
        t2s = [cx.sb("t2_%d" % i, [128, 512], F32) for i in range(2)]
        sgs = [cx.sb("sg%d" % i, [128, 512], F32) for i in range(2)]
        actb = cx.sb("actb", [128, NFF, 512], BF16)
        tmpo = [cx.sb("tmpo%d" % i, [128, 512], F32) for i in range(2)]
        hnb = hb
        carry = cx.sb("carry", [128, NFF, 2], F32)
        h1h = cx.sb("h1h", [128, 8, 2], F32)
        cath = cx.sb("cath", [128, 8, 2], BF16)
        hbh = cx.sb("hbh", [128, 8, 2], BF16)
        rstdh = cx.sb("rstdh", [128, 2], F32)
        gff = cx.sb("gff", [128, 8], F32)
        gnx = cx.sb("gnx", [128, 8], F32)
        cw = cx.sb("cw", [128, NFF * 3], F32)
        cb = cx.sb("cb", [128, NFF], F32)
        ones = cx.sb("ones", [128, 128], BF16)

        pms = [cx.ps("pm%d" % i, [128, 512]) for i in range(2)]
        pssq = cx.ps("pssq", [128, 512])
        pus = [cx.ps("pu%d" % i, [128, 512]) for i in range(2)]
        pvs = [cx.ps("pv%d" % i, [128, 512]) for i in range(2)]
        puh = cx.ps("puh", [128, 512])

        V, A, P, PE = nc.vector, nc.scalar, nc.gpsimd, nc.tensor

        cx.op("dve", lambda: V.memset(ones[:], 1.0), writes=[ones])
        epst = cx.sb("epst", [128, 1], F32)
        cx.op("dve", lambda: V.memset(epst[:], EPS), writes=[epst])
        for dst, src in ((gff, gff_d), (gnx, gnx_d), (cw, cw_d), (cb, cb_d)):
            cx.dma("sp", dst, dst[:], src, src.t.ap())

        def load_wup(gidx):
            c = gidx % NFF
            s = stg[gidx % 2]
            cx.dma("sp", s, s[:].rearrange("p a b -> p (a b)"), w_up, wup_ap[c])
            wb = wup_b[gidx % 3]
            cx.op("pool", lambda: P.tensor_copy(out=wb[:], in_=s[:]), reads=[s], writes=[wb])

        for k in range(8):
            s = stg[k % 2]
            sv = s[:].rearrange("p a b -> p (a b)")
            cx.dma("sp", s, sv[:, 0:1024], w_out, wout_ap[:, k, :])
            cx.op("act", lambda: A.copy(out=wout_b[:, k, :], in_=sv[:, 0:1024]), reads=[s], writes=[wout_b])
        for c in range(NFF):
            s = stg[c % 2]
            sv = s[:].rearrange("p a b -> p (a b)")
            cx.dma("sp", s, sv[:, 0:1024], w_down, wdown_ap[:, c, :])
            cx.op("act", lambda: A.copy(out=wdown_b[:, c, :], in_=sv[:, 0:1024]), reads=[s], writes=[wdown_b])

        def wout_stage(h1, catt, n):
            for nch in range(8):
                pm = pms[nch % 2]
                for k in range(8):
                    cx.op("pe", lambda: PE.matmul(pm[:, 0:n], wout_b[:, k, nch * 128:(nch + 1) * 128], catt[:, k, 0:n],
                                                  start=(k == 0), stop=(k == 7)),
                          reads=[wout_b, catt], writes=[pm], acc=(k > 0))
                cx.op("dve", lambda: V.tensor_tensor(h1[:, nch, 0:n], h1[:, nch, 0:n], pm[:, 0:n], ALU.add),
                      reads=[pm, h1], writes=[h1])

        def norm_stage(h1, n, g, hbt, rs):
            for k in range(8):
                sq = sqs[k % 2]
                cx.op("act", lambda: A.activation(out=sq[:, 0:n], in_=h1[:, k, 0:n], func=AF.Square),
                      reads=[h1], writes=[sq])
                cx.op("pe", lambda: PE.matmul(pssq[:, 0:n], ones[:, :], sq[:, 0:n], start=(k == 0), stop=(k == 7)),
                      reads=[sq, ones], writes=[pssq], acc=(k > 0))
                if hbt is not None:
                    cx.op("dve", lambda: V.tensor_scalar(hbt[:, k, 0:n], h1[:, k, 0:n], g[:, k:k + 1], None, ALU.mult),
                          reads=[h1, g], writes=[hbt])
            cx.op("act", lambda: A.activation(out=rs[:, 0:n], in_=pssq[:, 0:n], func=AF.Ln, scale=1.0 / D, bias=epst[:, 0:1]),
                  reads=[pssq, epst], writes=[rs])
            cx.op("act", lambda: A.activation(out=rs[:, 0:n], in_=rs[:, 0:n], func=AF.Exp, scale=-0.5),
                  reads=[rs], writes=[rs])

        V, A = nc.vector, nc.scalar
        if fused:
            selt = cx.sb("selt", [128, 4], F32)
            cx.dma("sp", selt, selt[:], dr["sel"], dr["sel"].t.ap())
            cands = [cx.sb("cand%d" % i, [128, 8, 512], BF16) for i in range(2)]
            candh = [cx.sb("candh%d" % i, [128, 8, 2], BF16) for i in range(2)]
            candf = [cx.sb("candf%d" % i, [128, 8, 2], F32) for i in range(2)]

        def blend(dst, srcs, width, bufs, src_t):
            first = True
            for n_, (j, (src_t, ap_)) in enumerate(srcs):
                cb_ = bufs[n_ % 2]
                cx.dma("sp", cb_, cb_[:, :, 0:width], src_t, ap_)
                if first:
                    cx.op("act", lambda: A.activation(out=dst[:, :, 0:width], in_=cb_[:, :, 0:width], func=AF.Copy, scale=selt[:, j:j + 1]),
                          reads=[cb_, selt], writes=[dst])
                    first = False
                else:
                    cx.op("dve", lambda: V.scalar_tensor_tensor(dst[:, :, 0:width], cb_[:, :, 0:width], selt[:, j:j + 1], dst[:, :, 0:width],
                                                                ALU.mult, ALU.add),
                          reads=[cb_, selt, dst], writes=[dst])

        def catw(t0, width):
            d_, a_ = cat_win(t0, width)
            return d_, a_.rearrange("(c p) t -> p c t", p=128)

        def load_tile(ti, h1):
            cx.dma("sp", h1, h1[:], hin, hin_ap[:, :, hoff + ti * 512:hoff + (ti + 1) * 512])
            if not fused:
                cx.dma("sp", catb, catb[:], cat, cat_ap[:, :, 2 + ti * 512:2 + (ti + 1) * 512])
            else:
                blend(catb, [(j, catw(j * own + ti * 512, 512)) for j in range(4)], 512, cands, None)

        if not fused:
            cx.dma("sp", h1h, h1h[:], hin, hin_ap[:, :, 0:2])
            cx.dma("sp", cath, cath[:], cat, cat_ap[:, :, 0:2])
        else:
            blend(cath, [(j, catw(j * own - 2, 2)) for j in range(1, 4)], 2, candh, None)
            if hin_halo is None:
                cx.dma("sp", h1h, h1h[:], hin, hin_ap[:, :, 0:2])
            else:
                hh_ap = hin_halo.t.ap().rearrange("(r c p) t -> r p c t", r=4, p=128)
                blend(h1h, [(r + 1, (hin_halo, hh_ap[r])) for r in range(3)], 2, candf, None)
        wout_stage(h1h, cath, 2)
        norm_stage(h1h, 2, gff, hbh, rstdh)

        load_wup(0)
        load_wup(1)
        for ti in range(NT):
            h1 = h1s[ti % 2]
            load_tile(ti, h1)
            wout_stage(h1, catb, 512)
            norm_stage(h1, 512, gff, hb, rstd)
            for c in range(NFF):
                gidx = ti * NFF + c
                if c == 4 and ti > 0 and "after_hn" in dr:
                    dr["after_hn"](ti - 1)
                if gidx + 2 < NT * NFF:
                    load_wup(gidx + 2)
                wb = wup_b[gidx % 3]
                pu, pv, U = pus[c % 2], pvs[c % 2], Us[c % 2]
                t1, t2, sg = t1s[c % 2], t2s[c % 2], sgs[c % 2]
                for k in range(8):
                    cx.op("pe", lambda: PE.matmul(pu[:], wb[:, k, 0:128], hb[:, k, :], start=(k == 0), stop=(k == 7)),
                          reads=[wb, hb], writes=[pu], acc=(k > 0))
                if ti == 0:
                    for k in range(8):
                        cx.op("pe", lambda: PE.matmul(puh[:, 0:2], wb[:, k, 0:128], hbh[:, k, 0:2], start=(k == 0), stop=(k == 7)),
                              reads=[wb, hbh], writes=[puh], acc=(k > 0))
                for k in range(8):
                    cx.op("pe", lambda: PE.matmul(pv[:], wb[:, k, 128:256], hb[:, k, :], start=(k == 0), stop=(k == 7)),
                          reads=[wb, hb], writes=[pv], acc=(k > 0))
                if ti == 0:
                    cx.op("dve", lambda: V.tensor_tensor(U[:, 0:2], puh[:, 0:2], rstdh[:, 0:2], ALU.mult),
                          reads=[puh, rstdh], writes=[U])
                else:
                    cx.op("act", lambda: A.copy(out=U[:, 0:2], in_=carry[:, c, :]), reads=[carry], writes=[U])
                cx.op("dve", lambda: V.tensor_tensor(U[:, 2:514], pu[:], rstd[:], ALU.mult),
                      reads=[pu, rstd, U], writes=[U])
                cx.op("act", lambda: A.copy(out=carry[:, c, :], in_=U[:, 512:514]), reads=[U], writes=[carry])
                cx.op("act", lambda: A.activation(out=t1[:], in_=U[:, 0:512], func=AF.Identity,
                                                  scale=cw[:, 3 * c:3 * c + 1], bias=cb[:, c:c + 1]),
                      reads=[U, cw, cb], writes=[t1])
                cx.op("dve", lambda: V.scalar_tensor_tensor(t2[:], U[:, 1:513], cw[:, 3 * c + 1:3 * c + 2], t1[:], ALU.mult, ALU.add),
                      reads=[U, cw, t1], writes=[t2])
                cx.op("dve", lambda: V.scalar_tensor_tensor(t1[:], U[:, 2:514], cw[:, 3 * c + 2:3 * c + 3], t2[:], ALU.mult, ALU.add),
                      reads=[U, cw, t2], writes=[t1])
                cx.op("act", lambda: A.activation(out=sg[:], in_=t1[:], func=AF.Silu), reads=[t1], writes=[sg])
                cx.op("dve", lambda: V.tensor_tensor(actb[:, c, :], sg[:], pv[:], ALU.mult),
                      reads=[sg, pv], writes=[actb])
            for nch in range(8):
                pm = pms[nch % 2]
                tm = tmpo[nch % 2]
                for c in range(NFF):
                    cx.op("pe", lambda: PE.matmul(pm[:], wdown_b[:, c, nch * 128:(nch + 1) * 128], actb[:, c, :],
                                                  start=(c == 0), stop=(c == NFF - 1)),
                          reads=[wdown_b, actb], writes=[pm], acc=(c > 0))
                cx.op("dve", lambda: V.tensor_tensor(tm[:], pm[:], rstd[:], ALU.mult), reads=[pm, rstd], writes=[tm])
                cx.op("dve", lambda: V.tensor_tensor(h1[:, nch, :], h1[:, nch, :], tm[:], ALU.add),
                      reads=[tm, h1], writes=[h1])
            if want_hn:
                norm_stage(h1, 512, gnx, None, rstd2)
                for k in range(8):
                    cx.op("dve", lambda: V.scalar_tensor_tensor(hnb[:, k, :], h1[:, k, :], gnx[:, k:k + 1], rstd2[:], ALU.mult, ALU.mult),
                          reads=[h1, gnx, rstd2], writes=[hnb])
                if hn_outs is None:
                    cx.dma("sp", hnout, hn_ap[:, :, ti * 512:(ti + 1) * 512], hnb, hnb[:])
                else:
                    for (hd_, ha_, ho_, hw_) in hn_outs(ti):
                        cx.dma("sp", hd_, ha_.rearrange("(c p) t -> p c t", p=128), hnb, hnb[:, :, ho_:ho_ + hw_])
            cx.dma("sp", hout, hout_ap[:, :, ti * 512:(ti + 1) * 512], h1, h1[:])
            if hlast is not None and ti == NT - 1:
                cx.dma("sp", hlast, hlast.t.ap().rearrange("(c p) t -> p c t", p=128), h1, h1[:, :, 510:512])
            if ti == NT - 1 and "after_hn" in dr:
                dr["after_hn"](ti)


def build_mix0(S, do_b=True, do_ret=True, do_sbn=True, seclimit=10 ** 9):
    nc = bass.Bass("TRN2", target_bir_lowering=False)
    with contextlib.ExitStack() as es:
        cx = Ctx(nc, es)
        cx.seclimit = seclimit
        dr = _mix0_drams(cx, S, "")
        dr["outT"] = mk_dram(cx, "outT", [256, S], BF16, "ExternalOutput")
        cx.begin_phase("")
        emit_mix0(cx, S, dr, do_b, do_ret, do_sbn)
        cx.end_phase()
        cx.finish()
    return nc


def _mix0_drams(cx, S, p):
    return dict(
        xT=mk_dram(cx, p + "xT", [(S // 512) * 128, 4096], F32, "ExternalInput"),
        wsel=mk_dram(cx, p + "wsel", [D, 896], F32, "ExternalInput"),
        g_mix=mk_dram(cx, p + "g_mix", [128, 8], F32, "ExternalInput"),
        gains=mk_dram(cx, p + "gains", [1, 384], F32, "ExternalInput"),
        rope=mk_dram(cx, p + "rope", [S, 256], F32, "ExternalInput"),
        rconst=mk_dram(cx, p + "rconst", [128, 131], F32, "ExternalInput"),
        cmat=mk_dram(cx, p + "cmat", [128, 384], BF16, "ExternalInput"),
        dmask=mk_dram(cx, p + "dmask", [128, 2048], F32, "ExternalInput"))


def emit_mix0(cx, S, dr, do_b=True, do_ret=True, do_sbn=True):
    nc = cx.nc
    NCH = S // 128
    NT = S // 512
    if True:
        xT, wsel, gmix_d, gains_d, rope_d, rconst_d, cmat_d, dmask_d, outT = (
            dr["xT"], dr["wsel"], dr["g_mix"], dr["gains"], dr["rope"], dr["rconst"], dr["cmat"], dr["dmask"], dr.get("outT"))

        x_ap = xT.t.ap().rearrange("(n p) f -> n p f", p=128)
        w_ap = wsel.t.ap().rearrange("(k p) n -> p k n", p=128)
        rope_ap = rope_d.t.ap().rearrange("(n p) c -> p n c", p=128)
        if "out_win" in dr:
            out_win = dr["out_win"]
        else:
            out_win = lambda ti: (outT, outT.t.ap()[:, ti * 512:(ti + 1) * 512])

        V, A, P, PE = nc.vector, nc.scalar, nc.gpsimd, nc.tensor

        qT = cx.sb("qT", [128, S], BF16)
        kT = cx.sb("kT", [128, S], BF16)
        vres = cx.sb("vres", [128, NCH, 128], BF16)
        wb = cx.sb("wb", [128, 8, 896], BF16)
        gmix = cx.sb("gmix", [128, 8], F32)
        gains = cx.sb("gains_s", [128, 384], F32)
        rconst = cx.sb("rconst_s", [128, 131], F32)
        cmat = cx.sb("cmat_s", [128, 384], BF16)
        dmask = cx.sb("dmask_s", [128, 2048], F32)
        ones = cx.sb("ones", [128, 128], BF16)
        epst = cx.sb("epst", [128, 1], F32)
        cx.op("dve", lambda: V.memset(ones[:], 1.0), writes=[ones])
        cx.op("dve", lambda: V.memset(epst[:], EPS), writes=[epst])
        cx.dma("sp", gmix, gmix[:], gmix_d, gmix_d.t.ap())
        cx.dma("sp", gains, gains[:], gains_d, gains_d.t.ap()[0:1, :].broadcast_to([128, 384]))
        cx.dma("sp", rconst, rconst[:], rconst_d, rconst_d.t.ap())
        cx.dma("sp", cmat, cmat[:], cmat_d, cmat_d.t.ap())
        cx.dma("sp", dmask, dmask[:], dmask_d, dmask_d.t.ap())
        cx.op("dve", lambda: V.tensor_scalar(gains[:, 0:128], gains[:, 0:128], 0.125, None, ALU.mult),
              reads=[gains], writes=[gains])
        ident = cmat
        decT = rconst

        with contextlib.ExitStack() as esA:
            cxa_sb = lambda name, shape, dt: T(esA.enter_context(nc.sbuf_tensor(cx.pfx + name, list(shape), dt)), cx.pfx + name)
            def cxa_ps(name, shape, dt=F32):
                t = T(esA.enter_context(nc.psum_tensor(cx.pfx + name, list(shape), dt)), cx.pfx + name)
                t.excl = True
                return t
            stg = [cxa_sb("stg%d" % i, [128, 896], F32) for i in range(2)]
            xts = [cxa_sb("xt%d" % i, [128, 8, 512], F32) for i in range(2)]
            ropes = [cxa_sb("rope%d" % i, [128, 4, 256], F32) for i in range(2)]
            hb = cxa_sb("hb", [128, 8, 512], BF16)
            sq = cxa_sb("sq", [128, 8, 512], BF16)
            rstd = cxa_sb("rstd", [128, 4], F32)
            Rs = [cxa_sb("R%d" % i, [128, 512], F32) for i in range(2)]
            SBs = [cxa_sb("SB%d" % i, [128, 384], F32) for i in range(2)]
            sqt = cxa_sb("sqt", [128, 256], F32)
            qs4 = cxa_sb("qs4", [128, 4], F32)
            rq4 = cxa_sb("rq4", [128, 4], F32)
            qkn = cxa_sb("qkn", [128, 256], BF16)
            ra = [cxa_sb("ra%d" % i, [128, 2, 64], F32) for i in range(4)]
            qkr = cxa_sb("qkr", [128, 2, 128], BF16)
            qTr = cxa_sb("qTr", [128, 128], BF16)
            kTr = cxa_sb("kTr", [128, 128], BF16)
            kz = cxa_sb("kz", [128, 128], BF16)
            vb = cxa_sb("vb", [128, 128], BF16)
            ATb = cxa_sb("ATb", [128, 128], BF16)
            o1 = cxa_sb("o1", [128, 128], F32)
            o2 = cxa_sb("o2", [128, 128], F32)
            junk = cxa_sb("junk", [128, 128], F32)
            os1 = cxa_sb("os1", [128, 1], F32)
            rr = cxa_sb("rr", [128, 1], F32)
            sgt = cxa_sb("sgt", [128, 128], F32)
            ob = cxa_sb("ob", [128, 128], BF16)
            Rst = cxa_sb("Rst", [128, 128], F32)
            Rstb = cxa_sb("Rstb", [128, 128], BF16)
            outb = [cxa_sb("outb%d" % i, [128, 512], BF16) for i in range(2)]

            pp0 = [cxa_ps("pp0_%d" % i, [128, 512]) for i in range(2)]
            pp1 = [cxa_ps("pp1_%d" % i, [128, 512]) for i in range(2)]
            pssq = cxa_ps("pssq", [128, 4])
            ptr_t = cxa_ps("ptr", [128, 8, 128], BF16)
            ptr = [T(ptr_t.t, "ptr%d" % i, (slice(None), i, slice(None)), st=ptr_t) for i in range(8)]
            pmA_t = cxa_ps("pmA", [128, 4, 128])
            pmA = [T(pmA_t.t, "pmA%d" % i, (slice(None), i, slice(None)), st=pmA_t) for i in range(4)]

            for k in range(8):
                s = stg[k % 2]
                cx.dma("sp", s, s[:], wsel, w_ap[:, k, :])
                cx.op("act", lambda: A.copy(out=wb[:, k, :], in_=s[:]), reads=[s], writes=[wb])
            cx.op("dve", lambda: V.memset(Rst[:], 0.0), writes=[Rst])
            cx.op("dve", lambda: V.memset(Rstb[:], 0.0), writes=[Rstb])

            def lnexp_rstd(out_t, out_ap_, in_t, in_ap_, scale):
                cx.op("act", lambda: A.activation(out=out_ap_, in_=in_ap_, func=AF.Ln, scale=scale, bias=epst[:, 0:1]),
                      reads=[in_t, epst], writes=[out_t])
                cx.op("act", lambda: A.activation(out=out_ap_, in_=out_ap_, func=AF.Exp, scale=-0.5),
                      reads=[out_t], writes=[out_t])

            for ti in range(NT):
                xt = xts[ti % 2]
                rp = ropes[ti % 2]
                cx.dma("sp", xt, xt[:].rearrange("p a b -> p (a b)"), xT, x_ap[ti])
                cx.dma("sp", rp, rp[:], rope_d, rope_ap[:, ti * 4:(ti + 1) * 4, :])
                for k in range(8):
                    cx.op("act", lambda: A.activation(out=sq[:, k, :], in_=xt[:, k, :], func=AF.Square), reads=[xt], writes=[sq])
                    cx.op("dve", lambda: V.tensor_scalar(hb[:, k, :], xt[:, k, :], gmix[:, k:k + 1], None, ALU.mult),
                          reads=[xt, gmix], writes=[hb])
                for j in range(4):
                    for k in range(8):
                        cx.op("pe", lambda: PE.matmul(pssq[:, j:j + 1], sq[:, k, j * 128:(j + 1) * 128], ones[:, 0:1],
                                                      start=(k == 0), stop=(k == 7)),
                              reads=[sq, ones], writes=[pssq], acc=(k > 0 or j > 0))
                lnexp_rstd(rstd, rstd[:], pssq, pssq[:], 1.0 / D)
                ob_stage = outb[ti % 2]
                for j in range(4):
                    n = ti * 4 + j
                    p0, p1 = pp0[n % 2], pp1[n % 2]
                    R, SBt = Rs[n % 2], SBs[n % 2]
                    for k in range(8):
                        cx.op("pe", lambda: PE.matmul(p0[:], hb[:, k, j * 128:(j + 1) * 128], wb[:, k, 0:512],
                                                      start=(k == 0), stop=(k == 7)),
                              reads=[hb, wb], writes=[p0], acc=(k > 0))
                    for k in range(8):
                        cx.op("pe", lambda: PE.matmul(p1[:, 0:384], hb[:, k, j * 128:(j + 1) * 128], wb[:, k, 512:896],
                                                      start=(k == 0), stop=(k == 7)),
                              reads=[hb, wb], writes=[p1], acc=(k > 0))
                    cx.op("act", lambda: A.activation(out=R[:], in_=p0[:], func=AF.Copy, scale=rstd[:, j:j + 1]),
                          reads=[p0, rstd], writes=[R])
                    cx.op("act", lambda: A.activation(out=SBt[:], in_=p1[:, 0:384], func=AF.Copy, scale=rstd[:, j:j + 1]),
                          reads=[p1, rstd], writes=[SBt])
                    cx.mute = not do_sbn
                    cx.op("dve", lambda: V.tensor_tensor(sqt[:], SBt[:, 0:256], SBt[:, 0:256], ALU.mult), reads=[SBt], writes=[sqt])
                    cx.op("dve", lambda: V.tensor_reduce(qs4[:], sqt[:].rearrange("p (a d) -> p a d", d=64), AX.X, ALU.add),
                          reads=[sqt], writes=[qs4])
                    lnexp_rstd(rq4, rq4[:], qs4, qs4[:], 1.0 / 64)
                    for h4 in range(4):
                        sl = slice(h4 * 64, (h4 + 1) * 64)
                        cx.op("dve", lambda: V.scalar_tensor_tensor(qkn[:, sl], SBt[:, sl], rq4[:, h4:h4 + 1], gains[:, sl], ALU.mult, ALU.mult),
                              reads=[SBt, rq4, gains], writes=[qkn])
                    cx.op("pe", lambda: PE.transpose(ptr[0][:], qkn[:, 0:128], ident[:, 0:128]), reads=[qkn, ident], writes=[ptr[0]])
                    cx.op("pe", lambda: PE.transpose(ptr[1][:], qkn[:, 128:256], ident[:, 0:128]), reads=[qkn, ident], writes=[ptr[1]])
                    cx.op("act", lambda: A.copy(out=qT[:, n * 128:(n + 1) * 128], in_=ptr[0][:]), reads=[ptr[0]], writes=[qT])
                    cx.op("act", lambda: A.copy(out=kT[:, n * 128:(n + 1) * 128], in_=ptr[1][:]), reads=[ptr[1]], writes=[kT])
                    cx.op("pool", lambda: P.tensor_copy(out=vres[:, n, :], in_=SBt[:, 256:384]), reads=[SBt], writes=[vres])
                    cx.mute = not do_ret
                    cx.insec = True
                    Rv = R[:, 0:256].rearrange("p (a d) -> p a d", a=2)
                    CC = rp[:, j, 0:128].rearrange("p (a d) -> p a d", a=2)
                    SS = rp[:, j, 128:256].rearrange("p (a d) -> p a d", a=2)
                    cx.op("dve", lambda: V.tensor_tensor(ra[0][:], Rv[:, :, 0:64], CC, ALU.mult), reads=[R, rp], writes=[ra[0]])
                    cx.op("dve", lambda: V.tensor_tensor(ra[1][:], Rv[:, :, 64:128], SS, ALU.mult), reads=[R, rp], writes=[ra[1]])
                    cx.op("dve", lambda: V.tensor_tensor(ra[2][:], Rv[:, :, 0:64], SS, ALU.mult), reads=[R, rp], writes=[ra[2]])
                    cx.op("dve", lambda: V.tensor_tensor(ra[3][:], Rv[:, :, 64:128], CC, ALU.mult), reads=[R, rp], writes=[ra[3]])
                    cx.op("dve", lambda: V.tensor_tensor(qkr[:, :, 0:64], ra[0][:], ra[1][:], ALU.subtract), reads=[ra[0], ra[1]], writes=[qkr])
                    cx.op("dve", lambda: V.tensor_tensor(qkr[:, :, 64:128], ra[2][:], ra[3][:], ALU.add), reads=[ra[2], ra[3]], writes=[qkr])
                    cx.op("pe", lambda: PE.transpose(ptr[2][:], qkr[:, 0, :], ident[:, 0:128]), reads=[qkr, ident], writes=[ptr[2]])
                    cx.op("pe", lambda: PE.transpose(ptr[3][:], qkr[:, 1, :], ident[:, 0:128]), reads=[qkr, ident], writes=[ptr[3]])
                    cx.op("act", lambda: A.copy(out=qTr[:], in_=ptr[2][:]), reads=[ptr[2]], writes=[qTr])
                    cx.op("act", lambda: A.copy(out=kTr[:], in_=ptr[3][:]), reads=[ptr[3]], writes=[kTr])
                    cx.op("dve", lambda: V.tensor_scalar(kz[:], qkr[:, 1, :], rconst[:, 128:129], None, ALU.mult), reads=[qkr, rconst], writes=[kz])
                    cx.op("pool", lambda: P.tensor_copy(out=vb[:], in_=R[:, 256:384]), reads=[R], writes=[vb])
                    psc, po, pc, pk = pmA
                    cx.op("pe", lambda: PE.matmul(psc[:], kTr[:], qTr[:], start=True, stop=True), reads=[kTr, qTr], writes=[psc])
                    cx.op("dve", lambda: V.tensor_tensor(ATb[:], psc[:], decT[:, 0:128], ALU.mult), reads=[psc, decT], writes=[ATb])
                    cx.op("pe", lambda: PE.matmul(pc[:], qTr[:], Rstb[:], start=True, stop=True), reads=[qTr, Rstb], writes=[pc])
                    cx.op("pe", lambda: PE.matmul(po[:], ATb[:], vb[:], start=True, stop=True), reads=[ATb, vb], writes=[po])
                    cx.op("pe", lambda: PE.matmul(pk[:], kz[:], vb[:], start=True, stop=True), reads=[kz, vb], writes=[pk])
                    cx.op("act", lambda: A.activation(out=o1[:], in_=pc[:], func=AF.Copy, scale=rconst[:, 129:130]),
                          reads=[pc, rconst], writes=[o1])
                    cx.op("dve", lambda: V.tensor_tensor(o2[:], o1[:], po[:], ALU.add), reads=[o1, po], writes=[o2])
                    cx.op("dve", lambda: V.scalar_tensor_tensor(Rst[:], Rst[:], rconst[:, 130:131], pk[:], ALU.mult, ALU.add),
                          reads=[Rst, rconst, pk], writes=[Rst])
                    cx.op("act", lambda: A.copy(out=Rstb[:], in_=Rst[:]), reads=[Rst], writes=[Rstb])
                    cx.op("act", lambda: A.activation(out=junk[:], in_=o2[:], func=AF.Square, accum_out=os1[:, 0:1]),
                          reads=[o2], writes=[junk, os1])
                    lnexp_rstd(rr, rr[:], os1, os1[:], 1.0 / 128)
                    cx.op("act", lambda: A.activation(out=sgt[:], in_=R[:, 384:512], func=AF.Silu), reads=[R], writes=[sgt])
                    cx.op("dve", lambda: V.scalar_tensor_tensor(o1[:], o2[:], rr[:, 0:1], gains[:, 256:384], ALU.mult, ALU.mult),
                          reads=[o2, rr, gains], writes=[o1])
                    cx.op("dve", lambda: V.tensor_tensor(ob[:], o1[:], sgt[:], ALU.mult), reads=[o1, sgt], writes=[ob])
                    cx.op("pe", lambda: PE.transpose(ptr[4][:], ob[:], ident[:, 0:128]), reads=[ob, ident], writes=[ptr[4]])
                    cx.op("act", lambda: A.copy(out=ob_stage[:, j * 128:(j + 1) * 128], in_=ptr[4][:]), reads=[ptr[4]], writes=[ob_stage])
                cx.mute = False
                cx.insec = False
                od_, oa_ = out_win(ti)
                cx.dma("sp", od_, oa_[128:256, :], ob_stage, ob_stage[:])
            cx.barrier()

        with contextlib.ExitStack() as esB:
          if do_b:
                cxb_sb = lambda name, shape, dt: T(esB.enter_context(nc.sbuf_tensor(cx.pfx + name, list(shape), dt)), cx.pfx + name)
                def cxb_ps(name, shape, dt=F32):
                    t = T(esB.enter_context(nc.psum_tensor(cx.pfx + name, list(shape), dt)), cx.pfx + name)
                    t.excl = True
                    return t
                Es = [[cxb_sb("E%d_%d" % (h, i), [128, 512], F32) for i in range(2)] for h in range(2)]
                SPs = [[cxb_sb("SP%d_%d" % (h, i), [128, 512], BF16) for i in range(3)] for h in range(2)]
                Xs = [[cxb_sb("X%d_%d" % (h, i), [128, 512], F32) for i in range(2)] for h in range(2)]
                Ws = [[cxb_sb("W%d_%d" % (h, i), [128, 512], BF16) for i in range(3)] for h in range(2)]
                oa = [[cxb_sb("oa%d_%d" % (h, i), [64, 512], BF16) for i in range(2)] for h in range(2)]
                pz = [[cxb_ps("pz%d_%d" % (h, i), [128, 512]) for i in range(2)] for h in range(2)]
                pcs = [cxb_ps("pcs%d" % h, [128, 512]) for h in range(2)]
                pout = [cxb_ps("pout%d" % h, [64, 512]) for h in range(2)]
                Umat = cmat
                for qs in range(NT):
                    kbs = list(range(4 * qs + 3, -1, -1))
                    nst = len(kbs)

                    def zmm(i):
                        kb = kbs[i]
                        for h in range(2):
                            hs = slice(h * 64, (h + 1) * 64)
                            p = pz[h][i % 2]
                            cx.op("pe", lambda: PE.matmul(p[:], kT[hs, kb * 128:(kb + 1) * 128], qT[hs, qs * 512:(qs + 1) * 512],
                                                          start=True, stop=True), reads=[kT, qT], writes=[p])
                    def emit_pv(j):
                        for h in range(2):
                            W = Ws[h][j % 3]
                            cx.op("pe", lambda: PE.matmul(pout[h][:], vres[:, kbs[j], h * 64:(h + 1) * 64], W[:], start=(j == 0), stop=(j == nst - 1)),
                                  reads=[vres, W], writes=[pout[h]], acc=(j > 0))

                    zmm(0)
                    for i in range(nst):
                        kb = kbs[i]
                        if i == 3 and qs > 0 and "after_out" in dr:
                            dr["after_out"](qs - 1)
                        for h in range(2):
                            E, SP, p = Es[h][i % 2], SPs[h][i % 3], pz[h][i % 2]
                            cx.op("act", lambda: A.activation(out=E[:], in_=p[:], func=AF.Exp), reads=[p], writes=[E])
                            if kb >= 4 * qs:
                                dm = kb - 4 * qs
                                cx.op("pool", lambda: P.tensor_tensor(E[:], E[:], dmask[:, dm * 512:(dm + 1) * 512], ALU.mult),
                                      reads=[E, dmask], writes=[E])
                            cx.op("act", lambda: A.activation(out=SP[:], in_=E[:], func=AF.Ln, bias=1.0), reads=[E], writes=[SP])
                        for h in range(2):
                            SP = SPs[h][i % 3]
                            cx.op("pe", lambda: PE.matmul(pcs[h][:], Umat[:, 128:256], SP[:], start=(i == 0), stop=False, skip_group_check=True),
                                  reads=[Umat, SP], writes=[pcs[h]], acc=(i > 0))
                            if i > 0:
                                SPp = SPs[h][(i - 1) % 3]
                                cx.op("pe", lambda: PE.matmul(pcs[h][:], Umat[:, 256:384], SPp[:], start=False, stop=False, skip_group_check=True),
                                      reads=[Umat, SPp], writes=[pcs[h]], acc=True)
                        if i + 1 < nst:
                            zmm(i + 1)
                        if i > 0:
                            emit_pv(i - 1)
                        for h in range(2):
                            X = Xs[h][i % 2]
                            cx.op("act", lambda: A.activation(out=X[:], in_=pcs[h][:], func=AF.Exp, scale=-1.0), reads=[pcs[h]], writes=[X])
                        for h in range(2):
                            X, W, E = Xs[h][i % 2], Ws[h][i % 3], Es[h][i % 2]
                            cx.op("dve", lambda: V.tensor_tensor(W[:], E[:], X[:], ALU.mult), reads=[E, X], writes=[W])
                    emit_pv(nst - 1)
                    for h in range(2):
                        o = oa[h][qs % 2]
                        cx.op("dve", lambda: V.tensor_copy(out=o[:], in_=pout[h][:]), reads=[pout[h]], writes=[o])
                        od_, oa_ = out_win(qs)
                        cx.dma("sp", od_, oa_[h * 64:(h + 1) * 64, :], o, o[:])
                    if qs == NT - 1 and "after_out" in dr:
                        dr["after_out"](qs)


def _v8(g):
    return np.ascontiguousarray(np.asarray(g, np.float32).reshape(8, 128).T)


def _const_tables(S):
    pos = np.arange(S, dtype=np.float32)
    inv = (np.float32(10000.0) ** (-(np.arange(64, dtype=np.float32) / np.float32(64)))).astype(np.float32)
    ang = (pos[:, None] * inv[None, :]).astype(np.float32).astype(np.float64)
    cos, sin = np.cos(ang).astype(np.float32), np.sin(ang).astype(np.float32)
    rope = np.ascontiguousarray(np.concatenate([cos, cos, sin, sin], axis=1))
    j = np.arange(128)
    ident = (j[:, None] == j[None, :])
    U = (j[:, None] >= j[None, :])
    L = (j[:, None] < j[None, :])
    cmat = np.concatenate([ident, U, L], axis=1).astype(np.float32).astype(NPBF)
    strict = (j[:, None] < j[None, :]).astype(np.float32)
    dmask = np.zeros((128, 4, 512), np.float32)
    for dm in range(4):
        for qb in range(4):
            if qb == dm:
                dmask[:, dm, qb * 128:(qb + 1) * 128] = strict
            elif qb > dm:
                dmask[:, dm, qb * 128:(qb + 1) * 128] = 1.0
    return rope, cmat, np.ascontiguousarray(dmask.reshape(128, 2048))


def _ret_consts(g):
    lg = np.log(np.float32(1.0) - np.exp2(np.float32(-5.0 - g))).astype(np.float64)
    j = np.arange(128, dtype=np.float64)
    sc = 128.0 ** -0.5
    diff = j[None, :] - j[:, None]
    decT = np.where(diff >= 0, np.exp(lg * np.maximum(diff, 0.0)), 0.0) * sc
    zeta = np.exp(lg * (127.0 - j)) * sc
    xi = np.exp(lg * (j + 1.0))
    gch = np.full(128, np.exp(lg * 128.0))
    return np.ascontiguousarray(np.concatenate([decT, zeta[:, None], xi[:, None], gch[:, None]], axis=1).astype(np.float32))


def _mix0_maps(xT_b, S, mix_g, w_in, q_gain, k_gain, ret_gain, tables):
    rope, cmat, dmask = tables
    xtl = [np.ascontiguousarray(xb.reshape(8, 128, S // 512, 512).transpose(2, 1, 0, 3).reshape((S // 512) * 128, 4096)) for xb in xT_b]
    maps = []
    for c in range(NCORES):
        b, g = c // 4, c % 4
        cols = np.concatenate([
            1536 + g * 128 + np.arange(128), 2048 + g * 128 + np.arange(128),
            2560 + g * 128 + np.arange(128), 3072 + g * 128 + np.arange(128),
            g * 128 + np.arange(128), 512 + g * 128 + np.arange(128), 1024 + g * 128 + np.arange(128)])
        wsel = np.ascontiguousarray(w_in[:, cols])
        gains = np.concatenate([q_gain, q_gain, k_gain, k_gain, ret_gain]).astype(np.float32)[None, :]
        maps.append(dict(xT=xtl[b], wsel=wsel, g_mix=_v8(mix_g), gains=np.ascontiguousarray(gains), rope=rope,
                         rconst=_ret_consts(g), cmat=cmat, dmask=dmask))
    return maps


def build_gla(S, oplimit=10 ** 9):
    nc = bass.Bass("TRN2", target_bir_lowering=False)
    with contextlib.ExitStack() as es:
        cx = Ctx(nc, es)
        cx.oplimit = oplimit
        dr = _gla_drams(cx, "")
        dr["hnT"] = mk_dram(cx, "hnT", [D, S], BF16, "ExternalInput")
        dr["outT"] = mk_dram(cx, "outT", [256, S], BF16, "ExternalOutput")
        cx.begin_phase("")
        emit_gla(cx, S, dr, gathered=False)
        cx.end_phase()
        cx.finish()
    return nc


def _gla_drams(cx, p):
    return dict(
        wsel=mk_dram(cx, p + "wsel", [D, 784], F32, "ExternalInput"),
        walpha=mk_dram(cx, p + "walpha", [17, 128], F32, "ExternalInput"),
        gain=mk_dram(cx, p + "gain", [1, 256], F32, "ExternalInput"),
        cmat=mk_dram(cx, p + "cmat", [128, 384], BF16, "ExternalInput"))


def emit_gla(cx, S, dr, gathered):
    nc = cx.nc
    NT = S // 512
    if True:
        hnT, wsel, walpha_d, gain_d, cmat_d, outT = dr.get("hnT"), dr["wsel"], dr["walpha"], dr["gain"], dr["cmat"], dr.get("outT")
        own = S // 4
        if gathered:
            hn_wins = dr["hn_wins"]
        else:
            h_ap = hnT.t.ap().rearrange("(c p) t -> p c t", p=128)
            hn_wins = lambda ti: [(hnT, h_ap[:, :, ti * 512:(ti + 1) * 512], 0, 512)]
        if "out_win" in dr:
            out_win = dr["out_win"]
        else:
            out_win = lambda ti: (outT, outT.t.ap()[:, ti * 512:(ti + 1) * 512])
        w_ap = wsel.t.ap().rearrange("(k p) n -> p k n", p=128)
        V, A, P, PE = nc.vector, nc.scalar, nc.gpsimd, nc.tensor

        wb = cx.sb("wb", [128, 8, 784], BF16)
        stg = [cx.sb("stg%d" % i, [128, 784], F32) for i in range(2)]
        wal_f = cx.sb("wal_f", [17, 128], F32)
        wal_b = cx.sb("wal_b", [17, 128], BF16)
        gain = cx.sb("gain_s", [128, 256], F32)
        cmat = cx.sb("cmat_s", [128, 384], BF16)
        trif = cx.sb("trif", [128, 128], F32)
        ones = cx.sb("ones", [128, 2], BF16)
        epst = cx.sb("epst", [128, 1], F32)
        hts = [cx.sb("ht%d" % i, [128, 8, 512], BF16) for i in range(2)]
        gaT = cx.sb("gaT", [32, 128], BF16)
        ee = cx.sb("ee", [128, 128], F32)
        sp = cx.sb("sp", [128, 128], F32)
        sph = cx.sb("sph", [128, 128], BF16)
        spl = cx.sb("spl", [128, 128], BF16)
        csum = cx.sb("csum", [128, 128], F32)
        dif = cx.sb("dif", [128, 128], F32)
        eq = cx.sb("eq", [128, 128], F32)
        ek = cx.sb("ek", [128, 128], F32)
        ekd = cx.sb("ekd", [128, 128], F32)
        qt = cx.sb("qt", [128, 128], BF16)
        kt = cx.sb("kt", [128, 128], BF16)
        kd = cx.sb("kd", [128, 128], BF16)
        vb = cx.sb("vb", [128, 256], BF16)
        sg = cx.sb("sg", [128, 256], F32)
        qTf = cx.sb("qTf", [128, 128], BF16)
        q0T = cx.sb("q0T", [128, 128], BF16)
        q1T = cx.sb("q1T", [128, 128], BF16)
        kTf = cx.sb("kTf", [128, 128], BF16)
        ATb = cx.sb("ATb", [128, 128], BF16)
        St = cx.sb("St", [128, 256], F32)
        Sb = [cx.sb("Sb%d" % i, [128, 256], BF16) for i in range(2)]
        dec = cx.sb("dec", [128, 2], F32)
        junk = cx.sb("junk", [128, 256], F32)
        os1 = cx.sb("os1", [128, 1], F32)
        rr = cx.sb("rr", [128, 1], F32)
        y1 = cx.sb("y1", [128, 256], F32)
        y2 = cx.sb("y2", [128, 256], BF16)
        outb = [cx.sb("outb%d" % i, [128, 2, 512], BF16) for i in range(2)]

        pA = [cx.ps("pA%d" % i, [128, 512]) for i in range(2)]
        pB_t = cx.ps("pBt", [128, 2, 256])
        pB = [T(pB_t.t, "pB%d" % i, (slice(None), i, slice(None)), st=pB_t) for i in range(2)]
        pm_t = cx.ps("pm", [128, 4, 128])
        py, pcum, ptot, paT = [T(pm_t.t, "pm%d" % i, (slice(None), i, slice(None)), st=pm_t) for i in range(4)]
        po = cx.ps("po", [128, 256])
        pk = cx.ps("pk", [128, 256])
        pm2_t = cx.ps("pm2", [128, 512])
        pga = T(pm2_t.t, "pga", (slice(0, 16), slice(0, 128)), st=pm2_t)
        pdec = T(pm2_t.t, "pdec", (slice(None), slice(128, 130)), st=pm2_t)
        ptr_t = cx.ps("ptr", [128, 8, 128], BF16)
        ptr = [T(ptr_t.t, "ptr%d" % i, (slice(None), i, slice(None)), st=ptr_t) for i in range(4)]

        cx.op("dve", lambda: V.memset(ones[:], 1.0), writes=[ones])
        cx.op("dve", lambda: V.memset(epst[:], EPS), writes=[epst])
        cx.op("dve", lambda: V.memset(gaT[:], 1.0), writes=[gaT])
        cx.op("dve", lambda: V.memset(q0T[:], 0.0), writes=[q0T])
        cx.op("dve", lambda: V.memset(q1T[:], 0.0), writes=[q1T])
        cx.op("dve", lambda: V.memset(St[:], 0.0), writes=[St])
        cx.op("dve", lambda: V.memset(Sb[0][:], 0.0), writes=[Sb[0]])
        cx.dma("sp", wal_f, wal_f[:], walpha_d, walpha_d.t.ap())
        cx.dma("sp", gain, gain[:], gain_d, gain_d.t.ap()[0:1, :].broadcast_to([128, 256]))
        cx.dma("sp", cmat, cmat[:], cmat_d, cmat_d.t.ap())
        cx.op("act", lambda: A.copy(out=wal_b[:], in_=wal_f[:]), reads=[wal_f], writes=[wal_b])
        cx.op("dve", lambda: V.tensor_copy(out=trif[:], in_=cmat[:, 128:256]), reads=[cmat], writes=[trif])
        for k in range(8):
            s = stg[k % 2]
            cx.dma("sp", s, s[:], wsel, w_ap[:, k, :])
            cx.op("act", lambda: A.copy(out=wb[:, k, :], in_=s[:]), reads=[s], writes=[wb])
        ident, tri, blk = cmat, cmat, cmat

        def lnexp_rstd(out_t, out_ap_, in_t, in_ap_, scale):
            cx.op("act", lambda: A.activation(out=out_ap_, in_=in_ap_, func=AF.Ln, scale=scale, bias=epst[:, 0:1]),
                  reads=[in_t, epst], writes=[out_t])
            cx.op("act", lambda: A.activation(out=out_ap_, in_=out_ap_, func=AF.Exp, scale=-0.5),
                  reads=[out_t], writes=[out_t])

        for ti in range(NT):
            ht = hts[ti % 2]
            for (hd_, ha_, ho_, hw_) in hn_wins(ti):
                cx.dma("sp", ht, ht[:, :, ho_:ho_ + hw_], hd_, ha_)
            ost = outb[ti % 2]
            for j in range(4):
                n = ti * 4 + j
                ts = slice(j * 128, (j + 1) * 128)
                a_, b_ = pA[n % 2], pB[n % 2]
                if j == 1 and ti > 0 and "after_out" in dr:
                    dr["after_out"](ti - 1)
                for k in range(8):
                    cx.op("pe", lambda: PE.matmul(a_[:], ht[:, k, ts], wb[:, k, 0:512], start=(k == 0), stop=(k == 7)),
                          reads=[ht, wb], writes=[a_], acc=(k > 0))
                for k in range(8):
                    cx.op("pe", lambda: PE.matmul(b_[:, 0:256], ht[:, k, ts], wb[:, k, 512:768], start=(k == 0), stop=(k == 7)),
                          reads=[ht, wb], writes=[b_], acc=(k > 0))
                for k in range(8):
                    cx.op("pe", lambda: PE.matmul(pga[:], wb[:, k, 768:784], ht[:, k, ts], start=(k == 0), stop=(k == 7)),
                          reads=[ht, wb], writes=[pga], acc=(k > 0))
                cx.op("act", lambda: A.copy(out=gaT[0:16, :], in_=pga[:]), reads=[pga], writes=[gaT])
                cx.op("pe", lambda: PE.matmul(py[:], gaT[0:17, :], wal_b[:, :], start=True, stop=True), reads=[gaT, wal_b], writes=[py])
                cx.op("act", lambda: A.activation(out=ee[:], in_=py[:], func=AF.Exp, scale=-1.0), reads=[py], writes=[ee])
                cx.op("act", lambda: A.activation(out=sp[:], in_=ee[:], func=AF.Ln, bias=1.0), reads=[ee], writes=[sp])
                cx.op("pool", lambda: P.tensor_copy(out=sph[:], in_=sp[:]), reads=[sp], writes=[sph])
                cx.op("dve", lambda: V.tensor_tensor(spl[:], sp[:], sph[:], ALU.subtract), reads=[sp, sph], writes=[spl])
                cx.op("pe", lambda: PE.matmul(pcum[:], tri[:, 128:256], sph[:], start=True, stop=False), reads=[tri, sph], writes=[pcum])
                cx.op("pe", lambda: PE.matmul(pcum[:], tri[:, 128:256], spl[:], start=False, stop=True), reads=[tri, spl], writes=[pcum], acc=True)
                cx.op("pe", lambda: PE.matmul(ptot[:], blk[:, 256:384], sph[:], start=True, stop=False), reads=[blk, sph], writes=[ptot])
                cx.op("pe", lambda: PE.matmul(ptot[:], blk[:, 256:384], spl[:], start=False, stop=True), reads=[blk, spl], writes=[ptot], acc=True)
                for c in range(2):
                    cs = slice(c * 64, (c + 1) * 64)
                    cx.op("pe", lambda: PE.matmul(pdec[:, c:c + 1], sph[cs, :], ones[cs, 0:1], start=True, stop=False),
                          reads=[sph, ones], writes=[pdec])
                    cx.op("pe", lambda: PE.matmul(pdec[:, c:c + 1], spl[cs, :], ones[cs, 0:1], start=False, stop=True),
                          reads=[spl, ones], writes=[pdec], acc=True)
                cx.op("act", lambda: A.copy(out=csum[:], in_=pcum[:]), reads=[pcum], writes=[csum])
                cx.op("dve", lambda: V.tensor_tensor(dif[:], csum[:], ptot[:], ALU.subtract), reads=[csum, ptot], writes=[dif])
                cx.op("act", lambda: A.activation(out=eq[:], in_=csum[:], func=AF.Exp, scale=-1.0 / 16), reads=[csum], writes=[eq])
                cx.op("act", lambda: A.activation(out=ek[:], in_=csum[:], func=AF.Exp, scale=1.0 / 16), reads=[csum], writes=[ek])
                cx.op("act", lambda: A.activation(out=ekd[:], in_=dif[:], func=AF.Exp, scale=1.0 / 16), reads=[dif], writes=[ekd])
                cx.op("act", lambda: A.activation(out=dec[:], in_=pdec[:], func=AF.Exp, scale=-1.0 / 16), reads=[pdec], writes=[dec])
                cx.op("dve", lambda: V.scalar_tensor_tensor(qt[:], a_[:, 0:128], 128.0 ** -0.5, eq[:], ALU.mult, ALU.mult),
                      reads=[a_, eq], writes=[qt])
                cx.op("dve", lambda: V.tensor_tensor(kt[:], a_[:, 128:256], ek[:], ALU.mult), reads=[a_, ek], writes=[kt])
                cx.op("dve", lambda: V.tensor_tensor(kd[:], a_[:, 128:256], ekd[:], ALU.mult), reads=[a_, ekd], writes=[kd])
                cx.op("act", lambda: A.copy(out=vb[:], in_=a_[:, 256:512]), reads=[a_], writes=[vb])
                cx.op("act", lambda: A.activation(out=sg[:], in_=b_[:, 0:256], func=AF.Silu), reads=[b_], writes=[sg])
                cx.op("pe", lambda: PE.transpose(ptr[0][:], qt[:], ident[:, 0:128]), reads=[qt, ident], writes=[ptr[0]])
                cx.op("pe", lambda: PE.transpose(ptr[1][:], kt[:], ident[:, 0:128]), reads=[kt, ident], writes=[ptr[1]])
                cx.op("act", lambda: A.copy(out=qTf[:], in_=ptr[0][:]), reads=[ptr[0]], writes=[qTf])
                cx.op("act", lambda: A.copy(out=kTf[:], in_=ptr[1][:]), reads=[ptr[1]], writes=[kTf])
                cx.op("pool", lambda: P.tensor_copy(out=q0T[:, 0:64], in_=qTf[:, 0:64]), reads=[qTf], writes=[q0T])
                cx.op("pool", lambda: P.tensor_copy(out=q1T[:, 64:128], in_=qTf[:, 64:128]), reads=[qTf], writes=[q1T])
                cx.op("pe", lambda: PE.matmul(paT[:], kTf[:], qTf[:], start=True, stop=True), reads=[kTf, qTf], writes=[paT])
                cx.op("dve", lambda: V.tensor_tensor(ATb[:], paT[:], trif[:], ALU.mult), reads=[paT, trif], writes=[ATb])
                for c in range(2):
                    cs = slice(c * 64, (c + 1) * 64)
                    if c == 0:
                        cx.op("pe", lambda: PE.matmul(po[:], ATb[:], vb[:], start=True, stop=False), reads=[ATb, vb], writes=[po])
                        cx.op("pe", lambda: PE.matmul(po[:], q0T[:], Sb[0][:], start=False, stop=False), reads=[q0T, Sb[0]], writes=[po], acc=True)
                    cx.op("pe", lambda: PE.matmul(pk[:], kd[cs, :], vb[cs, :], start=True, stop=True), reads=[kd, vb], writes=[pk])
                    cx.op("dve", lambda: V.scalar_tensor_tensor(St[:], St[:], dec[:, c:c + 1], pk[:], ALU.mult, ALU.add),
                          reads=[St, dec, pk], writes=[St])
                    nb = Sb[1 - c]
                    cx.op("act", lambda: A.copy(out=nb[:], in_=St[:]), reads=[St], writes=[nb])
                    if c == 0:
                        cx.op("pe", lambda: PE.matmul(po[:], q1T[:], Sb[1][:], start=False, stop=True), reads=[q1T, Sb[1]], writes=[po], acc=True)
                cx.op("act", lambda: A.activation(out=junk[:], in_=po[:], func=AF.Square, accum_out=os1[:, 0:1]), reads=[po], writes=[junk, os1])
                lnexp_rstd(rr, rr[:], os1, os1[:], 1.0 / 256)
                cx.op("dve", lambda: V.scalar_tensor_tensor(y1[:], po[:], rr[:, 0:1], gain[:], ALU.mult, ALU.mult), reads=[po, rr, gain], writes=[y1])
                cx.op("dve", lambda: V.tensor_tensor(y2[:], y1[:], sg[:], ALU.mult), reads=[y1, sg], writes=[y2])
                for ec in range(2):
                    cx.op("pe", lambda: PE.transpose(ptr[2 + ec][:], y2[:, ec * 128:(ec + 1) * 128], ident[:, 0:128]), reads=[y2, ident], writes=[ptr[2 + ec]])
                    cx.op("act", lambda: A.copy(out=ost[:, ec, ts], in_=ptr[2 + ec][:]), reads=[ptr[2 + ec]], writes=[ost])
            od_, oa_ = out_win(ti)
            cx.dma("sp", od_, oa_.rearrange("(c p) t -> p c t", p=128), ost, ost[:])
            if ti == NT - 1 and "after_out" in dr:
                dr["after_out"](ti)


def _gla_tables():
    j = np.arange(128)
    same = (j[:, None] // 64) == (j[None, :] // 64)
    ident = (j[:, None] == j[None, :])
    tri = same & (j[:, None] <= j[None, :])
    return np.concatenate([ident, tri, same], axis=1).astype(np.float32).astype(NPBF)


def _gla_maps(hnT_b, S, w_in, w_alpha, b_alpha, out_gain):
    cm = _gla_tables()
    maps = []
    for c in range(NCORES):
        b, g = c // 4, c % 4
        cols = np.concatenate([g * 128 + np.arange(128), 512 + g * 128 + np.arange(128),
                               1024 + g * 256 + np.arange(256), 2048 + g * 256 + np.arange(256), 3072 + np.arange(16)])
        wsel = np.ascontiguousarray(w_in[:, cols])
        wal = np.ascontiguousarray(np.concatenate([w_alpha[:, g * 128:(g + 1) * 128], b_alpha[None, g * 128:(g + 1) * 128]], axis=0).astype(np.float32))
        m = dict(wsel=wsel, walpha=wal, gain=np.ascontiguousarray(out_gain[None, :].astype(np.float32)), cmat=cm)
        if hnT_b[b] is not None:
            m["hnT"] = hnT_b[b]
        maps.append(m)
    return maps


_CACHE = {}


def _get(name, fn):
    if name not in _CACHE:
        _CACHE[name] = fn()
    return _CACHE[name]


def _tail_maps(hT_b, catT_b, own, w_out, g_ffn, g_next, w_up, conv_w, conv_b, w_down):
    cw = np.ascontiguousarray(np.asarray(conv_w, np.float32).reshape(3, NFF, 128).transpose(2, 1, 0).reshape(128, NFF * 3))
    cb = np.ascontiguousarray(np.asarray(conv_b, np.float32).reshape(NFF, 128).T)
    w_up = np.ascontiguousarray(np.asarray(w_up, np.float32).reshape(8, 128, 2, NFF, 128).transpose(3, 1, 0, 2, 4).reshape(NFF * 128, 2048))
    maps = []
    for c in range(NCORES):
        b, gi = c // 4, c % 4
        lo, hi = gi * own - 2, (gi + 1) * own

        def halo_slice(src, dt):
            if src is None:
                return None
            if lo < 0:
                return np.ascontiguousarray(np.concatenate([np.zeros((D, 2), dt), src[b][:, 0:hi]], axis=1))
            return np.ascontiguousarray(src[b][:, lo:hi])
        m = dict(w_out=w_out, w_up=w_up, w_down=w_down, g_ffn=_v8(g_ffn), g_next=_v8(g_next), conv_w=cw, conv_b=cb)
        hin, cat = halo_slice(hT_b, np.float32), halo_slice(catT_b, NPBF)
        if hin is not None:
            m["hin"] = hin
        if cat is not None:
            m["cat"] = cat
        maps.append(m)
    return maps


def kernel_unfused(x, mix_norm_g, even_w_in, sb_q_gain, sb_k_gain, ret_out_gain, even_w_out,
           odd_w_in, gla_w_alpha, gla_b_alpha, gla_out_gain, odd_w_out,
           ffn_norm_g, ffn_w_up, ffn_conv_w, ffn_conv_b, ffn_w_down):
    f = lambda a: np.ascontiguousarray(np.asarray(a, dtype=np.float32))
    x = f(x)
    B, S, _ = x.shape
    own = S // 4
    cores = list(range(NCORES))
    xT = [np.ascontiguousarray(x[b].T) for b in range(B)]

    nc1 = _get(("mix0", S), lambda: build_mix0(S))
    maps1 = _mix0_maps(xT, S, f(mix_norm_g)[0], f(even_w_in)[0], f(sb_q_gain)[0], f(sb_k_gain)[0], f(ret_out_gain)[0],
                       _const_tables(S))
    r1 = run_bass_kernel_spmd(nc1, maps1, core_ids=cores).results
    catT = [np.zeros((D, S), NPBF) for _ in range(B)]
    for c in cores:
        b, g = c // 4, c % 4
        o = np.asarray(r1[c]["outT"])
        catT[b][g * 128:(g + 1) * 128] = o[0:128]
        catT[b][512 + g * 128:512 + (g + 1) * 128] = o[128:256]

    nc2 = _get(("tail", own), lambda: build_tail(own, True))
    maps2 = _tail_maps(xT, catT, own, f(even_w_out)[0], f(ffn_norm_g)[0], f(mix_norm_g)[1], f(ffn_w_up)[0],
                       f(ffn_conv_w)[0], f(ffn_conv_b)[0], f(ffn_w_down)[0])
    r2 = run_bass_kernel_spmd(nc2, maps2, core_ids=cores).results
    h1T = [np.concatenate([np.asarray(r2[b * 4 + gi]["hout"]) for gi in range(4)], axis=1) for b in range(B)]
    hnT = [np.ascontiguousarray(np.concatenate([np.asarray(r2[b * 4 + gi]["hn"]) for gi in range(4)], axis=1)) for b in range(B)]

    nc3 = _get(("gla", S), lambda: build_gla(S))
    maps3 = _gla_maps(hnT, S, f(odd_w_in)[0], f(gla_w_alpha)[0], f(gla_b_alpha)[0], f(gla_out_gain)[0])
    r3 = run_bass_kernel_spmd(nc3, maps3, core_ids=cores).results
    cat2T = [np.zeros((D, S), NPBF) for _ in range(B)]
    for c in cores:
        b, g = c // 4, c % 4
        cat2T[b][g * 256:(g + 1) * 256] = np.asarray(r3[c]["outT"])

    maps4 = _tail_maps(h1T, cat2T, own, f(odd_w_out)[0], f(ffn_norm_g)[1], f(mix_norm_g)[1], f(ffn_w_up)[1],
                       f(ffn_conv_w)[1], f(ffn_conv_b)[1], f(ffn_w_down)[1])
    r4 = run_bass_kernel_spmd(nc2, maps4, core_ids=cores).results
    out = np.empty((B, S, D), np.float32)
    for b in range(B):
        oT = np.concatenate([np.asarray(r4[b * 4 + gi]["hout"]) for gi in range(4)], axis=1)
        out[b] = oT.T
    return out


def build_fused(S):
    own = S // 4
    nc = bass.Bass("TRN2", target_bir_lowering=False)
    groups = [[0, 1, 2, 3], [4, 5, 6, 7]]
    W1 = min(S, 1024)
    W2 = min(own, 256)
    with contextlib.ExitStack() as es:
        cx = Ctx(nc, es)
        m0 = _mix0_drams(cx, S, "m0_")
        t0 = _tail_drams(cx, "t0_")
        gl = _gla_drams(cx, "gl_")
        t1 = _tail_drams(cx, "t1_")
        xown = mk_dram(cx, "xown", [D, own + 2], F32, "ExternalInput")
        sel = mk_dram(cx, "sel", [128, 4], F32, "ExternalInput")
        outT = mk_dram(cx, "outT", [D, own], F32, "ExternalOutput")
        o1 = Chunked(cx, "i_o1", 256, S, W1, BF16)
        G1 = Chunked(cx, "i_G1", 1024, S, W1, BF16)
        h1d = mk_dram(cx, "i_h1d", [D, own], F32, "Internal")
        hnd = Chunked(cx, "i_hnd", D, own, W2, BF16)
        G2 = Chunked(cx, "i_G2", 4 * D, own, W2, BF16)
        hl = mk_dram(cx, "i_hl", [D, 2], F32, "Internal")
        H2 = mk_dram(cx, "i_H2", [4 * D, 2], F32, "Internal")
        o3 = Chunked(cx, "i_o3", 256, S, W1, BF16)
        G3 = Chunked(cx, "i_G3", 1024, S, W1, BF16)

        def chunk_hook(src, dst, W, key):
            def hook(ti):
                for k in range(src.n):
                    if (k + 1) * W <= (ti + 1) * 512 and k not in hook.done:
                        hook.done.add(k)
                        cx.coll_allgather(src.ch[k], dst.ch[k], groups, key=key)
            hook.done = set()
            return hook

        cx.begin_phase("m0_")
        m0["out_win"] = lambda ti: o1.win(ti * 512, 512)
        m0["after_out"] = chunk_hook(o1, G1, W1, "cc1")
        emit_mix0(cx, S, m0)
        cx.end_phase()
        assert len(m0["after_out"].done) == o1.n

        cx.begin_phase("t0_")
        t0.update(hin=xown, cat_win=G1.win, sel=sel, hout=h1d, hn=hnd.ch[0], hlast=hl,
                  hn_outs=lambda ti: hnd.wins(ti * 512, 512))
        t0["after_hn"] = chunk_hook(hnd, G2, W2, "cc2")
        emit_tail(cx, own, t0, True, fused=True)
        cx.coll_allgather(hl, H2, groups, key="cc2")
        cx.end_phase()
        assert len(t0["after_hn"].done) == hnd.n

        def hn_wins(ti):
            r = ti // (own // 512)
            c0 = (ti % (own // 512)) * 512
            res = []
            for (d_, a_, o_, w_) in G2.wins(c0, 512):
                res.append((d_, a_.rearrange("(r c p) t -> r p c t", r=4, p=128)[r], o_, w_))
            return res

        cx.begin_phase("gl_")
        gl.update(hn_wins=hn_wins, out_win=lambda ti: o3.win(ti * 512, 512))
        gl["after_out"] = chunk_hook(o3, G3, W1, "cc3")
        emit_gla(cx, S, gl, gathered=True)
        cx.end_phase()
        assert len(gl["after_out"].done) == o3.n

        cx.begin_phase("t1_")
        t1.update(hin=h1d, hin_halo=H2, cat_win=G3.win, sel=sel, hout=outT, hn=None)
        emit_tail(cx, own, t1, False, fused=True)
        cx.end_phase()
        cx.finish()
        print("fused: sems", len(cx.sems), "dmas", cx.ndma, "ops", getattr(cx, "nops", 0))
    return nc


def _fused_maps(x, mix_norm_g, even_w_in, sb_q_gain, sb_k_gain, ret_out_gain, even_w_out,
                odd_w_in, gla_w_alpha, gla_b_alpha, gla_out_gain, odd_w_out,
                ffn_norm_g, ffn_w_up, ffn_conv_w, ffn_conv_b, ffn_w_down):
    B, S, _ = x.shape
    own = S // 4
    xT = [np.ascontiguousarray(x[b].T) for b in range(B)]
    m0 = _mix0_maps(xT, S, mix_norm_g[0], even_w_in[0], sb_q_gain[0], sb_k_gain[0], ret_out_gain[0], _const_tables(S))
    perm = np.concatenate([np.concatenate([g * 128 + np.arange(128), 512 + g * 128 + np.arange(128)]) for g in range(4)])
    w_out0 = np.ascontiguousarray(even_w_out[0][perm])
    dummy_cat = [None] * B
    tl0 = _tail_maps(xT, None, own, w_out0, ffn_norm_g[0], mix_norm_g[1], ffn_w_up[0], ffn_conv_w[0], ffn_conv_b[0], ffn_w_down[0])
    tl1 = _tail_maps(None, None, own, np.ascontiguousarray(odd_w_out[0]), ffn_norm_g[1], mix_norm_g[1], ffn_w_up[1], ffn_conv_w[1],
                     ffn_conv_b[1], ffn_w_down[1])
    gm = _gla_maps([None] * B, S, odd_w_in[0], gla_w_alpha[0], gla_b_alpha[0], gla_out_gain[0])
    maps = []
    for c in range(NCORES):
        gi = c % 4
        m = {}
        for k, v in m0[c].items():
            m["m0_" + k] = v
        for k, v in tl0[c].items():
            if k == "hin":
                m["xown"] = v
            elif k != "cat":
                m["t0_" + k] = v
        for k, v in tl1[c].items():
            if k not in ("hin", "cat"):
                m["t1_" + k] = v
        for k, v in gm[c].items():
            if k != "hnT":
                m["gl_" + k] = v
        selv = np.zeros((128, 4), np.float32)
        selv[:, gi] = 1.0
        m["sel"] = selv
        maps.append(m)
    return maps


def kernel(x, mix_norm_g, even_w_in, sb_q_gain, sb_k_gain, ret_out_gain, even_w_out,
           odd_w_in, gla_w_alpha, gla_b_alpha, gla_out_gain, odd_w_out,
           ffn_norm_g, ffn_w_up, ffn_conv_w, ffn_conv_b, ffn_w_down):
    f = lambda a: np.ascontiguousarray(np.asarray(a, dtype=np.float32))
    args = [f(a) for a in (x, mix_norm_g, even_w_in, sb_q_gain, sb_k_gain, ret_out_gain, even_w_out,
                           odd_w_in, gla_w_alpha, gla_b_alpha, gla_out_gain, odd_w_out,
                           ffn_norm_g, ffn_w_up, ffn_conv_w, ffn_conv_b, ffn_w_down)]
    B, S, _ = args[0].shape
    nc = _get(("fused", S), lambda: build_fused(S))
    maps = _fused_maps(*args)
    res = run_bass_kernel_spmd(nc, maps, core_ids=list(range(NCORES))).results
    out = np.empty((B, S, D), np.float32)
    for b in range(B):
        oT = np.concatenate([np.asarray(res[b * 4 + gi]["outT"]) for gi in range(4)], axis=1)
        out[b] = oT.T
    return out
```

```python
import contextlib
import numpy as np
import ml_dtypes
import concourse.bass as bass
import concourse.mybir as mybir
from concourse.bass_utils import run_bass_kernel_spmd

F32 = mybir.dt.float32
BF16 = mybir.dt.bfloat16
AF = mybir.ActivationFunctionType
ALU = mybir.AluOpType
AX = mybir.AxisListType
NPBF = ml_dtypes.bfloat16

D = 1024
DFF = 2816
NFF = DFF // 128
EPS = 1e-6
NCORES = 8


class T:
    def __init__(self, t, name, sl=None, st=None):
        self.t = t if sl is None else t[sl]
        self.name = name
        self.st = st
        self.w = None
        self.r = {}
        self.dsem = None
        self.wd = {}

    def __getitem__(self, k):
        return self.t[k]


class RT:
    def __init__(self, ts):
        object.__setattr__(self, "_ts", ts)
        object.__setattr__(self, "_i", 0)

    def sel(self, i):
        object.__setattr__(self, "_i", i % len(self._ts))

    def __getattr__(self, k):
        return getattr(self._ts[self._i], k)

    def __setattr__(self, k, v):
        setattr(self._ts[self._i], k, v)

    def __getitem__(self, k):
        return self._ts[self._i][k]


class Ctx:
    def __init__(self, nc, es):
        self.nc = nc
        self.es = es
        self.sems = {}
        self.issued = {}
        self.isdma = set()
        self.eng = {"pe": nc.tensor, "act": nc.scalar, "dve": nc.vector, "pool": nc.gpsimd, "sp": nc.sync}
        self.waited = {e: {} for e in self.eng}
        for e in self.eng:
            self._mksem("e_" + e)
        self.ndma = 0
        self.pfx = ""
        self.es_phase = es
        self.ncc = 0

    def begin_phase(self, pfx):
        self.pfx = pfx
        self.es_phase = contextlib.ExitStack()
        self.es_phase.__enter__()

    def end_phase(self):
        self.barrier()
        self.es_phase.__exit__(None, None, None)
        self.es_phase = self.es

    def coll_allgather(self, in_t, out_t, groups, key="cc"):
        if key not in self.sems:
            self._mksem(key, dma=True)
        self._wait("pool", [in_t], [out_t])
        ins = self.nc.gpsimd.collective_compute("AllGather", ALU.bypass, replica_groups=groups,
                                                ins=[in_t.t.ap()], outs=[out_t.t.ap()])
        ins.then_inc(self.sems[key], 1)
        self.issued[key] += 1
        in_t.r[key] = self.issued[key]
        out_t.w = (key, self.issued[key])
        out_t.r = {}
        return ins

    def _mksem(self, key, dma=False):
        self.sems[key] = self.es.enter_context(self.nc.semaphore(key))
        self.issued[key] = 0
        if dma:
            self.isdma.add(key)
        return key

    def sb(self, name, shape, dt):
        name = self.pfx + name
        return T(self.es_phase.enter_context(self.nc.sbuf_tensor(name, list(shape), dt)), name)

    def ps(self, name, shape, dt=F32):
        name = self.pfx + name
        t = T(self.es_phase.enter_context(self.nc.psum_tensor(name, list(shape), dt)), name)
        t.excl = True
        return t

    def dram(self, name, shape, dt, kind):
        return T(self.nc.dram_tensor(name, list(shape), dt, kind=kind), name)

    def _wait(self, e, reads, writes, skip_self_waw=False):
        deps = {}

        def add(tok):
            if tok is None:
                return
            k, v = tok
            if deps.get(k, 0) < v:
                deps[k] = v

        reads = [t.st or t for t in reads]
        writes = [t.st or t for t in writes]
        for t in reads:
            add(t.w)
            for k_, v_ in t.wd.items():
                add((k_, v_))
            if getattr(t, "excl", False):
                for k, v in t.r.items():
                    if k != "e_" + e:
                        add((k, v))
        for t in writes:
            if not (skip_self_waw and t.w is not None and t.w[0] == "e_" + e):
                add(t.w)
            for k, v in t.r.items():
                add((k, v))
        w = self.waited[e]
        for k, v in deps.items():
            if k in self.isdma:
                v = self.issued[k]
            if w.get(k, 0) < v:
                self.eng[e].wait_ge(self.sems[k], v)
                w[k] = v

    def op(self, e, fn, reads=(), writes=(), acc=False):
        if getattr(self, "mute", False):
            return None
        self.nops = getattr(self, "nops", 0) + 1
        if self.nops > getattr(self, "oplimit", 10 ** 9):
            return None
        if getattr(self, "insec", False):
            self.seccount = getattr(self, "seccount", 0) + 1
            if self.seccount > getattr(self, "seclimit", 10 ** 9):
                return None
        self._wait(e, reads, writes, skip_self_waw=acc)
        ins = fn()
        k = "e_" + e
        self.issued[k] += 1
        ins.then_inc(self.sems[k], 1)
        tok = (k, self.issued[k])
        for t in reads:
            t = t.st or t
            if t.r.get(k, 0) < tok[1]:
                t.r[k] = tok[1]
        for t in writes:
            t = t.st or t
            t.w = tok
            t.r = {}
        return ins

    def dma(self, q, out_t, out_ap, in_t, in_ap, sbt=None):
        if sbt is None:
            sbt = out_t if not isinstance(out_t, DT) else in_t
        if sbt.dsem is None:
            sbt.dsem = self._mksem("d_%s" % sbt.name, dma=True)
        to_dram = isinstance(out_t, DT)
        if to_dram:
            self._wait(q, [in_t], [])
            w_ = self.waited[q]
            for k_, v_ in list(out_t.r.items()):
                if k_ in self.isdma:
                    v_ = self.issued[k_]
                if w_.get(k_, 0) < v_:
                    self.eng[q].wait_ge(self.sems[k_], v_)
                    w_[k_] = v_
        else:
            self._wait(q, [in_t], [out_t])
        ins = self.eng[q].dma_start(out=out_ap, in_=in_ap)
        k = sbt.dsem
        self.issued[k] += 16
        ins.then_inc(self.sems[k], 16)
        tok = (k, self.issued[k])
        (in_t.st or in_t).r[k] = tok[1]
        if to_dram:
            out_t.wd[k] = tok[1]
            out_t.w = tok
        else:
            (out_t.st or out_t).w = tok
            (out_t.st or out_t).r = {}
        self.ndma += 1
        return ins

    def view(self, base, name, sl):
        return T(base.t, name, sl)

    def barrier(self):
        for e in self.eng:
            for k, v in self.issued.items():
                if v > 0 and k != "e_" + e and self.waited[e].get(k, 0) < v:
                    self.eng[e].wait_ge(self.sems[k], v)
                    self.waited[e][k] = v

    def finish(self):
        for k, v in self.issued.items():
            if v > 0 and self.waited["sp"].get(k, 0) < v:
                self.nc.sync.wait_ge(self.sems[k], v)
                self.waited["sp"][k] = v


class DT(T):
    pass


class Chunked:
    def __init__(self, cx, name, R, total, W, dt):
        self.W = W
        self.n = total // W
        self.ch = [mk_dram(cx, "%s_%d" % (name, k), [R, W], dt, "Internal") for k in range(self.n)]

    def win(self, t0, width):
        k = t0 // self.W
        c0 = t0 - k * self.W
        assert c0 + width <= self.W
        return self.ch[k], self.ch[k].t.ap()[:, c0:c0 + width]

    def wins(self, t0, width):
        out = []
        t = t0
        while t < t0 + width:
            k = t // self.W
            c0 = t - k * self.W
            w = min(self.W - c0, t0 + width - t)
            out.append((self.ch[k], self.ch[k].t.ap()[:, c0:c0 + w], t - t0, w))
            t += w
        return out


def mk_dram(cx, name, shape, dt, kind):
    return DT(cx.nc.dram_tensor(name, list(shape), dt, kind=kind), name)


def build_tail(own, want_hn):
    nc = bass.Bass("TRN2", target_bir_lowering=False)
    with contextlib.ExitStack() as es:
        cx = Ctx(nc, es)
        dr = _tail_drams(cx, "")
        dr["hin"] = mk_dram(cx, "hin", [D, own + 2], F32, "ExternalInput")
        dr["cat"] = mk_dram(cx, "cat", [D, own + 2], BF16, "ExternalInput")
        dr["hout"] = mk_dram(cx, "hout", [D, own], F32, "ExternalOutput")
        dr["hn"] = mk_dram(cx, "hn", [D, own], BF16, "ExternalOutput")
        cx.begin_phase("")
        emit_tail(cx, own, dr, want_hn, fused=False)
        cx.end_phase()
        cx.finish()
    return nc


def _tail_drams(cx, p):
    return dict(
        w_out=mk_dram(cx, p + "w_out", [D, D], F32, "ExternalInput"),
        w_up=mk_dram(cx, p + "w_up", [NFF * 128, 2048], F32, "ExternalInput"),
        w_down=mk_dram(cx, p + "w_down", [DFF, D], F32, "ExternalInput"),
        g_ffn=mk_dram(cx, p + "g_ffn", [128, 8], F32, "ExternalInput"),
        g_next=mk_dram(cx, p + "g_next", [128, 8], F32, "ExternalInput"),
        conv_w=mk_dram(cx, p + "conv_w", [128, NFF * 3], F32, "ExternalInput"),
        conv_b=mk_dram(cx, p + "conv_b", [128, NFF], F32, "ExternalInput"))


def emit_tail(cx, own, dr, want_hn, fused):
    nc = cx.nc
    NT = own // 512
    if True:
        hin, cat, w_out, w_up, w_down = dr["hin"], dr.get("cat"), dr["w_out"], dr["w_up"], dr["w_down"]
        gff_d, gnx_d, cw_d, cb_d, hout, hnout = dr["g_ffn"], dr["g_next"], dr["conv_w"], dr["conv_b"], dr["hout"], dr.get("hn")
        hin_halo = dr.get("hin_halo")
        hlast = dr.get("hlast")
        hin_ap = hin.t.ap().rearrange("(c p) t -> p c t", p=128)
        cat_ap = cat.t.ap().rearrange("(c p) t -> p c t", p=128) if cat is not None else None
        cat_win = dr.get("cat_win")
        hn_outs = dr.get("hn_outs")
        hout_ap = hout.t.ap().rearrange("(c p) t -> p c t", p=128)
        hn_ap = hnout.t.ap().rearrange("(c p) t -> p c t", p=128) if hnout is not None else None
        wout_ap = w_out.t.ap().rearrange("(k p) n -> p k n", p=128)
        wup_ap = w_up.t.ap().rearrange("(c p) f -> c p f", p=128)
        wdown_ap = w_down.t.ap().rearrange("(k p) n -> p k n", p=128)
        hoff = 2 if hin_halo is None else 0

        wout_b = cx.sb("wout_b", [128, 8, 1024], BF16)
        wdown_b = cx.sb("wdown_b", [128, NFF, 1024], BF16)
        stg = [cx.sb("stg%d" % i, [128, 8, 256], F32) for i in range(2)]
        wup_b = [cx.sb("wup_b%d" % i, [128, 8, 256], BF16) for i in range(3)]
        h1s = [cx.sb("h1_%d" % i, [128, 8, 512], F32) for i in range(2)]
        catb = cx.sb("catb", [128, 8, 512], BF16)
        hb = cx.sb("hb", [128, 8, 512], BF16)
        sqs = [cx.sb("sq%d" % i, [128, 512], BF16) for i in range(2)]
        rstd = cx.sb("rstd", [128, 512], F32)
        rstd2 = cx.sb("rstd2", [128, 512], F32)
        Us = [cx.sb("U%d" % i, [128, 514], F32) for i in range(2)]
        t1s = [cx.sb("t1_%d" % i, [128, 512], F32) for i in range(2)]
        t2s = [cx.sb("t2_%d" % i, [128, 512], F32) for i in range(2)]
        sgs = [cx.sb("sg%d" % i, [128, 512], F32) for i in range(2)]
        actb = cx.sb("actb", [128, NFF, 512], BF16)
        tmpo = [cx.sb("tmpo%d" % i, [128, 512], F32) for i in range(2)]
        hnb = hb
        carry = cx.sb("carry", [128, NFF, 2], F32)
        h1h = cx.sb("h1h", [128, 8, 2], F32)
        cath = cx.sb("cath", [128, 8, 2], BF16)
        hbh = cx.sb("hbh", [128, 8, 2], BF16)
        rstdh = cx.sb("rstdh", [128, 2], F32)
        gff = cx.sb("gff", [128, 8], F32)
        gnx = cx.sb("gnx", [128, 8], F32)
        cw = cx.sb("cw", [128, NFF * 3], F32)
        cb = cx.sb("cb", [128, NFF], F32)
        ones = cx.sb("ones", [128, 128], BF16)

        pms = [cx.ps("pm%d" % i, [128, 512]) for i in range(2)]
        pssq = cx.ps("pssq", [128, 512])
        pus = [cx.ps("pu%d" % i, [128, 512]) for i in range(2)]
        pvs = [cx.ps("pv%d" % i, [128, 512]) for i in range(2)]
        puh = cx.ps("puh", [128, 512])

        V, A, P, PE = nc.vector, nc.scalar, nc.gpsimd, nc.tensor

        cx.op("dve", lambda: V.memset(ones[:], 1.0), writes=[ones])
        epst = cx.sb("epst", [128, 1], F32)
        cx.op("dve", lambda: V.memset(epst[:], EPS), writes=[epst])
        for dst, src in ((gff, gff_d), (gnx, gnx_d), (cw, cw_d), (cb, cb_d)):
            cx.dma("sp", dst, dst[:], src, src.t.ap())

        def load_wup(gidx):
            c = gidx % NFF
            s = stg[gidx % 2]
            cx.dma("sp", s, s[:].rearrange("p a b -> p (a b)"), w_up, wup_ap[c])
            wb = wup_b[gidx % 3]
            cx.op("pool", lambda: P.tensor_copy(out=wb[:], in_=s[:]), reads=[s], writes=[wb])

        for k in range(8):
            s = stg[k % 2]
            sv = s[:].rearrange("p a b -> p (a b)")
            cx.dma("sp", s, sv[:, 0:1024], w_out, wout_ap[:, k, :])
            cx.op("act", lambda: A.copy(out=wout_b[:, k, :], in_=sv[:, 0:1024]), reads=[s], writes=[wout_b])
        for c in range(NFF):
            s = stg[c % 2]
            sv = s[:].rearrange("p a b -> p (a b)")
            cx.dma("sp", s, sv[:, 0:1024], w_down, wdown_ap[:, c, :])
            cx.op("act", lambda: A.copy(out=wdown_b[:, c, :], in_=sv[:, 0:1024]), reads=[s], writes=[wdown_b])

        def wout_stage(h1, catt, n):
            for nch in range(8):
                pm = pms[nch % 2]
                for k in range(8):
                    cx.op("pe", lambda: PE.matmul(pm[:, 0:n], wout_b[:, k, nch * 128:(nch + 1) * 128], catt[:, k, 0:n],
                                                  start=(k == 0), stop=(k == 7)),
                          reads=[wout_b, catt], writes=[pm], acc=(k > 0))
                cx.op("dve", lambda: V.tensor_tensor(h1[:, nch, 0:n], h1[:, nch, 0:n], pm[:, 0:n], ALU.add),
                      reads=[pm, h1], writes=[h1])

        def norm_stage(h1, n, g, hbt, rs):
            for k in range(8):
                sq = sqs[k % 2]
                cx.op("act", lambda: A.activation(out=sq[:, 0:n], in_=h1[:, k, 0:n], func=AF.Square),
                      reads=[h1], writes=[sq])
                cx.op("pe", lambda: PE.matmul(pssq[:, 0:n], ones[:, :], sq[:, 0:n], start=(k == 0), stop=(k == 7)),
                      reads=[sq, ones], writes=[pssq], acc=(k > 0))
                if hbt is not None:
                    cx.op("dve", lambda: V.tensor_scalar(hbt[:, k, 0:n], h1[:, k, 0:n], g[:, k:k + 1], None, ALU.mult),
                          reads=[h1, g], writes=[hbt])
            cx.op("act", lambda: A.activation(out=rs[:, 0:n], in_=pssq[:, 0:n], func=AF.Ln, scale=1.0 / D, bias=epst[:, 0:1]),
                  reads=[pssq, epst], writes=[rs])
            cx.op("act", lambda: A.activation(out=rs[:, 0:n], in_=rs[:, 0:n], func=AF.Exp, scale=-0.5),
                  reads=[rs], writes=[rs])

        V, A = nc.vector, nc.scalar
        if fused:
            selt = cx.sb("selt", [128, 4], F32)
            cx.dma("sp", selt, selt[:], dr["sel"], dr["sel"].t.ap())
            cands = [cx.sb("cand%d" % i, [128, 8, 512], BF16) for i in range(2)]
            candh = [cx.sb("candh%d" % i, [128, 8, 2], BF16) for i in range(2)]
            candf = [cx.sb("candf%d" % i, [128, 8, 2], F32) for i in range(2)]

        def blend(dst, srcs, width, bufs, src_t):
            first = True
            for n_, (j, (src_t, ap_)) in enumerate(srcs):
                cb_ = bufs[n_ % 2]
                cx.dma("sp", cb_, cb_[:, :, 0:width], src_t, ap_)
                if first:
                    cx.op("act", lambda: A.activation(out=dst[:, :, 0:width], in_=cb_[:, :, 0:width], func=AF.Copy, scale=selt[:, j:j + 1]),
                          reads=[cb_, selt], writes=[dst])
                    first = False
                else:
                    cx.op("dve", lambda: V.scalar_tensor_tensor(dst[:, :, 0:width], cb_[:, :, 0:width], selt[:, j:j + 1], dst[:, :, 0:width],
                                                                ALU.mult, ALU.add),
                          reads=[cb_, selt, dst], writes=[dst])

        def catw(t0, width):
            d_, a_ = cat_win(t0, width)
            return d_, a_.rearrange("(c p) t -> p c t", p=128)

        def load_tile(ti, h1):
            cx.dma("sp", h1, h1[:], hin, hin_ap[:, :, hoff + ti * 512:hoff + (ti + 1) * 512])
            if not fused:
                cx.dma("sp", catb, catb[:], cat, cat_ap[:, :, 2 + ti * 512:2 + (ti + 1) * 512])
            else:
                blend(catb, [(j, catw(j * own + ti * 512, 512)) for j in range(4)], 512, cands, None)

        if not fused:
            cx.dma("sp", h1h, h1h[:], hin, hin_ap[:, :, 0:2])
            cx.dma("sp", cath, cath[:], cat, cat_ap[:, :, 0:2])
        else:
            blend(cath, [(j, catw(j * own - 2, 2)) for j in range(1, 4)], 2, candh, None)
            if hin_halo is None:
                cx.dma("sp", h1h, h1h[:], hin, hin_ap[:, :, 0:2])
            else:
                hh_ap = hin_halo.t.ap().rearrange("(r c p) t -> r p c t", r=4, p=128)
                blend(h1h, [(r + 1, (hin_halo, hh_ap[r])) for r in range(3)], 2, candf, None)
        wout_stage(h1h, cath, 2)
        norm_stage(h1h, 2, gff, hbh, rstdh)

        load_wup(0)
        load_wup(1)
        for ti in range(NT):
            h1 = h1s[ti % 2]
            load_tile(ti, h1)
            wout_stage(h1, catb, 512)
            norm_stage(h1, 512, gff, hb, rstd)
            for c in range(NFF):
                gidx = ti * NFF + c
                if c == 4 and ti > 0 and "after_hn" in dr:
                    dr["after_hn"](ti - 1)
                if gidx + 2 < NT * NFF:
                    load_wup(gidx + 2)
                wb = wup_b[gidx % 3]
                pu, pv, U = pus[c % 2], pvs[c % 2], Us[c % 2]
                t1, t2, sg = t1s[c % 2], t2s[c % 2], sgs[c % 2]
                for k in range(8):
                    cx.op("pe", lambda: PE.matmul(pu[:], wb[:, k, 0:128], hb[:, k, :], start=(k == 0), stop=(k == 7)),
                          reads=[wb, hb], writes=[pu], acc=(k > 0))
                if ti == 0:
                    for k in range(8):
                        cx.op("pe", lambda: PE.matmul(puh[:, 0:2], wb[:, k, 0:128], hbh[:, k, 0:2], start=(k == 0), stop=(k == 7)),
                              reads=[wb, hbh], writes=[puh], acc=(k > 0))
                for k in range(8):
                    cx.op("pe", lambda: PE.matmul(pv[:], wb[:, k, 128:256], hb[:, k, :], start=(k == 0), stop=(k == 7)),
                          reads=[wb, hb], writes=[pv], acc=(k > 0))
                if ti == 0:
                    cx.op("dve", lambda: V.tensor_tensor(U[:, 0:2], puh[:, 0:2], rstdh[:, 0:2], ALU.mult),
                          reads=[puh, rstdh], writes=[U])
                else:
                    cx.op("act", lambda: A.copy(out=U[:, 0:2], in_=carry[:, c, :]), reads=[carry], writes=[U])
                cx.op("dve", lambda: V.tensor_tensor(U[:, 2:514], pu[:], rstd[:], ALU.mult),
                      reads=[pu, rstd, U], writes=[U])
                cx.op("act", lambda: A.copy(out=carry[:, c, :], in_=U[:, 512:514]), reads=[U], writes=[carry])
                cx.op("act", lambda: A.activation(out=t1[:], in_=U[:, 0:512], func=AF.Identity,
                                                  scale=cw[:, 3 * c:3 * c + 1], bias=cb[:, c:c + 1]),
                      reads=[U, cw, cb], writes=[t1])
                cx.op("dve", lambda: V.scalar_tensor_tensor(t2[:], U[:, 1:513], cw[:, 3 * c + 1:3 * c + 2], t1[:], ALU.mult, ALU.add),
                      reads=[U, cw, t1], writes=[t2])
                cx.op("dve", lambda: V.scalar_tensor_tensor(t1[:], U[:, 2:514], cw[:, 3 * c + 2:3 * c + 3], t2[:], ALU.mult, ALU.add),
                      reads=[U, cw, t2], writes=[t1])
                cx.op("act", lambda: A.activation(out=sg[:], in_=t1[:], func=AF.Silu), reads=[t1], writes=[sg])
                cx.op("dve", lambda: V.tensor_tensor(actb[:, c, :], sg[:], pv[:], ALU.mult),
                      reads=[sg, pv], writes=[actb])
            for nch in range(8):
                pm = pms[nch % 2]
                tm = tmpo[nch % 2]
                for c in range(NFF):
                    cx.op("pe", lambda: PE.matmul(pm[:], wdown_b[:, c, nch * 128:(nch + 1) * 128], actb[:, c, :],
                                                  start=(c == 0), stop=(c == NFF - 1)),
                          reads=[wdown_b, actb], writes=[pm], acc=(c > 0))
                cx.op("dve", lambda: V.tensor_tensor(tm[:], pm[:], rstd[:], ALU.mult), reads=[pm, rstd], writes=[tm])
                cx.op("dve", lambda: V.tensor_tensor(h1[:, nch, :], h1[:, nch, :], tm[:], ALU.add),
                      reads=[tm, h1], writes=[h1])
            if want_hn:
                norm_stage(h1, 512, gnx, None, rstd2)
                for k in range(8):
                    cx.op("dve", lambda: V.scalar_tensor_tensor(hnb[:, k, :], h1[:, k, :], gnx[:, k:k + 1], rstd2[:], ALU.mult, ALU.mult),
                          reads=[h1, gnx, rstd2], writes=[hnb])
                if hn_outs is None:
                    cx.dma("sp", hnout, hn_ap[:, :, ti * 512:(ti + 1) * 512], hnb, hnb[:])
                else:
                    for (hd_, ha_, ho_, hw_) in hn_outs(ti):
                        cx.dma("sp", hd_, ha_.rearrange("(c p) t -> p c t", p=128), hnb, hnb[:, :, ho_:ho_ + hw_])
            cx.dma("sp", hout, hout_ap[:, :, ti * 512:(ti + 1) * 512], h1, h1[:])
            if hlast is not None and ti == NT - 1:
                cx.dma("sp", hlast, hlast.t.ap().rearrange("(c p) t -> p c t", p=128), h1, h1[:, :, 510:512])
            if ti == NT - 1 and "after_hn" in dr:
                dr["after_hn"](ti)


def build_mix0(S, do_b=True, do_ret=True, do_sbn=True, seclimit=10 ** 9):
    nc = bass.Bass("TRN2", target_bir_lowering=False)
    with contextlib.ExitStack() as es:
        cx = Ctx(nc, es)
        cx.seclimit = seclimit
        dr = _mix0_drams(cx, S, "")
        dr["outT"] = mk_dram(cx, "outT", [256, S], BF16, "ExternalOutput")
        cx.begin_phase("")
        emit_mix0(cx, S, dr, do_b, do_ret, do_sbn)
        cx.end_phase()
        cx.finish()
    return nc


def _mix0_drams(cx, S, p):
    return dict(
        xT=mk_dram(cx, p + "xT", [(S // 512) * 128, 4096], F32, "ExternalInput"),
        wsel=mk_dram(cx, p + "wsel", [D, 896], F32, "ExternalInput"),
        g_mix=mk_dram(cx, p + "g_mix", [128, 8], F32, "ExternalInput"),
        gains=mk_dram(cx, p + "gains", [1, 384], F32, "ExternalInput"),
        rope=mk_dram(cx, p + "rope", [S, 256], F32, "ExternalInput"),
        rconst=mk_dram(cx, p + "rconst", [128, 131], F32, "ExternalInput"),
        cmat=mk_dram(cx, p + "cmat", [128, 384], BF16, "ExternalInput"),
        dmask=mk_dram(cx, p + "dmask", [128, 2048], F32, "ExternalInput"))


def emit_mix0(cx, S, dr, do_b=True, do_ret=True, do_sbn=True):
    nc = cx.nc
    NCH = S // 128
    NT = S // 512
    if True:
        xT, wsel, gmix_d, gains_d, rope_d, rconst_d, cmat_d, dmask_d, outT = (
            dr["xT"], dr["wsel"], dr["g_mix"], dr["gains"], dr["rope"], dr["rconst"], dr["cmat"], dr["dmask"], dr.get("outT"))

        x_ap = xT.t.ap().rearrange("(n p) f -> n p f", p=128)
        w_ap = wsel.t.ap().rearrange("(k p) n -> p k n", p=128)
        rope_ap = rope_d.t.ap().rearrange("(n p) c -> p n c", p=128)
        if "out_win" in dr:
            out_win = dr["out_win"]
        else:
            out_win = lambda ti: (outT, outT.t.ap()[:, ti * 512:(ti + 1) * 512])

        V, A, P, PE = nc.vector, nc.scalar, nc.gpsimd, nc.tensor

        qT = cx.sb("qT", [128, S], BF16)
        kT = cx.sb("kT", [128, S], BF16)
        vres = cx.sb("vres", [128, NCH, 128], BF16)
        wb = cx.sb("wb", [128, 8, 896], BF16)
        gmix = cx.sb("gmix", [128, 8], F32)
        gains = cx.sb("gains_s", [128, 384], F32)
        rconst = cx.sb("rconst_s", [128, 131], F32)
        cmat = cx.sb("cmat_s", [128, 384], BF16)
        dmask = cx.sb("dmask_s", [128, 2048], F32)
        ones = cx.sb("ones", [128, 128], BF16)
        epst = cx.sb("epst", [128, 1], F32)
        cx.op("dve", lambda: V.memset(ones[:], 1.0), writes=[ones])
        cx.op("dve", lambda: V.memset(epst[:], EPS), writes=[epst])
        cx.dma("sp", gmix, gmix[:], gmix_d, gmix_d.t.ap())
        cx.dma("sp", gains, gains[:], gains_d, gains_d.t.ap()[0:1, :].broadcast_to([128, 384]))
        cx.dma("sp", rconst, rconst[:], rconst_d, rconst_d.t.ap())
        cx.dma("sp", cmat, cmat[:], cmat_d, cmat_d.t.ap())
        cx.dma("sp", dmask, dmask[:], dmask_d, dmask_d.t.ap())
        cx.op("dve", lambda: V.tensor_scalar(gains[:, 0:128], gains[:, 0:128], 0.125, None, ALU.mult),
              reads=[gains], writes=[gains])
        ident = cmat
        decT = rconst

        with contextlib.ExitStack() as esA:
            cxa_sb = lambda name, shape, dt: T(esA.enter_context(nc.sbuf_tensor(cx.pfx + name, list(shape), dt)), cx.pfx + name)
            def cxa_ps(name, shape, dt=F32):
                t = T(esA.enter_context(nc.psum_tensor(cx.pfx + name, list(shape), dt)), cx.pfx + name)
                t.excl = True
                return t
            stg = [cxa_sb("stg%d" % i, [128, 896], F32) for i in range(2)]
            xts = [cxa_sb("xt%d" % i, [128, 8, 512], F32) for i in range(2)]
            ropes = [cxa_sb("rope%d" % i, [128, 4, 256], F32) for i in range(2)]
            hb = cxa_sb("hb", [128, 8, 512], BF16)
            sq = cxa_sb("sq", [128, 8, 512], BF16)
            rstd = cxa_sb("rstd", [128, 4], F32)
            Rs = [cxa_sb("R%d" % i, [128, 512], F32) for i in range(2)]
            SBs = [cxa_sb("SB%d" % i, [128, 384], F32) for i in range(2)]
            sqt = cxa_sb("sqt", [128, 256], F32)
            qs4 = RT([cxa_sb("qs4_r%d" % r_, [128, 4], F32) for r_ in range(2)])
            rq4 = RT([cxa_sb("rq4_r%d" % r_, [128, 4], F32) for r_ in range(2)])
            qkn = RT([cxa_sb("qkn_r%d" % r_, [128, 256], BF16) for r_ in range(2)])
            ra = [RT([cxa_sb("ra%d_r%d" % (i, r_), [128, 2, 64], F32) for r_ in range(2)]) for i in range(4)]
            qkr = RT([cxa_sb("qkr_r%d" % r_, [128, 2, 128], BF16) for r_ in range(2)])
            qTr = RT([cxa_sb("qTr_r%d" % r_, [128, 128], BF16) for r_ in range(2)])
            kTr = RT([cxa_sb("kTr_r%d" % r_, [128, 128], BF16) for r_ in range(2)])
            kz = RT([cxa_sb("kz_r%d" % r_, [128, 128], BF16) for r_ in range(2)])
            vb = RT([cxa_sb("vb_r%d" % r_, [128, 128], BF16) for r_ in range(2)])
            ATb = cxa_sb("ATb", [128, 128], BF16)
            o1 = cxa_sb("o1", [128, 128], F32)
            o2 = RT([cxa_sb("o2_r%d" % r_, [128, 128], F32) for r_ in range(2)])
            junk = cxa_sb("junk", [128, 128], F32)
            os1 = RT([cxa_sb("os1_r%d" % r_, [128, 1], F32) for r_ in range(2)])
            rr = RT([cxa_sb("rr_r%d" % r_, [128, 1], F32) for r_ in range(2)])
            sgt = cxa_sb("sgt", [128, 128], F32)
            ob = cxa_sb("ob", [128, 128], BF16)
            Rst = cxa_sb("Rst", [128, 128], F32)
            Rstb = cxa_sb("Rstb", [128, 128], BF16)
            outb = [cxa_sb("outb%d" % i, [128, 512], BF16) for i in range(2)]

            pp0 = [cxa_ps("pp0_%d" % i, [128, 512]) for i in range(2)]
            pp1 = [cxa_ps("pp1_%d" % i, [128, 512]) for i in range(2)]
            pssq = cxa_ps("pssq", [128, 4])
            ptr_t = cxa_ps("ptr", [128, 8, 128], BF16)
            ptr = [T(ptr_t.t, "ptr%d" % i, (slice(None), i, slice(None)), st=ptr_t) for i in range(8)]
            pmA_t = cxa_ps("pmA", [128, 4, 128])
            pmA = [T(pmA_t.t, "pmA%d" % i, (slice(None), i, slice(None)), st=pmA_t) for i in range(4)]

            rotA = [qs4, rq4, qkn, qkr, qTr, kTr, kz, vb, o2, os1, rr] + ra
            for k in range(8):
                s = stg[k % 2]
                cx.dma("sp", s, s[:], wsel, w_ap[:, k, :])
                cx.op("act", lambda: A.copy(out=wb[:, k, :], in_=s[:]), reads=[s], writes=[wb])
            cx.op("dve", lambda: V.memset(Rst[:], 0.0), writes=[Rst])
            cx.op("dve", lambda: V.memset(Rstb[:], 0.0), writes=[Rstb])

            def lnexp_rstd(out_t, out_ap_, in_t, in_ap_, scale):
                cx.op("act", lambda: A.activation(out=out_ap_, in_=in_ap_, func=AF.Ln, scale=scale, bias=epst[:, 0:1]),
                      reads=[in_t, epst], writes=[out_t])
                cx.op("act", lambda: A.activation(out=out_ap_, in_=out_ap_, func=AF.Exp, scale=-0.5),
                      reads=[out_t], writes=[out_t])

            for ti in range(NT):
                xt = xts[ti % 2]
                rp = ropes[ti % 2]
                cx.dma("sp", xt, xt[:].rearrange("p a b -> p (a b)"), xT, x_ap[ti])
                cx.dma("sp", rp, rp[:], rope_d, rope_ap[:, ti * 4:(ti + 1) * 4, :])
                for k in range(8):
                    cx.op("act", lambda: A.activation(out=sq[:, k, :], in_=xt[:, k, :], func=AF.Square), reads=[xt], writes=[sq])
                    cx.op("dve", lambda: V.tensor_scalar(hb[:, k, :], xt[:, k, :], gmix[:, k:k + 1], None, ALU.mult),
                          reads=[xt, gmix], writes=[hb])
                for j in range(4):
                    for k in range(8):
                        cx.op("pe", lambda: PE.matmul(pssq[:, j:j + 1], sq[:, k, j * 128:(j + 1) * 128], ones[:, 0:1],
                                                      start=(k == 0), stop=(k == 7)),
                              reads=[sq, ones], writes=[pssq], acc=(k > 0 or j > 0))
                lnexp_rstd(rstd, rstd[:], pssq, pssq[:], 1.0 / D)
                ob_stage = outb[ti % 2]
                for j in range(4):
                    n = ti * 4 + j
                    for rt_ in rotA:
                        rt_.sel(n)
                    p0, p1 = pp0[n % 2], pp1[n % 2]
                    R, SBt = Rs[n % 2], SBs[n % 2]
                    for k in range(8):
                        cx.op("pe", lambda: PE.matmul(p0[:], hb[:, k, j * 128:(j + 1) * 128], wb[:, k, 0:512],
                                                      start=(k == 0), stop=(k == 7)),
                              reads=[hb, wb], writes=[p0], acc=(k > 0))
                    for k in range(8):
                        cx.op("pe", lambda: PE.matmul(p1[:, 0:384], hb[:, k, j * 128:(j + 1) * 128], wb[:, k, 512:896],
                                                      start=(k == 0), stop=(k == 7)),
                              reads=[hb, wb], writes=[p1], acc=(k > 0))
                    cx.op("act", lambda: A.activation(out=R[:], in_=p0[:], func=AF.Copy, scale=rstd[:, j:j + 1]),
                          reads=[p0, rstd], writes=[R])
                    cx.op("act", lambda: A.activation(out=SBt[:], in_=p1[:, 0:384], func=AF.Copy, scale=rstd[:, j:j + 1]),
                          reads=[p1, rstd], writes=[SBt])
                    cx.mute = not do_sbn
                    cx.op("dve", lambda: V.tensor_tensor(sqt[:], SBt[:, 0:256], SBt[:, 0:256], ALU.mult), reads=[SBt], writes=[sqt])
                    cx.op("dve", lambda: V.tensor_reduce(qs4[:], sqt[:].rearrange("p (a d) -> p a d", d=64), AX.X, ALU.add),
                          reads=[sqt], writes=[qs4])
                    lnexp_rstd(rq4, rq4[:], qs4, qs4[:], 1.0 / 64)
                    for h4 in range(4):
                        sl = slice(h4 * 64, (h4 + 1) * 64)
                        cx.op("dve", lambda: V.scalar_tensor_tensor(qkn[:, sl], SBt[:, sl], rq4[:, h4:h4 + 1], gains[:, sl], ALU.mult, ALU.mult),
                              reads=[SBt, rq4, gains], writes=[qkn])
                    cx.op("pe", lambda: PE.transpose(ptr[0][:], qkn[:, 0:128], ident[:, 0:128]), reads=[qkn, ident], writes=[ptr[0]])
                    cx.op("pe", lambda: PE.transpose(ptr[1][:], qkn[:, 128:256], ident[:, 0:128]), reads=[qkn, ident], writes=[ptr[1]])
                    cx.op("act", lambda: A.copy(out=qT[:, n * 128:(n + 1) * 128], in_=ptr[0][:]), reads=[ptr[0]], writes=[qT])
                    cx.op("act", lambda: A.copy(out=kT[:, n * 128:(n + 1) * 128], in_=ptr[1][:]), reads=[ptr[1]], writes=[kT])
                    cx.op("pool", lambda: P.tensor_copy(out=vres[:, n, :], in_=SBt[:, 256:384]), reads=[SBt], writes=[vres])
                    cx.mute = not do_ret
                    cx.insec = True
                    Rv = R[:, 0:256].rearrange("p (a d) -> p a d", a=2)
                    CC = rp[:, j, 0:128].rearrange("p (a d) -> p a d", a=2)
                    SS = rp[:, j, 128:256].rearrange("p (a d) -> p a d", a=2)
                    cx.op("dve", lambda: V.tensor_tensor(ra[0][:], Rv[:, :, 0:64], CC, ALU.mult), reads=[R, rp], writes=[ra[0]])
                    cx.op("dve", lambda: V.tensor_tensor(ra[1][:], Rv[:, :, 64:128], SS, ALU.mult), reads=[R, rp], writes=[ra[1]])
                    cx.op("dve", lambda: V.tensor_tensor(ra[2][:], Rv[:, :, 0:64], SS, ALU.mult), reads=[R, rp], writes=[ra[2]])
                    cx.op("dve", lambda: V.tensor_tensor(ra[3][:], Rv[:, :, 64:128], CC, ALU.mult), reads=[R, rp], writes=[ra[3]])
                    cx.op("dve", lambda: V.tensor_tensor(qkr[:, :, 0:64], ra[0][:], ra[1][:], ALU.subtract), reads=[ra[0], ra[1]], writes=[qkr])
                    cx.op("dve", lambda: V.tensor_tensor(qkr[:, :, 64:128], ra[2][:], ra[3][:], ALU.add), reads=[ra[2], ra[3]], writes=[qkr])
                    cx.op("pe", lambda: PE.transpose(ptr[2][:], qkr[:, 0, :], ident[:, 0:128]), reads=[qkr, ident], writes=[ptr[2]])
                    cx.op("pe", lambda: PE.transpose(ptr[3][:], qkr[:, 1, :], ident[:, 0:128]), reads=[qkr, ident], writes=[ptr[3]])
                    cx.op("act", lambda: A.copy(out=qTr[:], in_=ptr[2][:]), reads=[ptr[2]], writes=[qTr])
                    cx.op("act", lambda: A.copy(out=kTr[:], in_=ptr[3][:]), reads=[ptr[3]], writes=[kTr])
                    cx.op("dve", lambda: V.tensor_scalar(kz[:], qkr[:, 1, :], rconst[:, 128:129], None, ALU.mult), reads=[qkr, rconst], writes=[kz])
                    cx.op("pool", lambda: P.tensor_copy(out=vb[:], in_=R[:, 256:384]), reads=[R], writes=[vb])
                    psc, po, pc, pk = pmA
                    cx.op("pe", lambda: PE.matmul(psc[:], kTr[:], qTr[:], start=True, stop=True), reads=[kTr, qTr], writes=[psc])
                    cx.op("dve", lambda: V.tensor_tensor(ATb[:], psc[:], decT[:, 0:128], ALU.mult), reads=[psc, decT], writes=[ATb])
                    cx.op("pe", lambda: PE.matmul(pc[:], qTr[:], Rstb[:], start=True, stop=True), reads=[qTr, Rstb], writes=[pc])
                    cx.op("pe", lambda: PE.matmul(po[:], ATb[:], vb[:], start=True, stop=True), reads=[ATb, vb], writes=[po])
                    cx.op("pe", lambda: PE.matmul(pk[:], kz[:], vb[:], start=True, stop=True), reads=[kz, vb], writes=[pk])
                    cx.op("act", lambda: A.activation(out=o1[:], in_=pc[:], func=AF.Copy, scale=rconst[:, 129:130]),
                          reads=[pc, rconst], writes=[o1])
                    cx.op("dve", lambda: V.tensor_tensor(o2[:], o1[:], po[:], ALU.add), reads=[o1, po], writes=[o2])
                    cx.op("dve", lambda: V.scalar_tensor_tensor(Rst[:], Rst[:], rconst[:, 130:131], pk[:], ALU.mult, ALU.add),
                          reads=[Rst, rconst, pk], writes=[Rst])
                    cx.op("act", lambda: A.copy(out=Rstb[:], in_=Rst[:]), reads=[Rst], writes=[Rstb])
                    cx.op("act", lambda: A.activation(out=junk[:], in_=o2[:], func=AF.Square, accum_out=os1[:, 0:1]),
                          reads=[o2], writes=[junk, os1])
                    lnexp_rstd(rr, rr[:], os1, os1[:], 1.0 / 128)
                    cx.op("act", lambda: A.activation(out=sgt[:], in_=R[:, 384:512], func=AF.Silu), reads=[R], writes=[sgt])
                    cx.op("dve", lambda: V.scalar_tensor_tensor(o1[:], o2[:], rr[:, 0:1], gains[:, 256:384], ALU.mult, ALU.mult),
                          reads=[o2, rr, gains], writes=[o1])
                    cx.op("dve", lambda: V.tensor_tensor(ob[:], o1[:], sgt[:], ALU.mult), reads=[o1, sgt], writes=[ob])
                    cx.op("pe", lambda: PE.transpose(ptr[4][:], ob[:], ident[:, 0:128]), reads=[ob, ident], writes=[ptr[4]])
                    cx.op("act", lambda: A.copy(out=ob_stage[:, j * 128:(j + 1) * 128], in_=ptr[4][:]), reads=[ptr[4]], writes=[ob_stage])
                cx.mute = False
                cx.insec = False
                od_, oa_ = out_win(ti)
                cx.dma("sp", od_, oa_[128:256, :], ob_stage, ob_stage[:])
            cx.barrier()

        with contextlib.ExitStack() as esB:
          if do_b:
                cxb_sb = lambda name, shape, dt: T(esB.enter_context(nc.sbuf_tensor(cx.pfx + name, list(shape), dt)), cx.pfx + name)
                def cxb_ps(name, shape, dt=F32):
                    t = T(esB.enter_context(nc.psum_tensor(cx.pfx + name, list(shape), dt)), cx.pfx + name)
                    t.excl = True
                    return t
                Es = [[cxb_sb("E%d_%d" % (h, i), [128, 512], F32) for i in range(2)] for h in range(2)]
                SPs = [[cxb_sb("SP%d_%d" % (h, i), [128, 512], BF16) for i in range(3)] for h in range(2)]
                Xs = [[cxb_sb("X%d_%d" % (h, i), [128, 512], F32) for i in range(2)] for h in range(2)]
                Ws = [[cxb_sb("W%d_%d" % (h, i), [128, 512], BF16) for i in range(3)] for h in range(2)]
                oa = [[cxb_sb("oa%d_%d" % (h, i), [64, 512], BF16) for i in range(2)] for h in range(2)]
                pz = [[cxb_ps("pz%d_%d" % (h, i), [128, 512]) for i in range(2)] for h in range(2)]
                pcs = [cxb_ps("pcs%d" % h, [128, 512]) for h in range(2)]
                pout = [cxb_ps("pout%d" % h, [64, 512]) for h in range(2)]
                Umat = cmat
                for qs in range(NT):
                    kbs = list(range(4 * qs + 3, -1, -1))
                    nst = len(kbs)

                    def zmm(i):
                        kb = kbs[i]
                        for h in range(2):
                            hs = slice(h * 64, (h + 1) * 64)
                            p = pz[h][i % 2]
                            cx.op("pe", lambda: PE.matmul(p[:], kT[hs, kb * 128:(kb + 1) * 128], qT[hs, qs * 512:(qs + 1) * 512],
                                                          start=True, stop=True), reads=[kT, qT], writes=[p])
                    def emit_pv(j):
                        for h in range(2):
                            W = Ws[h][j % 3]
                            cx.op("pe", lambda: PE.matmul(pout[h][:], vres[:, kbs[j], h * 64:(h + 1) * 64], W[:], start=(j == 0), stop=(j == nst - 1)),
                                  reads=[vres, W], writes=[pout[h]], acc=(j > 0))

                    zmm(0)
                    for i in range(nst):
                        kb = kbs[i]
                        if i == 3 and qs > 0 and "after_out" in dr:
                            dr["after_out"](qs - 1)
                        for h in range(2):
                            E, SP, p = Es[h][i % 2], SPs[h][i % 3], pz[h][i % 2]
                            cx.op("act", lambda: A.activation(out=E[:], in_=p[:], func=AF.Exp), reads=[p], writes=[E])
                            if kb >= 4 * qs:
                                dm = kb - 4 * qs
                                cx.op("pool", lambda: P.tensor_tensor(E[:], E[:], dmask[:, dm * 512:(dm + 1) * 512], ALU.mult),
                                      reads=[E, dmask], writes=[E])
                            cx.op("act", lambda: A.activation(out=SP[:], in_=E[:], func=AF.Ln, bias=1.0), reads=[E], writes=[SP])
                        for h in range(2):
                            SP = SPs[h][i % 3]
                            cx.op("pe", lambda: PE.matmul(pcs[h][:], Umat[:, 128:256], SP[:], start=(i == 0), stop=False, skip_group_check=True),
                                  reads=[Umat, SP], writes=[pcs[h]], acc=(i > 0))
                            if i > 0:
                                SPp = SPs[h][(i - 1) % 3]
                                cx.op("pe", lambda: PE.matmul(pcs[h][:], Umat[:, 256:384], SPp[:], start=False, stop=False, skip_group_check=True),
                                      reads=[Umat, SPp], writes=[pcs[h]], acc=True)
                        if i + 1 < nst:
                            zmm(i + 1)
                        if i > 0:
                            emit_pv(i - 1)
                        for h in range(2):
                            X = Xs[h][i % 2]
                            cx.op("act", lambda: A.activation(out=X[:], in_=pcs[h][:], func=AF.Exp, scale=-1.0), reads=[pcs[h]], writes=[X])
                        for h in range(2):
                            X, W, E = Xs[h][i % 2], Ws[h][i % 3], Es[h][i % 2]
                            cx.op("dve", lambda: V.tensor_tensor(W[:], E[:], X[:], ALU.mult), reads=[E, X], writes=[W])
                    emit_pv(nst - 1)
                    for h in range(2):
                        o = oa[h][qs % 2]
                        cx.op("dve", lambda: V.tensor_copy(out=o[:], in_=pout[h][:]), reads=[pout[h]], writes=[o])
                        od_, oa_ = out_win(qs)
                        cx.dma("sp", od_, oa_[h * 64:(h + 1) * 64, :], o, o[:])
                    if qs == NT - 1 and "after_out" in dr:
                        dr["after_out"](qs)


def _v8(g):
    return np.ascontiguousarray(np.asarray(g, np.float32).reshape(8, 128).T)


def _const_tables(S):
    pos = np.arange(S, dtype=np.float32)
    inv = (np.float32(10000.0) ** (-(np.arange(64, dtype=np.float32) / np.float32(64)))).astype(np.float32)
    ang = (pos[:, None] * inv[None, :]).astype(np.float32).astype(np.float64)
    cos, sin = np.cos(ang).astype(np.float32), np.sin(ang).astype(np.float32)
    rope = np.ascontiguousarray(np.concatenate([cos, cos, sin, sin], axis=1))
    j = np.arange(128)
    ident = (j[:, None] == j[None, :])
    U = (j[:, None] >= j[None, :])
    L = (j[:, None] < j[None, :])
    cmat = np.concatenate([ident, U, L], axis=1).astype(np.float32).astype(NPBF)
    strict = (j[:, None] < j[None, :]).astype(np.float32)
    dmask = np.zeros((128, 4, 512), np.float32)
    for dm in range(4):
        for qb in range(4):
            if qb == dm:
                dmask[:, dm, qb * 128:(qb + 1) * 128] = strict
            elif qb > dm:
                dmask[:, dm, qb * 128:(qb + 1) * 128] = 1.0
    return rope, cmat, np.ascontiguousarray(dmask.reshape(128, 2048))


def _ret_consts(g):
    lg = np.log(np.float32(1.0) - np.exp2(np.float32(-5.0 - g))).astype(np.float64)
    j = np.arange(128, dtype=np.float64)
    sc = 128.0 ** -0.5
    diff = j[None, :] - j[:, None]
    decT = np.where(diff >= 0, np.exp(lg * np.maximum(diff, 0.0)), 0.0) * sc
    zeta = np.exp(lg * (127.0 - j)) * sc
    xi = np.exp(lg * (j + 1.0))
    gch = np.full(128, np.exp(lg * 128.0))
    return np.ascontiguousarray(np.concatenate([decT, zeta[:, None], xi[:, None], gch[:, None]], axis=1).astype(np.float32))


def _mix0_maps(xT_b, S, mix_g, w_in, q_gain, k_gain, ret_gain, tables):
    rope, cmat, dmask = tables
    xtl = [np.ascontiguousarray(xb.reshape(8, 128, S // 512, 512).transpose(2, 1, 0, 3).reshape((S // 512) * 128, 4096)) for xb in xT_b]
    maps = []
    for c in range(NCORES):
        b, g = c // 4, c % 4
        cols = np.concatenate([
            1536 + g * 128 + np.arange(128), 2048 + g * 128 + np.arange(128),
            2560 + g * 128 + np.arange(128), 3072 + g * 128 + np.arange(128),
            g * 128 + np.arange(128), 512 + g * 128 + np.arange(128), 1024 + g * 128 + np.arange(128)])
        wsel = np.ascontiguousarray(w_in[:, cols])
        gains = np.concatenate([q_gain, q_gain, k_gain, k_gain, ret_gain]).astype(np.float32)[None, :]
        maps.append(dict(xT=xtl[b], wsel=wsel, g_mix=_v8(mix_g), gains=np.ascontiguousarray(gains), rope=rope,
                         rconst=_ret_consts(g), cmat=cmat, dmask=dmask))
    return maps


def build_gla(S, oplimit=10 ** 9):
    nc = bass.Bass("TRN2", target_bir_lowering=False)
    with contextlib.ExitStack() as es:
        cx = Ctx(nc, es)
        cx.oplimit = oplimit
        dr = _gla_drams(cx, "")
        dr["hnT"] = mk_dram(cx, "hnT", [D, S], BF16, "ExternalInput")
        dr["outT"] = mk_dram(cx, "outT", [256, S], BF16, "ExternalOutput")
        cx.begin_phase("")
        emit_gla(cx, S, dr, gathered=False)
        cx.end_phase()
        cx.finish()
    return nc


def _gla_drams(cx, p):
    return dict(
        wsel=mk_dram(cx, p + "wsel", [D, 784], F32, "ExternalInput"),
        walpha=mk_dram(cx, p + "walpha", [17, 128], F32, "ExternalInput"),
        gain=mk_dram(cx, p + "gain", [1, 256], F32, "ExternalInput"),
        cmat=mk_dram(cx, p + "cmat", [128, 384], BF16, "ExternalInput"))


def emit_gla(cx, S, dr, gathered):
    nc = cx.nc
    NT = S // 512
    if True:
        hnT, wsel, walpha_d, gain_d, cmat_d, outT = dr.get("hnT"), dr["wsel"], dr["walpha"], dr["gain"], dr["cmat"], dr.get("outT")
        own = S // 4
        if gathered:
            hn_wins = dr["hn_wins"]
        else:
            h_ap = hnT.t.ap().rearrange("(c p) t -> p c t", p=128)
            hn_wins = lambda ti: [(hnT, h_ap[:, :, ti * 512:(ti + 1) * 512], 0, 512)]
        if "out_win" in dr:
            out_win = dr["out_win"]
        else:
            out_win = lambda ti: (outT, outT.t.ap()[:, ti * 512:(ti + 1) * 512])
        w_ap = wsel.t.ap().rearrange("(k p) n -> p k n", p=128)
        V, A, P, PE = nc.vector, nc.scalar, nc.gpsimd, nc.tensor

        wb = cx.sb("wb", [128, 8, 784], BF16)
        stg = [cx.sb("stg%d" % i, [128, 784], F32) for i in range(2)]
        wal_f = cx.sb("wal_f", [17, 128], F32)
        wal_b = cx.sb("wal_b", [17, 128], BF16)
        gain = cx.sb("gain_s", [128, 256], F32)
        cmat = cx.sb("cmat_s", [128, 384], BF16)
        trif = cx.sb("trif", [128, 128], F32)
        ones = cx.sb("ones", [128, 2], BF16)
        epst = cx.sb("epst", [128, 1], F32)
        hts = [cx.sb("ht%d" % i, [128, 8, 512], BF16) for i in range(2)]
        gaT = cx.sb("gaT", [32, 128], BF16)
        ee = RT([cx.sb("ee_r%d" % r_, [128, 128], F32) for r_ in range(2)])
        sp = RT([cx.sb("sp_r%d" % r_, [128, 128], F32) for r_ in range(2)])
        sph = RT([cx.sb("sph_r%d" % r_, [128, 128], BF16) for r_ in range(2)])
        spl = RT([cx.sb("spl_r%d" % r_, [128, 128], BF16) for r_ in range(2)])
        csum = RT([cx.sb("csum_r%d" % r_, [128, 128], F32) for r_ in range(2)])
        dif = RT([cx.sb("dif_r%d" % r_, [128, 128], F32) for r_ in range(2)])
        eq = RT([cx.sb("eq_r%d" % r_, [128, 128], F32) for r_ in range(2)])
        ek = RT([cx.sb("ek_r%d" % r_, [128, 128], F32) for r_ in range(2)])
        ekd = RT([cx.sb("ekd_r%d" % r_, [128, 128], F32) for r_ in range(2)])
        qt = RT([cx.sb("qt_r%d" % r_, [128, 128], BF16) for r_ in range(2)])
        kt = RT([cx.sb("kt_r%d" % r_, [128, 128], BF16) for r_ in range(2)])
        kd = RT([cx.sb("kd_r%d" % r_, [128, 128], BF16) for r_ in range(2)])
        vb = RT([cx.sb("vb_r%d" % r_, [128, 256], BF16) for r_ in range(2)])
        sg = RT([cx.sb("sg_r%d" % r_, [128, 256], F32) for r_ in range(2)])
        qTf = RT([cx.sb("qTf_r%d" % r_, [128, 128], BF16) for r_ in range(2)])
        q0T = cx.sb("q0T", [128, 128], BF16)
        q1T = cx.sb("q1T", [128, 128], BF16)
        kTf = RT([cx.sb("kTf_r%d" % r_, [128, 128], BF16) for r_ in range(2)])
        ATb = RT([cx.sb("ATb_r%d" % r_, [128, 128], BF16) for r_ in range(2)])
        St = cx.sb("St", [128, 256], F32)
        Sb = [cx.sb("Sb%d" % i, [128, 256], BF16) for i in range(2)]
        dec = RT([cx.sb("dec_r%d" % r_, [128, 2], F32) for r_ in range(2)])
        junk = RT([cx.sb("junk_r%d" % r_, [128, 256], F32) for r_ in range(2)])
        os1 = RT([cx.sb("os1_r%d" % r_, [128, 1], F32) for r_ in range(2)])
        rr = RT([cx.sb("rr_r%d" % r_, [128, 1], F32) for r_ in range(2)])
        y1 = RT([cx.sb("y1_r%d" % r_, [128, 256], F32) for r_ in range(2)])
        y2 = RT([cx.sb("y2_r%d" % r_, [128, 256], BF16) for r_ in range(2)])
        outb = [cx.sb("outb%d" % i, [128, 2, 512], BF16) for i in range(2)]

        pA = [cx.ps("pA%d" % i, [128, 512]) for i in range(2)]
        pB_t = cx.ps("pBt", [128, 2, 256])
        pB = [T(pB_t.t, "pB%d" % i, (slice(None), i, slice(None)), st=pB_t) for i in range(2)]
        pm_t = cx.ps("pm", [128, 4, 128])
        py, pcum, ptot, paT = [T(pm_t.t, "pm%d" % i, (slice(None), i, slice(None)), st=pm_t) for i in range(4)]
        po = cx.ps("po", [128, 256])
        pk = cx.ps("pk", [128, 256])
        pm2_t = cx.ps("pm2", [128, 512])
        pga = T(pm2_t.t, "pga", (slice(0, 16), slice(0, 128)), st=pm2_t)
        pdec = T(pm2_t.t, "pdec", (slice(None), slice(128, 130)), st=pm2_t)
        ptr_t = cx.ps("ptr", [128, 8, 128], BF16)
        ptr = [T(ptr_t.t, "ptr%d" % i, (slice(None), i, slice(None)), st=ptr_t) for i in range(4)]

        cx.op("dve", lambda: V.memset(ones[:], 1.0), writes=[ones])
        cx.op("dve", lambda: V.memset(epst[:], EPS), writes=[epst])
        cx.op("dve", lambda: V.memset(gaT[:], 1.0), writes=[gaT])
        cx.op("dve", lambda: V.memset(q0T[:], 0.0), writes=[q0T])
        cx.op("dve", lambda: V.memset(q1T[:], 0.0), writes=[q1T])
        cx.op("dve", lambda: V.memset(St[:], 0.0), writes=[St])
        cx.op("dve", lambda: V.memset(Sb[0][:], 0.0), writes=[Sb[0]])
        cx.dma("sp", wal_f, wal_f[:], walpha_d, walpha_d.t.ap())
        cx.dma("sp", gain, gain[:], gain_d, gain_d.t.ap()[0:1, :].broadcast_to([128, 256]))
        cx.dma("sp", cmat, cmat[:], cmat_d, cmat_d.t.ap())
        cx.op("act", lambda: A.copy(out=wal_b[:], in_=wal_f[:]), reads=[wal_f], writes=[wal_b])
        cx.op("dve", lambda: V.tensor_copy(out=trif[:], in_=cmat[:, 128:256]), reads=[cmat], writes=[trif])
        for k in range(8):
            s = stg[k % 2]
            cx.dma("sp", s, s[:], wsel, w_ap[:, k, :])
            cx.op("act", lambda: A.copy(out=wb[:, k, :], in_=s[:]), reads=[s], writes=[wb])
        ident, tri, blk = cmat, cmat, cmat

        def lnexp_rstd(out_t, out_ap_, in_t, in_ap_, scale):
            cx.op("act", lambda: A.activation(out=out_ap_, in_=in_ap_, func=AF.Ln, scale=scale, bias=epst[:, 0:1]),
                  reads=[in_t, epst], writes=[out_t])
            cx.op("act", lambda: A.activation(out=out_ap_, in_=out_ap_, func=AF.Exp, scale=-0.5),
                  reads=[out_t], writes=[out_t])

        for ti in range(NT):
            ht = hts[ti % 2]
            for (hd_, ha_, ho_, hw_) in hn_wins(ti):
                cx.dma("sp", ht, ht[:, :, ho_:ho_ + hw_], hd_, ha_)
            ost = outb[ti % 2]
            for j in range(4):
                n = ti * 4 + j
                for rt_ in (ee, sp, sph, spl, csum, dif, eq, ek, ekd, qt, kt, kd, vb, sg, qTf, kTf, ATb, dec, junk, os1, rr, y1, y2):
                    rt_.sel(n)
                ts = slice(j * 128, (j + 1) * 128)
                a_, b_ = pA[n % 2], pB[n % 2]
                if j == 1 and ti > 0 and "after_out" in dr:
                    dr["after_out"](ti - 1)
                for k in range(8):
                    cx.op("pe", lambda: PE.matmul(a_[:], ht[:, k, ts], wb[:, k, 0:512], start=(k == 0), stop=(k == 7)),
                          reads=[ht, wb], writes=[a_], acc=(k > 0))
                for k in range(8):
                    cx.op("pe", lambda: PE.matmul(b_[:, 0:256], ht[:, k, ts], wb[:, k, 512:768], start=(k == 0), stop=(k == 7)),
                          reads=[ht, wb], writes=[b_], acc=(k > 0))
                for k in range(8):
                    cx.op("pe", lambda: PE.matmul(pga[:], wb[:, k, 768:784], ht[:, k, ts], start=(k == 0), stop=(k == 7)),
                          reads=[ht, wb], writes=[pga], acc=(k > 0))
                cx.op("act", lambda: A.copy(out=gaT[0:16, :], in_=pga[:]), reads=[pga], writes=[gaT])
                cx.op("pe", lambda: PE.matmul(py[:], gaT[0:17, :], wal_b[:, :], start=True, stop=True), reads=[gaT, wal_b], writes=[py])
                cx.op("act", lambda: A.activation(out=ee[:], in_=py[:], func=AF.Exp, scale=-1.0), reads=[py], writes=[ee])
                cx.op("act", lambda: A.activation(out=sp[:], in_=ee[:], func=AF.Ln, bias=1.0), reads=[ee], writes=[sp])
                cx.op("pool", lambda: P.tensor_copy(out=sph[:], in_=sp[:]), reads=[sp], writes=[sph])
                cx.op("dve", lambda: V.tensor_tensor(spl[:], sp[:], sph[:], ALU.subtract), reads=[sp, sph], writes=[spl])
                cx.op("pe", lambda: PE.matmul(pcum[:], tri[:, 128:256], sph[:], start=True, stop=False), reads=[tri, sph], writes=[pcum])
                cx.op("pe", lambda: PE.matmul(pcum[:], tri[:, 128:256], spl[:], start=False, stop=True), reads=[tri, spl], writes=[pcum], acc=True)
                cx.op("pe", lambda: PE.matmul(ptot[:], blk[:, 256:384], sph[:], start=True, stop=False), reads=[blk, sph], writes=[ptot])
                cx.op("pe", lambda: PE.matmul(ptot[:], blk[:, 256:384], spl[:], start=False, stop=True), reads=[blk, spl], writes=[ptot], acc=True)
                for c in range(2):
                    cs = slice(c * 64, (c + 1) * 64)
                    cx.op("pe", lambda: PE.matmul(pdec[:, c:c + 1], sph[cs, :], ones[cs, 0:1], start=True, stop=False),
                          reads=[sph, ones], writes=[pdec])
                    cx.op("pe", lambda: PE.matmul(pdec[:, c:c + 1], spl[cs, :], ones[cs, 0:1], start=False, stop=True),
                          reads=[spl, ones], writes=[pdec], acc=True)
                cx.op("act", lambda: A.copy(out=csum[:], in_=pcum[:]), reads=[pcum], writes=[csum])
                cx.op("dve", lambda: V.tensor_tensor(dif[:], csum[:], ptot[:], ALU.subtract), reads=[csum, ptot], writes=[dif])
                cx.op("act", lambda: A.activation(out=eq[:], in_=csum[:], func=AF.Exp, scale=-1.0 / 16), reads=[csum], writes=[eq])
                cx.op("act", lambda: A.activation(out=ek[:], in_=csum[:], func=AF.Exp, scale=1.0 / 16), reads=[csum], writes=[ek])
                cx.op("act", lambda: A.activation(out=ekd[:], in_=dif[:], func=AF.Exp, scale=1.0 / 16), reads=[dif], writes=[ekd])
                cx.op("act", lambda: A.activation(out=dec[:], in_=pdec[:], func=AF.Exp, scale=-1.0 / 16), reads=[pdec], writes=[dec])
                cx.op("dve", lambda: V.scalar_tensor_tensor(qt[:], a_[:, 0:128], 128.0 ** -0.5, eq[:], ALU.mult, ALU.mult),
                      reads=[a_, eq], writes=[qt])
                cx.op("dve", lambda: V.tensor_tensor(kt[:], a_[:, 128:256], ek[:], ALU.mult), reads=[a_, ek], writes=[kt])
                cx.op("dve", lambda: V.tensor_tensor(kd[:], a_[:, 128:256], ekd[:], ALU.mult), reads=[a_, ekd], writes=[kd])
                cx.op("act", lambda: A.copy(out=vb[:], in_=a_[:, 256:512]), reads=[a_], writes=[vb])
                cx.op("act", lambda: A.activation(out=sg[:], in_=b_[:, 0:256], func=AF.Silu), reads=[b_], writes=[sg])
                cx.op("pe", lambda: PE.transpose(ptr[0][:], qt[:], ident[:, 0:128]), reads=[qt, ident], writes=[ptr[0]])
                cx.op("pe", lambda: PE.transpose(ptr[1][:], kt[:], ident[:, 0:128]), reads=[kt, ident], writes=[ptr[1]])
                cx.op("act", lambda: A.copy(out=qTf[:], in_=ptr[0][:]), reads=[ptr[0]], writes=[qTf])
                cx.op("act", lambda: A.copy(out=kTf[:], in_=ptr[1][:]), reads=[ptr[1]], writes=[kTf])
                cx.op("pool", lambda: P.tensor_copy(out=q0T[:, 0:64], in_=qTf[:, 0:64]), reads=[qTf], writes=[q0T])
                cx.op("pool", lambda: P.tensor_copy(out=q1T[:, 64:128], in_=qTf[:, 64:128]), reads=[qTf], writes=[q1T])
                cx.op("pe", lambda: PE.matmul(paT[:], kTf[:], qTf[:], start=True, stop=True), reads=[kTf, qTf], writes=[paT])
                cx.op("dve", lambda: V.tensor_tensor(ATb[:], paT[:], trif[:], ALU.mult), reads=[paT, trif], writes=[ATb])
                for c in range(2):
                    cs = slice(c * 64, (c + 1) * 64)
                    if c == 0:
                        cx.op("pe", lambda: PE.matmul(po[:], ATb[:], vb[:], start=True, stop=False), reads=[ATb, vb], writes=[po])
                        cx.op("pe", lambda: PE.matmul(po[:], q0T[:], Sb[0][:], start=False, stop=False), reads=[q0T, Sb[0]], writes=[po], acc=True)
                    cx.op("pe", lambda: PE.matmul(pk[:], kd[cs, :], vb[cs, :], start=True, stop=True), reads=[kd, vb], writes=[pk])
                    cx.op("dve", lambda: V.scalar_tensor_tensor(St[:], St[:], dec[:, c:c + 1], pk[:], ALU.mult, ALU.add),
                          reads=[St, dec, pk], writes=[St])
                    nb = Sb[1 - c]
                    cx.op("act", lambda: A.copy(out=nb[:], in_=St[:]), reads=[St], writes=[nb])
                    if c == 0:
                        cx.op("pe", lambda: PE.matmul(po[:], q1T[:], Sb[1][:], start=False, stop=True), reads=[q1T, Sb[1]], writes=[po], acc=True)
                cx.op("act", lambda: A.activation(out=junk[:], in_=po[:], func=AF.Square, accum_out=os1[:, 0:1]), reads=[po], writes=[junk, os1])
                lnexp_rstd(rr, rr[:], os1, os1[:], 1.0 / 256)
                cx.op("dve", lambda: V.scalar_tensor_tensor(y1[:], po[:], rr[:, 0:1], gain[:], ALU.mult, ALU.mult), reads=[po, rr, gain], writes=[y1])
                cx.op("dve", lambda: V.tensor_tensor(y2[:], y1[:], sg[:], ALU.mult), reads=[y1, sg], writes=[y2])
                for ec in range(2):
                    cx.op("pe", lambda: PE.transpose(ptr[2 + ec][:], y2[:, ec * 128:(ec + 1) * 128], ident[:, 0:128]), reads=[y2, ident], writes=[ptr[2 + ec]])
                    cx.op("act", lambda: A.copy(out=ost[:, ec, ts], in_=ptr[2 + ec][:]), reads=[ptr[2 + ec]], writes=[ost])
            od_, oa_ = out_win(ti)
            cx.dma("sp", od_, oa_.rearrange("(c p) t -> p c t", p=128), ost, ost[:])
            if ti == NT - 1 and "after_out" in dr:
                dr["after_out"](ti)


def _gla_tables():
    j = np.arange(128)
    same = (j[:, None] // 64) == (j[None, :] // 64)
    ident = (j[:, None] == j[None, :])
    tri = same & (j[:, None] <= j[None, :])
    return np.concatenate([ident, tri, same], axis=1).astype(np.float32).astype(NPBF)


def _gla_maps(hnT_b, S, w_in, w_alpha, b_alpha, out_gain):
    cm = _gla_tables()
    maps = []
    for c in range(NCORES):
        b, g = c // 4, c % 4
        cols = np.concatenate([g * 128 + np.arange(128), 512 + g * 128 + np.arange(128),
                               1024 + g * 256 + np.arange(256), 2048 + g * 256 + np.arange(256), 3072 + np.arange(16)])
        wsel = np.ascontiguousarray(w_in[:, cols])
        wal = np.ascontiguousarray(np.concatenate([w_alpha[:, g * 128:(g + 1) * 128], b_alpha[None, g * 128:(g + 1) * 128]], axis=0).astype(np.float32))
        m = dict(wsel=wsel, walpha=wal, gain=np.ascontiguousarray(out_gain[None, :].astype(np.float32)), cmat=cm)
        if hnT_b[b] is not None:
            m["hnT"] = hnT_b[b]
        maps.append(m)
    return maps


_CACHE = {}


def _get(name, fn):
    if name not in _CACHE:
        _CACHE[name] = fn()
    return _CACHE[name]


def _tail_maps(hT_b, catT_b, own, w_out, g_ffn, g_next, w_up, conv_w, conv_b, w_down):
    cw = np.ascontiguousarray(np.asarray(conv_w, np.float32).reshape(3, NFF, 128).transpose(2, 1, 0).reshape(128, NFF * 3))
    cb = np.ascontiguousarray(np.asarray(conv_b, np.float32).reshape(NFF, 128).T)
    w_up = np.ascontiguousarray(np.asarray(w_up, np.float32).reshape(8, 128, 2, NFF, 128).transpose(3, 1, 0, 2, 4).reshape(NFF * 128, 2048))
    maps = []
    for c in range(NCORES):
        b, gi = c // 4, c % 4
        lo, hi = gi * own - 2, (gi + 1) * own

        def halo_slice(src, dt):
            if src is None:
                return None
            if lo < 0:
                return np.ascontiguousarray(np.concatenate([np.zeros((D, 2), dt), src[b][:, 0:hi]], axis=1))
            return np.ascontiguousarray(src[b][:, lo:hi])
        m = dict(w_out=w_out, w_up=w_up, w_down=w_down, g_ffn=_v8(g_ffn), g_next=_v8(g_next), conv_w=cw, conv_b=cb)
        hin, cat = halo_slice(hT_b, np.float32), halo_slice(catT_b, NPBF)
        if hin is not None:
            m["hin"] = hin
        if cat is not None:
            m["cat"] = cat
        maps.append(m)
    return maps


def kernel_unfused(x, mix_norm_g, even_w_in, sb_q_gain, sb_k_gain, ret_out_gain, even_w_out,
           odd_w_in, gla_w_alpha, gla_b_alpha, gla_out_gain, odd_w_out,
           ffn_norm_g, ffn_w_up, ffn_conv_w, ffn_conv_b, ffn_w_down):
    f = lambda a: np.ascontiguousarray(np.asarray(a, dtype=np.float32))
    x = f(x)
    B, S, _ = x.shape
    own = S // 4
    cores = list(range(NCORES))
    xT = [np.ascontiguousarray(x[b].T) for b in range(B)]

    nc1 = _get(("mix0", S), lambda: build_mix0(S))
    maps1 = _mix0_maps(xT, S, f(mix_norm_g)[0], f(even_w_in)[0], f(sb_q_gain)[0], f(sb_k_gain)[0], f(ret_out_gain)[0],
                       _const_tables(S))
    r1 = run_bass_kernel_spmd(nc1, maps1, core_ids=cores).results
    catT = [np.zeros((D, S), NPBF) for _ in range(B)]
    for c in cores:
        b, g = c // 4, c % 4
        o = np.asarray(r1[c]["outT"])
        catT[b][g * 128:(g + 1) * 128] = o[0:128]
        catT[b][512 + g * 128:512 + (g + 1) * 128] = o[128:256]

    nc2 = _get(("tail", own), lambda: build_tail(own, True))
    maps2 = _tail_maps(xT, catT, own, f(even_w_out)[0], f(ffn_norm_g)[0], f(mix_norm_g)[1], f(ffn_w_up)[0],
                       f(ffn_conv_w)[0], f(ffn_conv_b)[0], f(ffn_w_down)[0])
    r2 = run_bass_kernel_spmd(nc2, maps2, core_ids=cores).results
    h1T = [np.concatenate([np.asarray(r2[b * 4 + gi]["hout"]) for gi in range(4)], axis=1) for b in range(B)]
    hnT = [np.ascontiguousarray(np.concatenate([np.asarray(r2[b * 4 + gi]["hn"]) for gi in range(4)], axis=1)) for b in range(B)]

    nc3 = _get(("gla", S), lambda: build_gla(S))
    maps3 = _gla_maps(hnT, S, f(odd_w_in)[0], f(gla_w_alpha)[0], f(gla_b_alpha)[0], f(gla_out_gain)[0])
    r3 = run_bass_kernel_spmd(nc3, maps3, core_ids=cores).results
    cat2T = [np.zeros((D, S), NPBF) for _ in range(B)]
    for c in cores:
        b, g = c // 4, c % 4
        cat2T[b][g * 256:(g + 1) * 256] = np.asarray(r3[c]["outT"])

    maps4 = _tail_maps(h1T, cat2T, own, f(odd_w_out)[0], f(ffn_norm_g)[1], f(mix_norm_g)[1], f(ffn_w_up)[1],
                       f(ffn_conv_w)[1], f(ffn_conv_b)[1], f(ffn_w_down)[1])
    r4 = run_bass_kernel_spmd(nc2, maps4, core_ids=cores).results
    out = np.empty((B, S, D), np.float32)
    for b in range(B):
        oT = np.concatenate([np.asarray(r4[b * 4 + gi]["hout"]) for gi in range(4)], axis=1)
        out[b] = oT.T
    return out


def build_fused(S):
    own = S // 4
    nc = bass.Bass("TRN2", target_bir_lowering=False)
    groups = [[0, 1, 2, 3], [4, 5, 6, 7]]
    W1 = min(S, 1024)
    W2 = min(own, 256)
    with contextlib.ExitStack() as es:
        cx = Ctx(nc, es)
        m0 = _mix0_drams(cx, S, "m0_")
        t0 = _tail_drams(cx, "t0_")
        gl = _gla_drams(cx, "gl_")
        t1 = _tail_drams(cx, "t1_")
        xown = mk_dram(cx, "xown", [D, own + 2], F32, "ExternalInput")
        sel = mk_dram(cx, "sel", [128, 4], F32, "ExternalInput")
        outT = mk_dram(cx, "outT", [D, own], F32, "ExternalOutput")
        o1 = Chunked(cx, "i_o1", 256, S, W1, BF16)
        G1 = Chunked(cx, "i_G1", 1024, S, W1, BF16)
        h1d = mk_dram(cx, "i_h1d", [D, own], F32, "Internal")
        hnd = Chunked(cx, "i_hnd", D, own, W2, BF16)
        G2 = Chunked(cx, "i_G2", 4 * D, own, W2, BF16)
        hl = mk_dram(cx, "i_hl", [D, 2], F32, "Internal")
        H2 = mk_dram(cx, "i_H2", [4 * D, 2], F32, "Internal")
        o3 = Chunked(cx, "i_o3", 256, S, W1, BF16)
        G3 = Chunked(cx, "i_G3", 1024, S, W1, BF16)

        def chunk_hook(src, dst, W, key):
            def hook(ti):
                for k in range(src.n):
                    if (k + 1) * W <= (ti + 1) * 512 and k not in hook.done:
                        hook.done.add(k)
                        cx.coll_allgather(src.ch[k], dst.ch[k], groups, key=key)
            hook.done = set()
            return hook

        cx.begin_phase("m0_")
        m0["out_win"] = lambda ti: o1.win(ti * 512, 512)
        m0["after_out"] = chunk_hook(o1, G1, W1, "cc1")
        emit_mix0(cx, S, m0)
        cx.end_phase()
        assert len(m0["after_out"].done) == o1.n

        cx.begin_phase("t0_")
        t0.update(hin=xown, cat_win=G1.win, sel=sel, hout=h1d, hn=hnd.ch[0], hlast=hl,
                  hn_outs=lambda ti: hnd.wins(ti * 512, 512))
        t0["after_hn"] = chunk_hook(hnd, G2, W2, "cc2")
        emit_tail(cx, own, t0, True, fused=True)
        cx.coll_allgather(hl, H2, groups, key="cc2")
        cx.end_phase()
        assert len(t0["after_hn"].done) == hnd.n

        def hn_wins(ti):
            r = ti // (own // 512)
            c0 = (ti % (own // 512)) * 512
            res = []
            for (d_, a_, o_, w_) in G2.wins(c0, 512):
                res.append((d_, a_.rearrange("(r c p) t -> r p c t", r=4, p=128)[r], o_, w_))
            return res

        cx.begin_phase("gl_")
        gl.update(hn_wins=hn_wins, out_win=lambda ti: o3.win(ti * 512, 512))
        gl["after_out"] = chunk_hook(o3, G3, W1, "cc3")
        emit_gla(cx, S, gl, gathered=True)
        cx.end_phase()
        assert len(gl["after_out"].done) == o3.n

        cx.begin_phase("t1_")
        t1.update(hin=h1d, hin_halo=H2, cat_win=G3.win, sel=sel, hout=outT, hn=None)
        emit_tail(cx, own, t1, False, fused=True)
        cx.end_phase()
        cx.finish()
        print("fused: sems", len(cx.sems), "dmas", cx.ndma, "ops", getattr(cx, "nops", 0))
    return nc


def _fused_maps(x, mix_norm_g, even_w_in, sb_q_gain, sb_k_gain, ret_out_gain, even_w_out,
                odd_w_in, gla_w_alpha, gla_b_alpha, gla_out_gain, odd_w_out,
                ffn_norm_g, ffn_w_up, ffn_conv_w, ffn_conv_b, ffn_w_down):
    B, S, _ = x.shape
    own = S // 4
    xT = [np.ascontiguousarray(x[b].T) for b in range(B)]
    m0 = _mix0_maps(xT, S, mix_norm_g[0], even_w_in[0], sb_q_gain[0], sb_k_gain[0], ret_out_gain[0], _const_tables(S))
    perm = np.concatenate([np.concatenate([g * 128 + np.arange(128), 512 + g * 128 + np.arange(128)]) for g in range(4)])
    w_out0 = np.ascontiguousarray(even_w_out[0][perm])
    dummy_cat = [None] * B
    tl0 = _tail_maps(xT, None, own, w_out0, ffn_norm_g[0], mix_norm_g[1], ffn_w_up[0], ffn_conv_w[0], ffn_conv_b[0], ffn_w_down[0])
    tl1 = _tail_maps(None, None, own, np.ascontiguousarray(odd_w_out[0]), ffn_norm_g[1], mix_norm_g[1], ffn_w_up[1], ffn_conv_w[1],
                     ffn_conv_b[1], ffn_w_down[1])
    gm = _gla_maps([None] * B, S, odd_w_in[0], gla_w_alpha[0], gla_b_alpha[0], gla_out_gain[0])
    maps = []
    for c in range(NCORES):
        gi = c % 4
        m = {}
        for k, v in m0[c].items():
            m["m0_" + k] = v
        for k, v in tl0[c].items():
            if k == "hin":
                m["xown"] = v
            elif k != "cat":
                m["t0_" + k] = v
        for k, v in tl1[c].items():
            if k not in ("hin", "cat"):
                m["t1_" + k] = v
        for k, v in gm[c].items():
            if k != "hnT":
                m["gl_" + k] = v
        selv = np.zeros((128, 4), np.float32)
        selv[:, gi] = 1.0
        m["sel"] = selv
        maps.append(m)
    return maps


def kernel(x, mix_norm_g, even_w_in, sb_q_gain, sb_k_gain, ret_out_gain, even_w_out,
           odd_w_in, gla_w_alpha, gla_b_alpha, gla_out_gain, odd_w_out,
           ffn_norm_g, ffn_w_up, ffn_conv_w, ffn_conv_b, ffn_w_down):
    f = lambda a: np.ascontiguousarray(np.asarray(a, dtype=np.float32))
    args = [f(a) for a in (x, mix_norm_g, even_w_in, sb_q_gain, sb_k_gain, ret_out_gain, even_w_out,
                           odd_w_in, gla_w_alpha, gla_b_alpha, gla_out_gain, odd_w_out,
                           ffn_norm_g, ffn_w_up, ffn_conv_w, ffn_conv_b, ffn_w_down)]
    B, S, _ = args[0].shape
    nc = _get(("fused", S), lambda: build_fused(S))
    maps = _fused_maps(*args)
    res = run_bass_kernel_spmd(nc, maps, core_ids=list(range(NCORES))).results
    out = np.empty((B, S, D), np.float32)
    for b in range(B):
        oT = np.concatenate([np.asarray(res[b * 4 + gi]["outT"]) for gi in range(4)], axis=1)
        out[b] = oT.T
    return out
```
